# Optimizing a Trainium2 kernel written in Bass

```python
import math
import jax, jax.numpy as jnp
from jax import lax
import numpy as np

D_MODEL = 1024
BATCH = 8
SEQ = 8192
DEPTH = 4
DEC_BATCH = 4
DEC_SEQ = 4096
PAST_LEN = 128

D_GROUP = 512
N_MIXERS = 3
D_MIX = N_MIXERS * D_GROUP
D_FF = 2816
EPS = 1e-6
CONV_W = 4
CONV_LEFT = CONV_W // 2
CONV_RIGHT = CONV_W - 1 - CONV_LEFT
LRU_BLOCKS = 8
LRU_BLOCK = D_GROUP // LRU_BLOCKS
LRU_C = 8.0
SSD_HEADS = 8
SSD_HEAD_DIM = D_GROUP // SSD_HEADS
SSD_STATE = 64
SSD_GROUPS = 2
SSD_HPG = SSD_HEADS // SSD_GROUPS
SSD_CHUNK = 128
SSD_XBC = D_GROUP + 2 * SSD_GROUPS * SSD_STATE
RET_HEADS = 4
RET_HEAD_DIM = D_GROUP // RET_HEADS
RET_CHUNK = 128
ROPE_BASE = 10000.0
PROJ_SIZES = (D_GROUP, D_GROUP, D_GROUP, SSD_XBC, 2 * SSD_HEADS, D_GROUP, D_GROUP, D_GROUP, D_GROUP)
D_PROJ = 7 * D_GROUP + SSD_XBC + 2 * SSD_HEADS

kernel_name = 'hybrid_bidir_rglru_ssd_retention_encoder'


def _rmsnorm(x, w):
    xf = x.astype(jnp.float32)
    xf = xf * lax.rsqrt(jnp.mean(xf * xf, axis=-1, keepdims=True) + EPS)
    return (xf * w.astype(jnp.float32)).astype(x.dtype)


def _swiglu(x, w_gu, w_down):
    g, u = jnp.split(x @ w_gu, 2, axis=-1)
    return (jax.nn.silu(g) * u) @ w_down


def _centred_dwconv(x, w, b):
    s = x.shape[1]
    xp = jnp.pad(x, ((0, 0), (CONV_LEFT, CONV_RIGHT), (0, 0)))
    out = b
    for tap in range(CONV_W):
        out = out + xp[:, tap:tap + s] * w[tap]
    return out


def _linear_scan(a, b, reverse):
    def combine(e1, e2):
        a1, b1 = e1
        a2, b2 = e2
        return a1 * a2, a2 * b1 + b2
    return lax.associative_scan(combine, (a, b), reverse=reverse, axis=1)[1]


def _rglru_group(xb, gate, conv_w, conv_b, w_a, b_a, w_i, b_i, lam):
    f32 = jnp.float32
    bsz, s, _ = xb.shape
    xc = _centred_dwconv(xb, conv_w, conv_b).astype(f32)
    xblk = xc.reshape(bsz, s, LRU_BLOCKS, LRU_BLOCK)
    h_sum = jnp.zeros_like(xc)
    for d, rev in ((0, False), (1, True)):
        r = jax.nn.sigmoid(jnp.einsum('bsni,nij->bsnj', xblk, w_a[d].astype(f32)).reshape(bsz, s, D_GROUP) + b_a[d].astype(f32))
        i = jax.nn.sigmoid(jnp.einsum('bsni,nij->bsnj', xblk, w_i[d].astype(f32)).reshape(bsz, s, D_GROUP) + b_i[d].astype(f32))
        log_a = -LRU_C * r * jax.nn.softplus(-lam[d].astype(f32))
        u = jnp.sqrt(-jnp.expm1(2.0 * log_a)) * (i * xc)
        h_sum = h_sum + _linear_scan(jnp.exp(log_a), u, rev)
    return h_sum * jax.nn.gelu(gate.astype(f32))


def _ssd_chunked(x, dt, a, bm, cm):
    bsz, s, h, p = x.shape
    L = SSD_CHUNK
    nc = s // L
    x = x.reshape(bsz, nc, L, SSD_GROUPS, SSD_HPG, p)
    dt = dt.reshape(bsz, nc, L, SSD_GROUPS, SSD_HPG)
    bm = bm.reshape(bsz, nc, L, SSD_GROUPS, SSD_STATE)
    cm = cm.reshape(bsz, nc, L, SSD_GROUPS, SSD_STATE)
    acum = jnp.cumsum(dt * a.reshape(SSD_GROUPS, SSD_HPG), axis=2)
    acum_t = jnp.moveaxis(acum, 2, -1)
    dt_t = jnp.moveaxis(dt, 2, -1)
    tril = jnp.tril(jnp.ones((L, L), dtype=bool))
    seg = acum_t[..., :, None] - acum_t[..., None, :]
    decay = jnp.exp(jnp.where(tril, seg, -jnp.inf))
    cb = jnp.einsum('bcign,bcjgn->bcgij', cm, bm)
    w = cb[:, :, :, None] * decay * dt_t[..., None, :]
    y_diag = jnp.einsum('bcgkij,bcjgkp->bcigkp', w, x)
    decay_states = jnp.exp(acum_t[..., -1:] - acum_t) * dt_t
    states = jnp.einsum('bcgkl,bclgn,bclgkp->bcgkpn', decay_states, bm, x)
    chunk_decay = jnp.exp(acum_t[..., -1])

    def step(carry, inp):
        st, ad = inp
        return carry * ad[..., None, None] + st, carry

    init = jnp.zeros((bsz, SSD_GROUPS, SSD_HPG, p, SSD_STATE), x.dtype)
    _, prev = lax.scan(step, init, (jnp.moveaxis(states, 1, 0), jnp.moveaxis(chunk_decay, 1, 0)))
    prev = jnp.moveaxis(prev, 0, 1)
    y_off = jnp.einsum('bclgn,bcgkpn,bcgkl->bclgkp', cm, prev, jnp.exp(acum_t))
    return (y_diag + y_off).reshape(bsz, s, h, p)


def _ssd_group(z, xbc, dt_raw, conv_w, conv_b, dt_bias, a_log, d_skip, norm_w):
    f32 = jnp.float32
    bsz, s, _ = xbc.shape
    nbc = SSD_GROUPS * SSD_STATE
    xbc = jax.nn.silu(_centred_dwconv(xbc, conv_w, conv_b).astype(f32))
    xs = xbc[..., :D_GROUP].reshape(bsz, s, SSD_HEADS, SSD_HEAD_DIM)
    bm = xbc[..., D_GROUP:D_GROUP + nbc].reshape(bsz, s, SSD_GROUPS, SSD_STATE)
    cm = xbc[..., D_GROUP + nbc:].reshape(bsz, s, SSD_GROUPS, SSD_STATE)
    dt = jax.nn.softplus(dt_raw.astype(f32).reshape(bsz, s, 2, SSD_HEADS) + dt_bias.astype(f32))
    a = -jnp.exp(a_log.astype(f32))
    flip = lambda t: jnp.flip(t, axis=1)
    y_f = _ssd_chunked(xs, dt[:, :, 0], a[0], bm, cm)
    y_b = flip(_ssd_chunked(flip(xs), flip(dt[:, :, 1]), a[1], flip(bm), flip(cm)))
    y = (y_f + y_b + d_skip.astype(f32)[:, None] * xs).reshape(bsz, s, D_GROUP)
    y = y * jax.nn.silu(z.astype(f32))
    y = y * lax.rsqrt(jnp.mean(y * y, axis=-1, keepdims=True) + EPS)
    return y * norm_w.astype(f32)


def _rope(t):
    s, d = t.shape[1], t.shape[-1]
    inv_freq = 1.0 / (ROPE_BASE ** (jnp.arange(0, d, 2, dtype=jnp.float32) / d))
    ang = jnp.arange(s, dtype=jnp.float32)[:, None] * inv_freq[None, :]
    cos = jnp.cos(ang)[None, :, None, :]
    sin = jnp.sin(ang)[None, :, None, :]
    t1, t2 = jnp.split(t, 2, axis=-1)
    return jnp.concatenate([t1 * cos - t2 * sin, t1 * sin + t2 * cos], axis=-1)


def _retention_group(q, k, v, g, norm_w):
    f32 = jnp.float32
    bsz, s, _ = q.shape
    L = RET_CHUNK
    nc = s // L
    shp = (bsz, s, RET_HEADS, RET_HEAD_DIM)
    cshp = (bsz, nc, L, RET_HEADS, RET_HEAD_DIM)
    q = _rope(q.astype(f32).reshape(shp)).reshape(cshp)
    k = (_rope(k.astype(f32).reshape(shp)) * RET_HEAD_DIM ** -0.5).reshape(cshp)
    v = v.astype(f32).reshape(cshp)
    log_gamma = jnp.log1p(-jnp.exp2(-5.0 - jnp.arange(RET_HEADS, dtype=f32)))
    pos = jnp.arange(L, dtype=f32)
    d_intra = jnp.exp(log_gamma[:, None, None] * jnp.abs(pos[:, None] - pos[None, :]))
    scores = jnp.einsum('bcihd,bcjhd->bchij', q, k) * d_intra
    y = jnp.einsum('bchij,bcjhe->bcihe', scores, v)
    kv_f = jnp.einsum('bclhd,bclhe,hl->bchde', k, v, jnp.exp(log_gamma[:, None] * (L - 1.0 - pos)[None]))
    kv_b = jnp.einsum('bclhd,bclhe,hl->bchde', k, v, jnp.exp(log_gamma[:, None] * pos[None]))
    cdec = jnp.exp(log_gamma * L)[:, None, None]

    def step(carry, kv):
        return carry * cdec + kv, carry

    init = jnp.zeros((bsz, RET_HEADS, RET_HEAD_DIM, RET_HEAD_DIM), f32)
    _, r_f = lax.scan(step, init, jnp.moveaxis(kv_f, 1, 0))
    _, r_b = lax.scan(step, init, jnp.moveaxis(kv_b, 1, 0), reverse=True)
    r_f = jnp.moveaxis(r_f, 0, 1)
    r_b = jnp.moveaxis(r_b, 0, 1)
    y = y + jnp.einsum('bclhd,bchde,hl->bclhe', q, r_f, jnp.exp(log_gamma[:, None] * (pos + 1.0)[None]))
    y = y + jnp.einsum('bclhd,bchde,hl->bclhe', q, r_b, jnp.exp(log_gamma[:, None] * (L - pos)[None]))
    y = y.reshape(shp)
    mu = jnp.mean(y, axis=-1, keepdims=True)
    var = jnp.mean(jnp.square(y - mu), axis=-1, keepdims=True)
    y = ((y - mu) * lax.rsqrt(var + EPS)).reshape(bsz, s, D_GROUP) * norm_w.astype(f32)
    return y * jax.nn.silu(g.astype(f32))


def _mixer(xn, l, p):
    offsets = [int(o) for o in np.cumsum(PROJ_SIZES)[:-1]]
    (lru_x, lru_gate, ssd_z, ssd_xbc, ssd_dt, ret_q, ret_k, ret_v, ret_g) = jnp.split(xn @ p['w_in'][l], offsets, axis=-1)
    y_lru = _rglru_group(lru_x, lru_gate, p['lru_conv_w'][l], p['lru_conv_b'][l], p['lru_w_a'][l], p['lru_b_a'][l], p['lru_w_i'][l], p['lru_b_i'][l], p['lru_lam'][l])
    y_ssd = _ssd_group(ssd_z, ssd_xbc, ssd_dt, p['ssd_conv_w'][l], p['ssd_conv_b'][l], p['ssd_dt_bias'][l], p['ssd_a_log'][l], p['ssd_d'][l], p['ssd_norm'][l])
    y_ret = _retention_group(ret_q, ret_k, ret_v, ret_g, p['ret_norm'][l])
    y = jnp.concatenate([y_lru, y_ssd, y_ret], axis=-1).astype(xn.dtype)
    return y @ p['w_out'][l]


def _trunk(x, p):
    for l in range(DEPTH):
        x = x + 0.5 * _swiglu(_rmsnorm(x, p['ffn1_norm'][l]), p['ffn1_w_gu'][l], p['ffn1_w_down'][l])
        x = x + _mixer(_rmsnorm(x, p['mix_norm'][l]), l, p)
        x = x + 0.5 * _swiglu(_rmsnorm(x, p['ffn2_norm'][l]), p['ffn2_w_gu'][l], p['ffn2_w_down'][l])
    return _rmsnorm(x, p['final_norm'])


def setup_inputs(seed: int = 0) -> dict:
    key = jax.random.key(seed)
    ks = jax.random.split(key, 26)
    f32 = jnp.float32

    def nrm(k, shape, scale):
        return jax.random.normal(k, shape, f32) * scale

    def gain(k, shape):
        return 1.0 + 0.02 * jax.random.normal(k, shape, f32)

    u = jax.random.uniform(ks[13], (DEPTH, 2, D_GROUP), f32, 0.9, 0.999)
    a0 = u ** (1.0 / LRU_C)
    dt0 = jnp.exp(jax.random.uniform(ks[16], (DEPTH, 2, SSD_HEADS), f32, math.log(1e-3), math.log(1e-1)))
    return {
        'x_prompt': jax.random.normal(ks[0], (BATCH, SEQ, D_MODEL), f32),
        'x_sample': jax.random.normal(ks[1], (DEC_BATCH, DEC_SEQ, D_MODEL), f32),
        'ffn1_norm': gain(ks[2], (DEPTH, D_MODEL)),
        'ffn1_w_gu': nrm(ks[3], (DEPTH, D_MODEL, 2 * D_FF), D_MODEL ** -0.5),
        'ffn1_w_down': nrm(ks[4], (DEPTH, D_FF, D_MODEL), D_FF ** -0.5),
        'mix_norm': gain(ks[5], (DEPTH, D_MODEL)),
        'w_in': nrm(ks[6], (DEPTH, D_MODEL, D_PROJ), D_MODEL ** -0.5),
        'lru_conv_w': nrm(ks[7], (DEPTH, CONV_W, D_GROUP), CONV_W ** -0.5),
        'lru_conv_b': nrm(ks[8], (DEPTH, D_GROUP), 0.02),
        'lru_w_a': nrm(ks[9], (DEPTH, 2, LRU_BLOCKS, LRU_BLOCK, LRU_BLOCK), LRU_BLOCK ** -0.5),
        'lru_b_a': nrm(ks[10], (DEPTH, 2, D_GROUP), 0.02),
        'lru_w_i': nrm(ks[11], (DEPTH, 2, LRU_BLOCKS, LRU_BLOCK, LRU_BLOCK), LRU_BLOCK ** -0.5),
        'lru_b_i': nrm(ks[12], (DEPTH, 2, D_GROUP), 0.02),
        'lru_lam': jnp.log(a0) - jnp.log1p(-a0),
        'ssd_conv_w': nrm(ks[14], (DEPTH, CONV_W, SSD_XBC), CONV_W ** -0.5),
        'ssd_conv_b': nrm(ks[15], (DEPTH, SSD_XBC), 0.02),
        'ssd_dt_bias': dt0 + jnp.log(-jnp.expm1(-dt0)),
        'ssd_a_log': jnp.log(jax.random.uniform(ks[17], (DEPTH, 2, SSD_HEADS), f32, 1.0, 16.0)),
        'ssd_d': gain(ks[18], (DEPTH, SSD_HEADS)),
        'ssd_norm': gain(ks[19], (DEPTH, D_GROUP)),
        'ret_norm': gain(ks[20], (DEPTH, D_GROUP)),
        'w_out': nrm(ks[21], (DEPTH, D_MIX, D_MODEL), D_MIX ** -0.5),
        'ffn2_norm': gain(ks[22], (DEPTH, D_MODEL)),
        'ffn2_w_gu': nrm(ks[23], (DEPTH, D_MODEL, 2 * D_FF), D_MODEL ** -0.5),
        'ffn2_w_down': nrm(ks[24], (DEPTH, D_FF, D_MODEL), D_FF ** -0.5),
        'final_norm': gain(ks[25], (D_MODEL,)),
    }


def reference(x_prompt, x_sample, ffn1_norm, ffn1_w_gu, ffn1_w_down, mix_norm, w_in,
              lru_conv_w, lru_conv_b, lru_w_a, lru_b_a, lru_w_i, lru_b_i, lru_lam,
              ssd_conv_w, ssd_conv_b, ssd_dt_bias, ssd_a_log, ssd_d, ssd_norm,
              ret_norm, w_out, ffn2_norm, ffn2_w_gu, ffn2_w_down, final_norm):
    p = {
        'ffn1_norm': ffn1_norm, 'ffn1_w_gu': ffn1_w_gu, 'ffn1_w_down': ffn1_w_down,
        'mix_norm': mix_norm, 'w_in': w_in,
        'lru_conv_w': lru_conv_w, 'lru_conv_b': lru_conv_b, 'lru_w_a': lru_w_a, 'lru_b_a': lru_b_a,
        'lru_w_i': lru_w_i, 'lru_b_i': lru_b_i, 'lru_lam': lru_lam,
        'ssd_conv_w': ssd_conv_w, 'ssd_conv_b': ssd_conv_b, 'ssd_dt_bias': ssd_dt_bias,
        'ssd_a_log': ssd_a_log, 'ssd_d': ssd_d, 'ssd_norm': ssd_norm,
        'ret_norm': ret_norm, 'w_out': w_out,
        'ffn2_norm': ffn2_norm, 'ffn2_w_gu': ffn2_w_gu, 'ffn2_w_down': ffn2_w_down,
        'final_norm': final_norm,
    }
    y_prompt = _trunk(x_prompt, p)
    y_sample = _trunk(x_sample, p)
    return (y_prompt, y_sample)
```

```python
import contextlib
import numpy as np
import ml_dtypes
import concourse.bass as bass
import concourse.mybir as mybir
from concourse.bass_utils import run_bass_kernel_spmd

F32 = mybir.dt.float32
BF16 = mybir.dt.bfloat16
AF = mybir.ActivationFunctionType
ALU = mybir.AluOpType

D = 1024
DFF = 2816
NKC = 8
NFC = 22
DPROJ = 4368
EPS = 1e-6
COMPUTE = ('pe', 'act', 'dve', 'pool')
ENGS = ('pe', 'act', 'dve', 'pool', 'sp')
SB_BASE = 16640
SB_END = 229376
DMA_RING = {'sp': 16, 'act': 4, 'pool': 16}


class Buf:
    __slots__ = ('w', 'r')

    def __init__(self):
        self.w = {}
        self.r = {}


class Prog:
    def __init__(self, nc):
        self.nc = nc
        self.ops = {e: [] for e in ENGS}
        self.known = {e: {} for e in ENGS}
        self.ndma = {e: 0 for e in DMA_RING}
        self.dma_last = {}
        self.flag = {e: set() for e in COMPUTE}
        self.sb_off = SB_BASE
        self.sb_cnt = 0
        self.sb_max = 0
        self.pb = 0

    def sb(self, shape, dtype, name='t'):
        esz = 4 if dtype == F32 else 2
        per_part = int(np.prod(shape[1:])) * esz
        off = (self.sb_off + 63) // 64 * 64
        self.sb_cnt += 1
        h = self.nc.alloc_sbuf_tensor_at(f"{name}{self.sb_cnt}", list(shape), dtype, offset=off)
        self.sb_off = off + per_part
        self.sb_max = max(self.sb_max, self.sb_off)
        assert self.sb_off <= SB_END, f"SBUF overflow {self.sb_off} at {name}"
        return h

    def mark(self):
        return self.sb_off

    def release(self, m):
        self.barrier()
        self.sb_off = m

    def _need(self, eng, toks):
        out = []
        kn = self.known[eng]
        for src, idx in toks.items():
            if kn.get(src, -1) >= idx:
                continue
            kn[src] = idx
            out.append((src, idx))
            if src in COMPUTE:
                self.flag[src].add(idx)
        return out

    def op(self, eng, fn, r=(), w=()):
        deps = {}
        for b in r:
            for src, idx in b.w.items():
                if deps.get(src, -1) < idx:
                    deps[src] = idx
        for b in w:
            for src, idx in b.w.items():
                if src != eng and deps.get(src, -1) < idx:
                    deps[src] = idx
            for src, idx in b.r.items():
                if src != eng and deps.get(src, -1) < idx:
                    deps[src] = idx
        waits = self._need(eng, deps)
        idx = len(self.ops[eng])
        self.ops[eng].append(('c', fn, waits))
        for b in r:
            b.r[eng] = idx
        for b in w:
            b.w = {eng: idx}
            b.r = {}
        return idx

    def dma(self, q, out, in_, r=(), w=(), **kw):
        deps = {}
        for b in r:
            for src, idx in b.w.items():
                if deps.get(src, -1) < idx:
                    deps[src] = idx
        for b in w:
            for src, idx in list(b.w.items()) + list(b.r.items()):
                if deps.get(src, -1) < idx:
                    deps[src] = idx
        n = self.ndma[q]
        self.ndma[q] = n + 1
        K = DMA_RING[q]
        key = ('dma', q, n % K)
        val = 16 * (n // K + 1)
        if n >= K:
            deps[key] = max(deps.get(key, -1), val - 16)
        waits = self._need(q, deps)
        self.dma_last[key] = val
        self.ops[q].append(('d', (out, in_, kw), waits, key))
        for b in r:
            b.r[key] = val
        for b in w:
            b.w = {key: val}
            b.r = {}

    def barrier(self):
        last = {}
        for e in COMPUTE:
            for i in range(len(self.ops[e]) - 1, -1, -1):
                if self.ops[e][i][0] == 'c':
                    last[e] = i
                    break
        for key, val in self.dma_last.items():
            last[key] = val
        for e in ENGS:
            deps = {s: i for s, i in last.items() if s != e}
            waits = self._need(e, deps)
            if waits:
                self.ops[e].append(('w', None, waits))

    def bank(self):
        b = self.pb
        self.pb = (b + 1) % 8
        return b

    def bank2(self):
        b = (self.pb + 1) // 2 * 2 % 8
        self.pb = (b + 2) % 8
        return b

    def emit(self):
        nc = self.nc
        val = {}
        for e in COMPUTE:
            cnt = 0
            v = {}
            for i, o in enumerate(self.ops[e]):
                if o[0] == 'c' and i in self.flag[e]:
                    cnt += 1
                    v[i] = cnt
            val[e] = v
        sems = {}
        with contextlib.ExitStack() as st:
            for e in COMPUTE:
                sems[e] = st.enter_context(nc.semaphore(f"s_{e}"))
            for q, K in DMA_RING.items():
                for k in range(K):
                    sems[('dma', q, k)] = st.enter_context(nc.semaphore(f"d_{q}{k}"))
            block = st.enter_context(nc.Block())

            def run(e):
                def body(eng):
                    fl = self.flag.get(e, ())
                    for i, o in enumerate(self.ops[e]):
                        for src, idx in o[2]:
                            v = val[src][idx] if src in COMPUTE else idx
                            eng.wait_ge(sems[src], v)
                        if o[0] == 'c':
                            ins = o[1](eng)
                            if i in fl:
                                ins.then_inc(sems[e], 1)
                        elif o[0] == 'd':
                            out, in_, kw = o[1]
                            eng.dma_start(out=out, in_=in_, **kw).then_inc(sems[o[3]], 16)
                return body
            block.tensor(run('pe'))
            block.scalar(run('act'))
            block.vector(run('dve'))
            block.gpsimd(run('pool'))
            block.sync(run('sp'))


def mm(P, out, lhsT, rhs, start, stop, r, w):
    P.op('pe', lambda e: e.matmul(out, lhsT=lhsT, rhs=rhs, start=start, stop=stop), r=r, w=w)


def trp(P, out, in_, ident, r, w):
    P.op('pe', lambda e: e.transpose(out, in_, ident), r=r, w=w)


def act(P, out, in_, func, r, w, bias=None, scale=None, accum=None):
    kw = {}
    if bias is not None:
        kw['bias'] = bias
    if scale is not None:
        kw['scale'] = scale
    if accum is not None:
        kw['accum_out'] = accum
    P.op('act', lambda e: e.activation(out, in_, func, **kw), r=r, w=w)


def tt(P, eng, out, in0, in1, op, r, w):
    P.op(eng, lambda e: e.tensor_tensor(out, in0, in1, op), r=r, w=w)


def ts(P, eng, out, in0, s1, s2, op0, op1, r, w):
    if s2 is None:
        P.op(eng, lambda e: e.tensor_scalar(out, in0, s1, None, op0), r=r, w=w)
    else:
        P.op(eng, lambda e: e.tensor_scalar(out, in0, s1, s2, op0, op1), r=r, w=w)


def stt(P, out, in0, scalar, in1, op0, op1, r, w):
    P.op('dve', lambda e: e.scalar_tensor_tensor(out, in0, scalar, in1, op0, op1), r=r, w=w)


def cpy(P, eng, out, in_, r, w):
    if eng == 'act':
        P.op('act', lambda e: e.copy(out, in_), r=r, w=w)
    else:
        P.op(eng, lambda e: e.tensor_copy(out, in_), r=r, w=w)


def mset(P, eng, ap, v, w):
    P.op(eng, lambda e: e.memset(ap, v), w=w)


def amul(P, out, in_, m_ap, r, w):
    P.op('act', lambda e: e.mul(out, in_, m_ap), r=r, w=w)


def recip(P, out, in_, r, w):
    P.op('dve', lambda e: e.reciprocal(out, in_), r=r, w=w)


def scan(P, out, d0, d1, init, r, w):
    P.op('dve', lambda e: e.tensor_tensor_scan(out, d0, d1, init, ALU.mult, ALU.add), r=r, w=w)


def make_consts():
    c = {}
    c['ident'] = np.eye(128, dtype=np.float32)
    c['identb'] = np.eye(128, dtype=np.float32).astype(ml_dtypes.bfloat16)
    c['ones'] = np.ones((128, 128), np.float32)
    k = np.arange(128)
    c['m_le'] = (k[:, None] <= k[None, :]).astype(np.float32)
    c['m_ge'] = (k[:, None] >= k[None, :]).astype(np.float32)
    c['m_gt'] = (k[:, None] > k[None, :]).astype(np.float32)
    c['m_lt'] = (k[:, None] < k[None, :]).astype(np.float32)
    bm = np.zeros((128, 512), np.float32)
    bm[:64, :256] = 1.0
    bm[64:, 256:] = 1.0
    c['blockmask'] = bm
    d = 128
    inv_freq = (1.0 / (10000.0 ** (np.arange(0, d, 2, dtype=np.float32) / np.float32(d)))).astype(np.float32)
    pos = np.arange(8192, dtype=np.float32)
    ang = (pos[:, None] * inv_freq[None, :]).astype(np.float32)
    cos = np.cos(ang.astype(np.float64)).T
    sin = np.sin(ang.astype(np.float64)).T
    cosf = np.concatenate([cos, cos], 0)
    sins = np.concatenate([-sin, sin], 0)
    sc = 128.0 ** -0.5
    c['cosq'] = cosf.astype(np.float32)
    c['sinq'] = sins.astype(np.float32)
    c['cosk'] = (cosf * sc).astype(np.float32)
    c['sink'] = (sins * sc).astype(np.float32)
    lg = np.log1p(-np.exp2(-5.0 - np.arange(4, dtype=np.float64)))
    pl = np.arange(128, dtype=np.float64)
    dm = np.exp(lg[None, :, None] * np.abs(pl[:, None, None] - pl[None, None, :]))
    c['dmask'] = dm.astype(np.float32)
    gf = np.exp(lg[:, None] * (pl + 1.0)[None, :])
    gb = np.exp(lg[:, None] * (128.0 - pl)[None, :])
    c['gf'] = np.broadcast_to(gf[None], (128, 4, 128)).astype(np.float32).copy()
    c['gb'] = np.broadcast_to(gb[None], (128, 4, 128)).astype(np.float32).copy()
    wf = np.exp(lg[None, :] * (127.0 - pl)[:, None])
    wb = np.exp(lg[None, :] * pl[:, None])
    c['wfb'] = np.concatenate([wf, wb], 1).astype(np.float32)
    c['cdec'] = np.broadcast_to(np.exp(lg * 128.0)[None, :], (128, 4)).astype(np.float32).copy()
    return c


CONST_SHAPES = None


def build(cfg, debug=False):
    TA, TB, DEPTH = cfg['TA'], cfg['TB'], cfg['DEPTH']
    NT = TA + TB
    TMAX = max(TA, TB)
    seqs = [(0, TA), (TA, TB)]
    nc = bass.Bass("TRN2", target_bir_lowering=False)
    P = Prog(nc)

    def din(name, shape, dt=F32):
        return nc.dram_tensor(name, list(shape), dt, kind="ExternalInput").ap()

    def dscr(name, shape, dt=F32):
        return nc.dram_tensor(name, list(shape), dt, kind="ExternalOutput" if debug else "Internal").ap()

    xa = din('xa', [TA, D])
    xb = din('xb', [TB, D])
    Wd = {}
    L = cfg.get('LW', 4)
    for name, shape in [('ffn1_norm', [L, D]), ('ffn1_w_gu', [L, D, 2 * DFF]), ('ffn1_w_down', [L, DFF, D]),
                        ('mix_norm', [L, D]), ('w_in', [L, D, DPROJ]), ('lru_conv_w', [L, 4, 512]),
                        ('lru_conv_b', [L, 512]), ('lru_w_a', [L, 2, 8, 64, 64]), ('lru_b_a', [L, 2, 512]),
                        ('lru_w_i', [L, 2, 8, 64, 64]), ('lru_b_i', [L, 2, 512]), ('lru_lam', [L, 2, 512]),
                        ('ssd_conv_w', [L, 4, 768]), ('ssd_conv_b', [L, 768]), ('ssd_dt_bias', [L, 16]),
                        ('ssd_a_log', [L, 16]), ('ssd_d', [L, 8]), ('ssd_norm', [L, 512]),
                        ('ret_norm', [L, 512]), ('w_out', [L, 1536, D]), ('ffn2_norm', [L, D]),
                        ('ffn2_w_gu', [L, D, 2 * DFF]), ('ffn2_w_down', [L, DFF, D]), ('final_norm', [1, D])]:
        Wd[name] = din(name, shape)
    consts = make_consts()
    Cd = {}
    for name, arr in consts.items():
        Cd[name] = din('c_' + name, arr.shape, BF16 if arr.dtype == ml_dtypes.bfloat16 else F32)
    ya = nc.dram_tensor('ya', [TA, D], F32, kind="ExternalOutput").ap()
    yb = nc.dram_tensor('yb', [TB, D], F32, kind="ExternalOutput").ap()

    X = dscr('X', [D, NT])
    LX = dscr('LX', [512, NT])
    LG = dscr('LG', [512, NT])
    XBC = dscr('XBC', [768, NT])
    RG = dscr('RG', [512, NT])
    Qd = dscr('Q', [512, NT], BF16)
    Kd = dscr('K', [512, NT], BF16)
    SZ = dscr('SZ', [NT, 512])
    Vd = dscr('V', [NT, 512], BF16)
    DT = dscr('DT', [NT, 16])
    Y = dscr('Y', [1536, NT], BF16)
    PF = dscr('PF', [TMAX // 128, 128, 512], BF16)

    ps = nc.alloc_psum_tensor("ps", [128, 4096], F32)
    pb = [Buf() for _ in range(8)]

    def pbank(b, n=512):
        return ps[:, b * 512:b * 512 + n]

    def pbankb(b):
        return ps[:, b * 512:(b + 1) * 512].bitcast(BF16)

    ident = P.sb([128, 128], F32, 'ident')
    identb = P.sb([128, 128], BF16, 'identb')
    ones = P.sb([128, 128], F32, 'ones')
    epst = P.sb([128, 1], F32, 'eps')
    cb_ = Buf()
    P.dma('sp', ident[:], Cd['ident'], w=[cb_])
    P.dma('sp', identb[:], Cd['identb'], w=[cb_])
    P.dma('sp', ones[:], Cd['ones'], w=[cb_])
    mset(P, 'dve', epst[:], EPS, [cb_])
    P.barrier()
    base_mark = P.mark()

    def rmsnorm_fm(xt, xbuf, nw, nwb, out, outb, TT, sq, sqb, rs, rsb, nfeat=D):
        b = P.bank()
        for kc in range(NKC):
            j = kc % len(sq)
            act(P, sq[j][:, :TT], xt[:, kc, :], AF.Square, r=[xbuf], w=[sqb[j]])
            mm(P, pbank(b, TT), ones[:], sq[j][:, :TT], kc == 0, kc == NKC - 1, r=[sqb[j], cb_], w=[pb[b]])
        act(P, rs[:, :TT], pbank(b, TT), AF.Sqrt, r=[pb[b], cb_], w=[rsb], bias=epst[:, 0:1], scale=1.0 / nfeat)
        recip(P, rs[:, :TT], rs[:, :TT], r=[rsb], w=[rsb])
        for kc in range(NKC):
            stt(P, out[:, kc, :], xt[:, kc, :], nw[:, kc:kc + 1], rs[:, :TT], ALU.mult, ALU.mult,
                r=[xbuf, rsb, nwb], w=[outb])

    def load_vec_pc(dst, src_1d, n, buf):
        P.dma('sp', dst, src_1d.rearrange("(c p) -> p c", p=128), w=[buf], allow_slow_non_contiguous=True)

    def s0_pass():
        m = P.mark()
        xin = [P.sb([128, 4, D], F32, 'xin') for _ in range(2)]
        xin_b = [Buf(), Buf()]
        xfm = [P.sb([128, NKC, 512], F32, 'xfm') for _ in range(2)]
        xfm_b = [Buf(), Buf()]
        i = 0
        for (src, (s0, T)) in zip((xa, xb), seqs):
            TT = 512 if T % 512 == 0 else 128
            nb = TT // 128
            for t0 in range(0, T, TT):
                xi, xib, xf, xfb = xin[i % 2], xin_b[i % 2], xfm[i % 2], xfm_b[i % 2]
                P.dma('sp', xi[:, 0:nb, :], src[t0:t0 + TT, :].rearrange("(b p) f -> p b f", p=128), w=[xib])
                for kc in range(NKC):
                    b = P.bank()
                    for bl in range(nb):
                        trp(P, pbank(b)[:, bl * 128:(bl + 1) * 128], xi[:, bl, kc * 128:(kc + 1) * 128], ident[:],
                            r=[xib, cb_], w=[pb[b]])
                    cpy(P, 'act' if kc % 2 else 'dve', xf[:, kc, 0:TT], pbank(b, TT), r=[pb[b]], w=[xfb])
                P.dma('pool', X[:, s0 + t0:s0 + t0 + TT].rearrange("(c p) t -> p c t", p=128), xf[:, :, 0:TT], r=[xfb])
                i += 1
        P.release(m)

    def ffn_pass(l, pre):
        m = P.mark()
        TT = cfg['TTF']
        wgu = P.sb([128, NKC, 2 * DFF], BF16, 'wgu')
        wd = P.sb([128, NFC, D], BF16, 'wd')
        wgub = [Buf() for _ in range(NKC)]
        wdb = [Buf() for _ in range(NFC)]
        nw = P.sb([128, NKC], F32, 'nw')
        nwb = Buf()
        load_vec_pc(nw[:], Wd[pre + '_norm'][l], NKC, nwb)
        for kc in range(NKC):
            P.dma('pool', wgu[:, kc, :], Wd[pre + '_w_gu'][l, kc * 128:(kc + 1) * 128, :], w=[wgub[kc]])
        for fc in range(NFC):
            P.dma('pool', wd[:, fc, :], Wd[pre + '_w_down'][l, fc * 128:(fc + 1) * 128, :], w=[wdb[fc]])
        xt = [P.sb([128, NKC, TT], F32, 'xt') for _ in range(2)]
        xtb = [Buf(), Buf()]
        xn2 = [P.sb([128, NKC, TT], BF16, 'xn') for _ in range(2)]
        xn2b = [Buf(), Buf()]
        h = P.sb([128, NFC, TT], BF16, 'h')
        hb = Buf()
        sq = [P.sb([128, TT], F32, 'sq') for _ in range(3)]
        sqb = [Buf() for _ in range(3)]
        rs = P.sb([128, TT], F32, 'rs')
        rsb = Buf()
        sg = [P.sb([128, TT], F32, 'sg') for _ in range(2)]
        sgb = [Buf(), Buf()]
        ntile = NT // TT

        def load(i):
            P.dma('sp', xt[i % 2][:], X[:, i * TT:(i + 1) * TT].rearrange("(c p) t -> p c t", p=128), w=[xtb[i % 2]])
        load(0)
        if ntile > 1:
            load(1)
        rmsnorm_fm(xt[0], xtb[0], nw, nwb, xn2[0], xn2b[0], TT, sq, sqb, rs, rsb)
        for i in range(ntile):
            x_, xb_ = xt[i % 2], xtb[i % 2]
            xn, xnb = xn2[i % 2], xn2b[i % 2]
            for mc in range(NFC):
                bg = P.bank()
                bu = P.bank()
                for kc in range(NKC):
                    mm(P, pbank(bg, TT), wgu[:, kc, mc * 128:(mc + 1) * 128], xn[:, kc, :], kc == 0, kc == NKC - 1,
                       r=[wgub[kc], xnb], w=[pb[bg]])
                for kc in range(NKC):
                    mm(P, pbank(bu, TT), wgu[:, kc, DFF + mc * 128:DFF + (mc + 1) * 128], xn[:, kc, :], kc == 0,
                       kc == NKC - 1, r=[wgub[kc], xnb], w=[pb[bu]])
                j = mc % 2
                act(P, sg[j][:], pbank(bg, TT), AF.Silu, r=[pb[bg]], w=[sgb[j]])
                tt(P, 'dve', h[:, mc, :], sg[j][:], pbank(bu, TT), ALU.mult, r=[sgb[j], pb[bu]], w=[hb])
            if i + 1 < ntile:
                rmsnorm_fm(xt[(i + 1) % 2], xtb[(i + 1) % 2], nw, nwb, xn2[(i + 1) % 2], xn2b[(i + 1) % 2], TT, sq, sqb, rs, rsb)
            for n in range(NKC):
                b = P.bank()
                for mc in range(NFC):
                    mm(P, pbank(b, TT), wd[:, mc, n * 128:(n + 1) * 128], h[:, mc, :], mc == 0, mc == NFC - 1,
                       r=[wdb[mc], hb], w=[pb[b]])
                stt(P, x_[:, n, :], pbank(b, TT), 0.5, x_[:, n, :], ALU.mult, ALU.add, r=[pb[b], xb_], w=[xb_])
            P.dma('pool', X[:, i * TT:(i + 1) * TT].rearrange("(c p) t -> p c t", p=128), x_[:], r=[xb_])
            if i + 2 < ntile:
                load(i + 2)
        P.release(m)

    def s2_pass(l):
        m = P.mark()
        TT = cfg['TT2']
        nbl = TT // 128
        win = P.sb([128, NKC, DPROJ], BF16, 'win')
        winb = [Buf() for _ in range(NKC)]
        wsw = P.sb([128, NKC, 1024], BF16, 'wsw')
        wswb = Buf()
        nw = P.sb([128, NKC], F32, 'nw')
        nwb = Buf()
        load_vec_pc(nw[:], Wd['mix_norm'][l], NKC, nwb)
        for kc in range(NKC):
            P.dma('pool', win[:, kc, :], Wd['w_in'][l, kc * 128:(kc + 1) * 128, :], w=[winb[kc]])
        for qk in range(2):
            base = 2320 + qk * 512
            for hh in range(4):
                for half in range(2):
                    c0 = base + hh * 128 + (1 - half) * 64
                    d0 = (qk * 4 + hh) * 128 + half * 64
                    P.dma('pool', wsw[:, :, d0:d0 + 64],
                          Wd['w_in'][l, :, c0:c0 + 64].rearrange("(c p) n -> p c n", p=128), w=[wswb])
        dtb = P.sb([128, 16], F32, 'dtb')
        dtbb = Buf()
        P.dma('sp', dtb[:], Wd['ssd_dt_bias'][l:l + 1, :].to_broadcast([128, 16]), w=[dtbb])
        xt = [P.sb([128, NKC, TT], F32, 'xt') for _ in range(2)]
        xtb = [Buf(), Buf()]
        rope = [P.sb([128, 4, TT], F32, 'rope') for _ in range(2)]
        ropeb = [Buf(), Buf()]
        xn2 = [P.sb([128, NKC, TT], BF16, 'xn') for _ in range(2)]
        xn2b = [Buf(), Buf()]
        xn, xnb = xn2[0], xn2b[0]
        sq = [P.sb([128, TT], F32, 'sq') for _ in range(3)]
        sqb = [Buf() for _ in range(3)]
        rs = P.sb([128, TT], F32, 'rs')
        rsb = Buf()

        def stage(shape, dt, name):
            return [P.sb(shape, dt, name) for _ in range(2)], [Buf(), Buf()]
        lxs, lxsb = stage([128, 4, TT], F32, 'lxs')
        lgs, lgsb = stage([128, 4, TT], F32, 'lgs')
        xbs, xbsb = stage([128, 6, TT], F32, 'xbs')
        rgs, rgsb = stage([128, 4, TT], F32, 'rgs')
        qs, qsb = stage([128, 4, TT], BF16, 'qs')
        ks, ksb = stage([128, 4, TT], BF16, 'ks')
        szs, szsb = stage([128, nbl, 512], F32, 'szs')
        vs, vsb = stage([128, nbl, 512], BF16, 'vs')
        dts, dtsb = stage([128, nbl, 16], F32, 'dts')
        t1 = [P.sb([128, TT], F32, 't1') for _ in range(2)]
        t1b = [Buf(), Buf()]
        t2 = [P.sb([128, TT], F32, 't2') for _ in range(2)]
        t2b = [Buf(), Buf()]
        sp1 = P.sb([128, nbl, 16], F32, 'sp1')
        sp2 = P.sb([128, nbl, 16], F32, 'sp2')
        sp3 = P.sb([128, nbl, 16], F32, 'sp3')
        spb = Buf()
        tiles = []
        for (s0, T) in seqs:
            for t0 in range(0, T, TT):
                tiles.append((s0, t0))

        def load(i):
            s0, t0 = tiles[i]
            P.dma('sp', xt[i % 2][:], X[:, s0 + t0:s0 + t0 + TT].rearrange("(c p) t -> p c t", p=128), w=[xtb[i % 2]])
            for k_, nm in enumerate(('cosq', 'sinq', 'cosk', 'sink')):
                P.dma('sp', rope[i % 2][:, k_, :], Cd[nm][:, t0:t0 + TT], w=[ropeb[i % 2]])

        def proj_fm(wt, wbufs, c0):
            b = P.bank()
            for kc in range(NKC):
                mm(P, pbank(b, TT), wt[:, kc, c0:c0 + 128], xn[:, kc, :], kc == 0, kc == NKC - 1,
                   r=[wbufs[kc] if isinstance(wbufs, list) else wbufs, xnb], w=[pb[b]])
            return b
        load(0)
        if len(tiles) > 1:
            load(1)
        rmsnorm_fm(xt[0], xtb[0], nw, nwb, xn2[0], xn2b[0], TT, sq, sqb, rs, rsb)
        for i in range(len(tiles)):
            s0, t0 = tiles[i]
            g0 = s0 + t0
            rp, rpb = rope[i % 2], ropeb[i % 2]
            xn, xnb = xn2[i % 2], xn2b[i % 2]
            j = i % 2
            for c in range(4):
                b = proj_fm(win, winb, c * 128)
                cpy(P, 'act' if c % 2 else 'dve', lxs[j][:, c, :], pbank(b, TT), r=[pb[b]], w=[lxsb[j]])
            P.dma('pool', LX[:, g0:g0 + TT].rearrange("(c p) t -> p c t", p=128), lxs[j][:], r=[lxsb[j]])
            for c in range(4):
                b = proj_fm(win, winb, 512 + c * 128)
                act(P, lgs[j][:, c, :], pbank(b, TT), AF.Gelu_apprx_tanh, r=[pb[b]], w=[lgsb[j]])
            P.dma('pool', LG[:, g0:g0 + TT].rearrange("(c p) t -> p c t", p=128), lgs[j][:], r=[lgsb[j]])
            for c in range(6):
                b = proj_fm(win, winb, 1536 + c * 128)
                cpy(P, 'act' if c % 2 else 'dve', xbs[j][:, c, :], pbank(b, TT), r=[pb[b]], w=[xbsb[j]])
            P.dma('pool', XBC[:, g0:g0 + TT].rearrange("(c p) t -> p c t", p=128), xbs[j][:], r=[xbsb[j]])
            for c in range(4):
                b = proj_fm(win, winb, 3856 + c * 128)
                act(P, rgs[j][:, c, :], pbank(b, TT), AF.Silu, r=[pb[b]], w=[rgsb[j]])
            P.dma('pool', RG[:, g0:g0 + TT].rearrange("(c p) t -> p c t", p=128), rgs[j][:], r=[rgsb[j]])
            for qk, (stg, stgb, dst) in enumerate(((qs, qsb, Qd), (ks, ksb, Kd))):
                for hh in range(4):
                    b1 = proj_fm(win, winb, 2320 + qk * 512 + hh * 128)
                    b2 = proj_fm(wsw, wswb, (qk * 4 + hh) * 128)
                    jj = hh % 2
                    tt(P, 'dve', t1[jj][:], pbank(b1, TT), rp[:, 2 * qk, :], ALU.mult, r=[pb[b1], rpb], w=[t1b[jj]])
                    tt(P, 'dve', t2[jj][:], pbank(b2, TT), rp[:, 2 * qk + 1, :], ALU.mult, r=[pb[b2], rpb], w=[t2b[jj]])
                    tt(P, 'pool', stg[j][:, hh, :], t1[jj][:], t2[jj][:], ALU.add, r=[t1b[jj], t2b[jj]], w=[stgb[j]])
                P.dma('pool', dst[:, g0:g0 + TT].rearrange("(c p) t -> p c t", p=128), stg[j][:], r=[stgb[j]])
            if i + 1 < len(tiles):
                rmsnorm_fm(xt[(i + 1) % 2], xtb[(i + 1) % 2], nw, nwb, xn2[(i + 1) % 2], xn2b[(i + 1) % 2], TT, sq, sqb, rs, rsb)
            if i + 2 < len(tiles):
                load(i + 2)
            bdt = P.bank()
            for bl in range(nbl):
                b = P.bank()
                for kc in range(NKC):
                    mm(P, pbank(b), xn[:, kc, bl * 128:(bl + 1) * 128], win[:, kc, 1024:1536], kc == 0, kc == NKC - 1,
                       r=[winb[kc], xnb], w=[pb[b]])
                act(P, szs[j][:, bl, :], pbank(b), AF.Silu, r=[pb[b]], w=[szsb[j]])
                b = P.bank()
                if b == bdt:
                    b = P.bank()
                for kc in range(NKC):
                    mm(P, pbank(b), xn[:, kc, bl * 128:(bl + 1) * 128], win[:, kc, 3344:3856], kc == 0, kc == NKC - 1,
                       r=[winb[kc], xnb], w=[pb[b]])
                cpy(P, 'dve', vs[j][:, bl, :], pbank(b), r=[pb[b]], w=[vsb[j]])
                for kc in range(NKC):
                    mm(P, pbank(bdt)[:, bl * 16:(bl + 1) * 16], xn[:, kc, bl * 128:(bl + 1) * 128], win[:, kc, 2304:2320],
                       kc == 0, kc == NKC - 1, r=[winb[kc], xnb], w=[pb[bdt]])
            P.dma('pool', SZ[g0:g0 + TT, :].rearrange("(b p) f -> p b f", p=128), szs[j][:], r=[szsb[j]])
            P.dma('pool', Vd[g0:g0 + TT, :].rearrange("(b p) f -> p b f", p=128), vs[j][:], r=[vsb[j]])
            tt(P, 'dve', sp1[:], pbank(bdt)[:, 0:nbl * 16].rearrange("p (b h) -> p b h", h=16),
               dtb[:].unsqueeze(1).to_broadcast([128, nbl, 16]), ALU.add, r=[pb[bdt], dtbb], w=[spb])
            act(P, sp2[:], sp1[:], AF.Abs, r=[spb], w=[spb])
            act(P, sp3[:], sp2[:], AF.Exp, r=[spb], w=[spb], scale=-1.0)
            act(P, sp2[:], sp3[:], AF.Ln, r=[spb], w=[spb], bias=1.0)
            stt(P, dts[j][:], sp1[:], 0.0, sp2[:], ALU.max, ALU.add, r=[spb], w=[dtsb[j]])
            P.dma('pool', DT[g0:g0 + TT, :].rearrange("(b p) h -> p b h", p=128), dts[j][:], r=[dtsb[j]])
        P.release(m)

    def lru_pass(l):
        m = P.mark()
        cw = P.sb([128, 4, 4], F32, 'cw')
        cbv = P.sb([128, 4], F32, 'cbv')
        bab = P.sb([128, 2, 4], F32, 'bab')
        bib = P.sb([128, 2, 4], F32, 'bib')
        lam = P.sb([128, 2, 4], F32, 'lam')
        c1 = P.sb([128, 2, 4], F32, 'c1')
        c2 = P.sb([128, 2, 4], F32, 'c2')
        tl1 = P.sb([128, 2, 4], F32, 'tl1')
        tl2 = P.sb([128, 2, 4], F32, 'tl2')
        kb = Buf()
        for k_ in range(4):
            load_vec_pc(cw[:, :, k_], Wd['lru_conv_w'][l, k_], 4, kb)
        load_vec_pc(cbv[:], Wd['lru_conv_b'][l], 4, kb)
        for nm, dst in (('lru_b_a', bab), ('lru_b_i', bib), ('lru_lam', lam)):
            for d_ in range(2):
                load_vec_pc(dst[:, d_, :], Wd[nm][l, d_], 4, kb)
        act(P, tl1[:], lam[:], AF.Abs, r=[kb], w=[kb])
        act(P, tl2[:], tl1[:], AF.Exp, r=[kb], w=[kb], scale=-1.0)
        act(P, tl1[:], tl2[:], AF.Ln, r=[kb], w=[kb], bias=1.0)
        ts(P, 'dve', tl2[:], lam[:], -1.0, 0.0, ALU.mult, ALU.max, r=[kb], w=[kb])
        tt(P, 'dve', tl1[:], tl1[:], tl2[:], ALU.add, r=[kb], w=[kb])
        ts(P, 'dve', c1[:], tl1[:], -8.0, None, ALU.mult, None, r=[kb], w=[kb])
        ts(P, 'dve', c2[:], tl1[:], -16.0, None, ALU.mult, None, r=[kb], w=[kb])
        WA = P.sb([128, 8, 128], BF16, 'WA')
        WI = P.sb([128, 8, 128], BF16, 'WI')
        wb_ = Buf()
        mset(P, 'pool', WA[:], 0.0, [wb_])
        mset(P, 'pool', WI[:], 0.0, [wb_])
        for nm, dst in (('lru_w_a', WA), ('lru_w_i', WI)):
            for d_ in range(2):
                for r_ in range(2):
                    src = Wd[nm][l, d_].rearrange("(c r) i j -> r i c j", r=2)[r_]
                    P.dma('pool', dst[r_ * 64:(r_ + 1) * 64, d_ * 4:(d_ + 1) * 4, r_ * 64:(r_ + 1) * 64], src, w=[wb_])
        hba = P.sb([128, 2, 4], F32, 'hba')
        hbi = P.sb([128, 2, 4], F32, 'hbi')
        hc1 = P.sb([128, 2, 4], F32, 'hc1')
        ts(P, 'dve', hba[:], bab[:], 0.5, None, ALU.mult, None, r=[kb], w=[kb])
        ts(P, 'dve', hbi[:], bib[:], 0.5, None, ALU.mult, None, r=[kb], w=[kb])
        ts(P, 'dve', hc1[:], c1[:], 0.5, None, ALU.mult, None, r=[kb], w=[kb])
        m2 = P.mark()
        bset = 0
        for (s0, T) in seqs:
            TL = 1024 if T % 1024 == 0 else (512 if T % 512 == 0 else 128)
            ntl = T // TL
            NBT = 2
            nbatch = (ntl + NBT - 1) // NBT
            H = P.sb([128, T], F32, 'H')
            XC = P.sb([128, T], F32, 'XC')
            XCb = P.sb([128, T], BF16, 'XCb')
            Hb, XCbuf, XCbb = Buf(), Buf(), Buf()
            xraw = [P.sb([128, TL + 3], F32, 'xraw') for _ in range(2)]
            xrawb = [Buf(), Buf()]

            def ring(n, shape, dt, name):
                return [P.sb(shape, dt, name) for _ in range(n)], [Buf() for _ in range(n)]
            A_ = [P.sb([128, NBT * TL], F32, 'A') for _ in range(2)]
            S_ = [P.sb([128, NBT * TL], F32, 'S') for _ in range(2)]
            TI = [P.sb([128, NBT * TL], F32, 'TI') for _ in range(2)]
            A_b = [[Buf() for _ in range(NBT)] for _ in range(2)]
            S_b = [[Buf() for _ in range(NBT)] for _ in range(2)]
            TI_b = [[Buf() for _ in range(NBT)] for _ in range(2)]
            tha, thab = ring(2, [128, TL], F32, 'tha')
            hbk, hbkb = ring(2, [128, TL], F32, 'hbk')
            gt, gtb = ring(2, [128, TL], F32, 'gt')
            hs, hsb = ring(2, [128, TL], F32, 'hs')
            yst, ystb = ring(2, [128, TL], BF16, 'yst')
            cnt = 0
            gcnt = 0
            for cc in range(4):
                def gates(d_, t0, st, si, gk):
                    nb = max(1, TL // 512)
                    w_ = min(TL, 512)
                    if TL >= 1024:
                        b_a = P.bank2()
                        b_i = P.bank2()
                    else:
                        b_a = P.bank()
                        b_i = P.bank()
                    for bl in range(nb):
                        mm(P, pbank(b_a + bl, w_), WA[:, d_ * 4 + cc, :], XCb[:, t0 + bl * w_:t0 + (bl + 1) * w_], True, True,
                           r=[wb_, XCbb], w=[pb[b_a + bl]])
                        mm(P, pbank(b_i + bl, w_), WI[:, d_ * 4 + cc, :], XCb[:, t0 + bl * w_:t0 + (bl + 1) * w_], True, True,
                           r=[wb_, XCbb], w=[pb[b_i + bl]])
                    j = gk % 2
                    sl = slice(si * TL, (si + 1) * TL)
                    pa = ps[:, b_a * 512:b_a * 512 + TL]
                    pi = ps[:, b_i * 512:b_i * 512 + TL]
                    pra = [pb[b_a + x] for x in range(nb)]
                    pri = [pb[b_i + x] for x in range(nb)]
                    act(P, tha[j][:], pa, AF.Tanh, r=pra + [kb], w=[thab[j]], bias=hba[:, d_, cc:cc + 1], scale=0.5)
                    act(P, TI[st][:, sl], pi, AF.Tanh, r=pri + [kb], w=[TI_b[st][si]], bias=hbi[:, d_, cc:cc + 1], scale=0.5)
                    act(P, A_[st][:, sl], tha[j][:], AF.Exp, r=[thab[j], kb], w=[A_b[st][si]],
                        scale=hc1[:, d_, cc:cc + 1], bias=hc1[:, d_, cc:cc + 1])
                    act(P, S_[st][:, sl], tha[j][:], AF.Exp, r=[thab[j], kb], w=[S_b[st][si]],
                        scale=c1[:, d_, cc:cc + 1], bias=c1[:, d_, cc:cc + 1])

                def finish_batch(st, n):
                    act(P, S_[st][:, 0:n * TL], S_[st][:, 0:n * TL], AF.Sqrt, r=S_b[st][0:n], w=S_b[st][0:n], scale=-1.0, bias=1.0)

                def make_u(st, si, t0):
                    sl = slice(si * TL, (si + 1) * TL)
                    stt(P, TI[st][:, sl], TI[st][:, sl], 1.0, S_[st][:, sl], ALU.add, ALU.mult, r=[TI_b[st][si], S_b[st][si]], w=[TI_b[st][si]])
                    stt(P, TI[st][:, sl], TI[st][:, sl], 0.5, XC[:, t0:t0 + TL], ALU.mult, ALU.mult, r=[TI_b[st][si], XCbuf], w=[TI_b[st][si]])
                for bi in range(nbatch):
                    tiles = [k for k in range(bi * NBT, min(ntl, (bi + 1) * NBT))]
                    st = bset % 2
                    bset += 1
                    for si, k in enumerate(tiles):
                        t0 = k * TL
                        xr, xrb = xraw[cnt % 2], xrawb[cnt % 2]
                        cnt += 1
                        lo = 2 if k == 0 else 0
                        hi = 1 if k == ntl - 1 else 0
                        if lo:
                            mset(P, 'pool', xr[:, 0:2], 0.0, [xrb])
                        if hi:
                            mset(P, 'pool', xr[:, TL + 2:TL + 3], 0.0, [xrb])
                        P.dma('sp', xr[:, lo:TL + 3 - hi],
                              LX[cc * 128:(cc + 1) * 128, s0 + t0 - 2 + lo:s0 + t0 + TL + 1 - hi], w=[xrb])
                        xc = XC[:, t0:t0 + TL]
                        ts(P, 'dve', xc, xr[:, 0:TL], cw[:, cc, 0:1], cbv[:, cc:cc + 1], ALU.mult, ALU.add, r=[xrb, kb], w=[XCbuf])
                        for tap in range(1, 4):
                            stt(P, xc, xr[:, tap:tap + TL], cw[:, cc, tap:tap + 1], xc, ALU.mult, ALU.add, r=[xrb, kb, XCbuf], w=[XCbuf])
                        cpy(P, 'act', XCb[:, t0:t0 + TL], xc, r=[XCbuf], w=[XCbb])
                        gates(0, t0, st, si, gcnt)
                        gcnt += 1
                    finish_batch(st, len(tiles))
                    for si, k in enumerate(tiles):
                        t0 = k * TL
                        make_u(st, si, t0)
                        init = H[:, t0 - 1:t0] if k > 0 else 0.0
                        sl = slice(si * TL, (si + 1) * TL)
                        scan(P, H[:, t0:t0 + TL], A_[st][:, sl], TI[st][:, sl], init, r=[A_b[st][si], TI_b[st][si], Hb], w=[Hb])
                kk = 0
                for bi in range(nbatch - 1, -1, -1):
                    tiles = [k for k in range(min(ntl, (bi + 1) * NBT) - 1, bi * NBT - 1, -1)]
                    st = bset % 2
                    bset += 1
                    for si, k in enumerate(tiles):
                        gates(1, k * TL, st, si, gcnt)
                        gcnt += 1
                    finish_batch(st, len(tiles))
                    for si, k in enumerate(tiles):
                        t0 = k * TL
                        j2 = kk % 2
                        P.dma('sp', gt[j2][:], LG[cc * 128:(cc + 1) * 128, s0 + t0:s0 + t0 + TL], w=[gtb[j2]])
                        make_u(st, si, t0)
                        sl = slice(si * TL, (si + 1) * TL)
                        init = hbk[1 - j2][:, 0:1] if kk > 0 else 0.0
                        rr = [A_b[st][si], TI_b[st][si]] + ([hbkb[1 - j2]] if kk > 0 else [])
                        scan(P, hbk[j2][:, ::-1], A_[st][:, sl][:, ::-1], TI[st][:, sl][:, ::-1], init, r=rr, w=[hbkb[j2]])
                        tt(P, 'dve', hs[j2][:], hbk[j2][:], H[:, t0:t0 + TL], ALU.add, r=[hbkb[j2], Hb], w=[hsb[j2]])
                        tt(P, 'pool', yst[j2][:], hs[j2][:], gt[j2][:], ALU.mult, r=[hsb[j2], gtb[j2]], w=[ystb[j2]])
                        P.dma('pool', Y[cc * 128:(cc + 1) * 128, s0 + t0:s0 + t0 + TL], yst[j2][:], r=[ystb[j2]])
                        kk += 1
            P.release(m2)
        P.release(m)

    def ssd_pass(l):
        m = P.mark()
        scw = P.sb([128, 6, 4], F32, 'scw')
        scb = P.sb([128, 6], F32, 'scb')
        A16 = P.sb([128, 16], F32, 'A16')
        dsk = P.sb([128, 8], F32, 'dsk')
        snw = P.sb([128, 4], F32, 'snw')
        m_le = P.sb([128, 128], F32, 'm_le')
        m_ge = P.sb([128, 128], F32, 'm_ge')
        m_gt = P.sb([128, 128], F32, 'm_gt')
        m_lt = P.sb([128, 128], F32, 'm_lt')
        bmask = P.sb([128, 512], F32, 'bmask')
        kb = Buf()
        for k_ in range(4):
            load_vec_pc(scw[:, :, k_], Wd['ssd_conv_w'][l, k_], 6, kb)
        load_vec_pc(scb[:], Wd['ssd_conv_b'][l], 6, kb)
        load_vec_pc(snw[:], Wd['ssd_norm'][l], 4, kb)
        P.dma('sp', A16[:], Wd['ssd_a_log'][l:l + 1, :].to_broadcast([128, 16]), w=[kb])
        P.dma('sp', dsk[:], Wd['ssd_d'][l:l + 1, :].to_broadcast([128, 8]), w=[kb])
        for nm, dst in (('m_le', m_le), ('m_ge', m_ge), ('m_gt', m_gt), ('m_lt', m_lt), ('blockmask', bmask)):
            P.dma('sp', dst[:], Cd[nm], w=[kb])
        act(P, A16[:], A16[:], AF.Exp, r=[kb], w=[kb])
        ts(P, 'dve', A16[:], A16[:], -1.0, None, ALU.mult, None, r=[kb], w=[kb])
        m2 = P.mark()
        for (s0, T) in seqs:
            NCH = T // 128
            XS = P.sb([128, NCH, 512], BF16, 'XS')
            BT = P.sb([128, NCH, 128], BF16, 'BT')
            BCf = P.sb([128, 2, T], BF16, 'BCf')
            XSb, BTb, BCb = Buf(), Buf(), Buf()
            DTt = P.sb([128, NCH, 16], F32, 'DTt')
            dtA = P.sb([128, NCH, 16], F32, 'dtA')
            EAC = P.sb([128, NCH, 16], F32, 'EAC')
            DS = P.sb([128, NCH, 16], F32, 'DS')
            CDE = P.sb([128, NCH, 16], F32, 'CDE')
            sb_ = Buf()
            mt = P.mark()
            ACUM = P.sb([128, NCH, 16], F32, 'ACUM')
            TOT = P.sb([128, NCH, 16], F32, 'TOT')
            for c0 in range(0, NCH, 16):
                c1_ = min(NCH, c0 + 16)
                P.dma('sp', DTt[:, c0:c1_, :], DT[s0 + c0 * 128:s0 + c1_ * 128, :].rearrange("(c p) h -> p c h", p=128), w=[sb_])
            tt(P, 'dve', dtA[:], DTt[:], A16[:].unsqueeze(1).to_broadcast([128, NCH, 16]), ALU.mult, r=[sb_, kb], w=[sb_])
            ncol = NCH * 16
            dflat = dtA[:].rearrange("p c h -> p (c h)")
            for c0 in range(0, ncol, 512):
                w_ = min(512, ncol - c0)
                ch0, nch_ = c0 // 16, w_ // 16
                b1, b2, b3 = P.bank(), P.bank(), P.bank()
                mm(P, pbank(b1, w_), m_le[:], dflat[:, c0:c0 + w_], True, True, r=[kb, sb_], w=[pb[b1]])
                mm(P, pbank(b2, w_), m_ge[:], dflat[:, c0:c0 + w_], True, True, r=[kb, sb_], w=[pb[b2]])
                mm(P, pbank(b3, w_), ones[:], dflat[:, c0:c0 + w_], True, True, r=[cb_, sb_], w=[pb[b3]])
                cpy(P, 'dve', ACUM[:, ch0:ch0 + nch_, 0:8], pbank(b1, w_).rearrange("p (c h) -> p c h", h=16)[:, :, 0:8],
                    r=[pb[b1]], w=[sb_])
                cpy(P, 'dve', ACUM[:, ch0:ch0 + nch_, 8:16], pbank(b2, w_).rearrange("p (c h) -> p c h", h=16)[:, :, 8:16],
                    r=[pb[b2]], w=[sb_])
                cpy(P, 'act', TOT[:, ch0:ch0 + nch_, :], pbank(b3, w_).rearrange("p (c h) -> p c h", h=16), r=[pb[b3]], w=[sb_])
            act(P, EAC[:], ACUM[:], AF.Exp, r=[sb_], w=[sb_])
            act(P, CDE[:], TOT[:], AF.Exp, r=[sb_], w=[sb_])
            tt(P, 'dve', DS[:], TOT[:], ACUM[:], ALU.subtract, r=[sb_], w=[sb_])
            act(P, DS[:], DS[:], AF.Exp, r=[sb_], w=[sb_])
            tt(P, 'dve', DS[:], DS[:], DTt[:], ALU.mult, r=[sb_], w=[sb_])
            P.release(mt)
            if cfg.get('ssd_stop', 9) <= 1:
                P.release(m2)
                continue
            m3 = P.mark()
            TL = 512 if T % 512 == 0 else 128
            ntl = T // TL
            xr = [P.sb([128, 6, TL + 3], F32, 'xr') for _ in range(2)]
            xrb = [Buf(), Buf()]
            cv = P.sb([128, 6, TL], F32, 'cv')
            cvb = Buf()
            xsf = P.sb([128, 4, TL], BF16, 'xsf')
            xsfb = Buf()
            for k in range(ntl):
                t0 = k * TL
                x_, xb_ = xr[k % 2], xrb[k % 2]
                lo = 2 if k == 0 else 0
                hi = 1 if k == ntl - 1 else 0
                if lo:
                    mset(P, 'pool', x_[:, :, 0:2], 0.0, [xb_])
                if hi:
                    mset(P, 'pool', x_[:, :, TL + 2:TL + 3], 0.0, [xb_])
                P.dma('sp', x_[:, :, lo:TL + 3 - hi],
                      XBC[:, s0 + t0 - 2 + lo:s0 + t0 + TL + 1 - hi].rearrange("(c p) t -> p c t", p=128), w=[xb_])
                for c in range(6):
                    ts(P, 'dve', cv[:, c, :], x_[:, c, 0:TL], scw[:, c, 0:1], scb[:, c:c + 1], ALU.mult, ALU.add, r=[xb_, kb], w=[cvb])
                    for tap in range(1, 4):
                        stt(P, cv[:, c, :], x_[:, c, tap:tap + TL], scw[:, c, tap:tap + 1], cv[:, c, :], ALU.mult, ALU.add,
                            r=[xb_, kb, cvb], w=[cvb])
                if cfg.get('prep_stop', 9) <= 1:
                    continue
                act(P, xsf[:], cv[:, 0:4, :], AF.Silu, r=[cvb], w=[xsfb])
                act(P, BCf[:, :, t0:t0 + TL], cv[:, 4:6, :], AF.Silu, r=[cvb], w=[BCb])
                if cfg.get('prep_stop', 9) <= 2:
                    continue
                for bl in range(TL // 128):
                    c = t0 // 128 + bl
                    b = P.bank()
                    pv = pbankb(b)
                    for cc in range(4):
                        trp(P, pv[:, cc * 128:(cc + 1) * 128], xsf[:, cc, bl * 128:(bl + 1) * 128], identb[:], r=[xsfb, cb_], w=[pb[b]])
                    if cfg.get('prep_stop', 9) >= 4:
                        trp(P, pv[:, 512:640], BCf[:, 0, t0 + bl * 128:t0 + (bl + 1) * 128], identb[:], r=[BCb, cb_], w=[pb[b]])
                    if cfg.get('prep_stop', 9) <= 4:
                        continue
                    cpv = cfg.get('cpv', 0)
                    if cpv == 0:
                        cpy(P, 'act', XS[:, c, :], pv[:, 0:512], r=[pb[b]], w=[XSb])
                        cpy(P, 'act', BT[:, c, :], pv[:, 512:640], r=[pb[b]], w=[BTb])
                    elif cpv == 1:
                        cpy(P, 'act', XS[:, c, :], pv[:, 0:512], r=[pb[b]], w=[XSb])
                    elif cpv == 2:
                        cpy(P, 'act', BT[:, c, :], pv[:, 512:640], r=[pb[b]], w=[BTb])
                    elif cpv == 3:
                        cpy(P, 'dve', XS[:, c, :].bitcast(F32), pbank(b)[:, 0:256], r=[pb[b]], w=[XSb])
            P.release(m3)
            if cfg.get('ssd_stop', 9) <= 2:
                P.release(m2)
                continue

            def ring(n, shape, dt, name):
                return [P.sb(shape, dt, name) for _ in range(n)], [Buf() for _ in range(n)]
            prev = [P.sb([128, 512], F32, 'prev') for _ in range(2)]
            prevb_ = [Buf(), Buf()]
            tmp, tmpb = ring(2, [128, 512], F32, 'tmp')
            pvb, pvbb = ring(3, [128, 512], BF16, 'pvb')
            xds, xdsb = ring(2, [128, 512], BF16, 'xds')
            pfb = [Buf() for _ in range(NCH)]

            def v8(ap):
                return ap.rearrange("p (h x) -> p h x", h=8)

            def bc8(ap8, n=64):
                return ap8.unsqueeze(2).to_broadcast([128, 8, n])
            mset(P, 'dve', prev[0][:], 0.0, [prevb_[0]])
            mset(P, 'pool', pvb[0][:], 0.0, [pvbb[0]])
            def fwA(c):
                jx = c % 2
                tt(P, 'pool', v8(xds[jx][:]), v8(XS[:, c, :]), bc8(DS[:, c, 0:8]), ALU.mult, r=[XSb, sb_], w=[xdsb[jx]])
                mm(P, pbank(jx), BT[:, c, :], xds[jx][:], True, True, r=[BTb, xdsb[jx]], w=[pb[jx]])

            def fwB(c):
                b = c % 2
                tt(P, 'dve', v8(tmp[0][:]), v8(prev[0][:]), bc8(CDE[:, c, 0:8]), ALU.mult, r=[prevb_[0], sb_], w=[tmpb[0]])
                tt(P, 'dve', prev[0][:], tmp[0][:], pbank(b), ALU.add, r=[tmpb[0], pb[b]], w=[prevb_[0]])
                jn = (c + 1) % 3
                tt(P, 'dve', pvb[jn][:], prev[0][:], bmask[:], ALU.mult, r=[prevb_[0], kb], w=[pvbb[jn]])
                P.dma('sp', PF[c + 1], pvb[jn][:], r=[pvbb[jn]], w=[pfb[c + 1]])
            P.dma('sp', PF[0], pvb[0][:], r=[pvbb[0]], w=[pfb[0]])
            for step in range(NCH):
                if 0 <= step - 1 < NCH - 1:
                    fwB(step - 1)
                if step < NCH - 1:
                    fwA(step)
            if cfg.get('ssd_stop', 9) <= 3:
                P.release(m2)
                continue
            rhs, rhsb = ring(2, [128, 1024], F32, 'rhs')
            E, Eb = ring(2, [128, 1024], F32, 'E')
            CBm, CBmb = ring(2, [128, 256], F32, 'CBm')
            Wt = [[P.sb([128, 1024], BF16, 'Wt') for _ in range(2)] for _ in range(2)]
            Wtb = [[Buf(), Buf()], [Buf(), Buf()]]
            xdt = [[P.sb([128, 512], BF16, 'xdt') for _ in range(2)] for _ in range(2)]
            xdtb = [[Buf(), Buf()], [Buf(), Buf()]]
            xd, xdb = ring(2, [128, 512], BF16, 'xd')
            xdo, xdob = xds, xdsb
            pfl, pflb = ring(2, [128, 512], BF16, 'pfl')
            szt, sztb = ring(3, [128, 512], F32, 'szt')
            y1 = P.sb([128, 512], F32, 'y1')
            y2 = P.sb([128, 512], F32, 'y2')
            y1b, y2b = Buf(), Buf()
            y3, y3b = ring(2, [128, 512], F32, 'y3')
            yn, ynb = ring(2, [128, 512], F32, 'yn')
            ssq, ssqb = ring(2, [128, 2], F32, 'ssq')
            yst, ystb = ring(2, [128, 4, 128], BF16, 'yst')
            pq, pqb = ring(2, [128, 512], BF16, 'pq')
            cml = m_le
            cmg = m_ge
            mset(P, 'dve', prev[1][:], 0.0, [prevb_[1]])
            mset(P, 'pool', pq[0][:], 0.0, [pqb[0]])
            BK_SF, BK_SB, BK_S, BK_YD, BK_OF, BK_OB = 0, 2, 4, 5, 6, 7

            def stA(kk):
                c = NCH - 1 - kk
                p2, p3 = kk % 2, kk % 3
                tok = slice(c * 128, (c + 1) * 128)
                P.dma('sp', pfl[p2][:], PF[c], r=[pfb[c]], w=[pflb[p2]])
                P.dma('sp', szt[p3][:], SZ[s0 + c * 128:s0 + (c + 1) * 128, :], w=[sztb[p3]])
                for d_, (mrhs, mlhs, bk) in enumerate(((m_le, m_gt, BK_SF), (m_ge, m_lt, BK_SB))):
                    tt(P, 'pool', rhs[d_][:].rearrange("p (h i) -> p h i", h=8), mrhs[:].unsqueeze(1).to_broadcast([128, 8, 128]),
                       dtA[:, c, d_ * 8:(d_ + 1) * 8].unsqueeze(2).to_broadcast([128, 8, 128]), ALU.mult, r=[kb, sb_], w=[rhsb[d_]])
                    mm(P, pbank(bk), mlhs[:], rhs[d_][:, 0:512], True, True, r=[kb, rhsb[d_]], w=[pb[bk]])
                    mm(P, pbank(bk + 1), mlhs[:], rhs[d_][:, 512:1024], True, True, r=[kb, rhsb[d_]], w=[pb[bk + 1]])
                    act(P, E[d_][:], ps[:, bk * 512:bk * 512 + 1024], AF.Exp, r=[pb[bk], pb[bk + 1]], w=[Eb[d_]])
                for g in range(2):
                    mm(P, pbank(BK_SF + g, 128), BCf[g * 64:(g + 1) * 64, 0, tok], BCf[g * 64:(g + 1) * 64, 1, tok],
                       True, True, r=[BCb], w=[pb[BK_SF + g]])
                for d_, cmk in enumerate((cml, cmg)):
                    for g in range(2):
                        tt(P, 'dve', CBm[d_][:, g * 128:(g + 1) * 128], pbank(BK_SF + g, 128), cmk[:], ALU.mult,
                           r=[pb[BK_SF + g], kb], w=[CBmb[d_]])
                for d_ in range(2):
                    tt(P, 'dve', Wt[d_][p2][:].rearrange("p (g k i) -> p g k i", g=2, k=4),
                       E[d_][:].rearrange("p (g k i) -> p g k i", g=2, k=4),
                       CBm[d_][:].rearrange("p (g i) -> p g i", g=2).unsqueeze(2).to_broadcast([128, 2, 4, 128]), ALU.mult,
                       r=[Eb[d_], CBmb[d_]], w=[Wtb[d_][p2]])
                    tt(P, 'pool', v8(xdt[d_][p2][:]), v8(XS[:, c, :]), bc8(DTt[:, c, d_ * 8:(d_ + 1) * 8]), ALU.mult,
                       r=[XSb, sb_], w=[xdtb[d_][p2]])
                tt(P, 'pool', v8(xd[p2][:]), v8(XS[:, c, :]), bc8(dsk[:, 0:8]), ALU.mult, r=[XSb, kb], w=[xdb[p2]])
                if c > 0:
                    tt(P, 'pool', v8(xdo[p2][:]), v8(XS[:, c, :]), bc8(DS[:, c, 8:16]), ALU.mult, r=[XSb, sb_], w=[xdob[p2]])

            def stB(kk):
                c = NCH - 1 - kk
                p2 = kk % 2
                tok = slice(c * 128, (c + 1) * 128)
                mm(P, pbank(BK_YD), identb[:], xd[p2][:], True, False, r=[cb_, xdb[p2]], w=[pb[BK_YD]])
                for hh in range(8):
                    for d_ in range(2):
                        mm(P, pbank(BK_YD)[:, hh * 64:(hh + 1) * 64], Wt[d_][p2][:, hh * 128:(hh + 1) * 128],
                           xdt[d_][p2][:, hh * 64:(hh + 1) * 64], False, (hh == 7 and d_ == 1),
                           r=[Wtb[d_][p2], xdtb[d_][p2]], w=[pb[BK_YD]])
                mm(P, pbank(BK_OF), BCf[:, 1, tok], pfl[p2][:], True, True, r=[BCb, pflb[p2]], w=[pb[BK_OF]])
                mm(P, pbank(BK_OB), BCf[:, 1, tok], pq[p2][:], True, True, r=[BCb, pqb[p2]], w=[pb[BK_OB]])
                if c > 0:
                    mm(P, pbank(BK_S), BT[:, c, :], xdo[p2][:], True, True, r=[BTb, xdob[p2]], w=[pb[BK_S]])
                tt(P, 'dve', v8(y1[:]), v8(pbank(BK_OF)), bc8(EAC[:, c, 0:8]), ALU.mult, r=[pb[BK_OF], sb_], w=[y1b])
                tt(P, 'dve', v8(y2[:]), v8(pbank(BK_OB)), bc8(EAC[:, c, 8:16]), ALU.mult, r=[pb[BK_OB], sb_], w=[y2b])
                tt(P, 'dve', y3[p2][:], y1[:], pbank(BK_YD), ALU.add, r=[y1b, pb[BK_YD]], w=[y3b[p2]])
                tt(P, 'dve', y3[p2][:], y3[p2][:], y2[:], ALU.add, r=[y3b[p2], y2b], w=[y3b[p2]])
                if c > 0:
                    tt(P, 'dve', v8(tmp[1][:]), v8(prev[1][:]), bc8(CDE[:, c, 8:16]), ALU.mult, r=[prevb_[1], sb_], w=[tmpb[1]])
                    tt(P, 'dve', prev[1][:], tmp[1][:], pbank(BK_S), ALU.add, r=[tmpb[1], pb[BK_S]], w=[prevb_[1]])
                    tt(P, 'dve', pq[1 - p2][:], prev[1][:], bmask[:], ALU.mult, r=[prevb_[1], kb], w=[pqb[1 - p2]])

            def stC(kk):
                c = NCH - 1 - kk
                p2, p3 = kk % 2, kk % 3
                tt(P, 'dve', y3[p2][:], y3[p2][:], szt[p3][:], ALU.mult, r=[y3b[p2], sztb[p3]], w=[y3b[p2]])
                act(P, yn[p2][:], y3[p2][:], AF.Square, r=[y3b[p2]], w=[ynb[p2], ssqb[p2]], accum=ssq[p2][:, 0:1])
                act(P, ssq[p2][:, 1:2], ssq[p2][:, 0:1], AF.Ln, r=[ssqb[p2], cb_], w=[ssqb[p2]], bias=epst[:, 0:1], scale=1.0 / 512)
                act(P, ssq[p2][:, 1:2], ssq[p2][:, 1:2], AF.Exp, r=[ssqb[p2]], w=[ssqb[p2]], scale=-0.5)
                amul(P, yn[p2][:], y3[p2][:], ssq[p2][:, 1:2], r=[y3b[p2], ssqb[p2], ynb[p2]], w=[ynb[p2]])
                for cc in range(4):
                    trp(P, pbank(BK_YD)[:, cc * 128:(cc + 1) * 128], yn[p2][:, cc * 128:(cc + 1) * 128], ident[:],
                        r=[ynb[p2], cb_], w=[pb[BK_YD]])
                for cc in range(4):
                    amul(P, yst[p2][:, cc, :], pbank(BK_YD)[:, cc * 128:(cc + 1) * 128], snw[:, cc:cc + 1],
                         r=[pb[BK_YD], kb], w=[ystb[p2]])
                P.dma('act', Y[512:1024, s0 + c * 128:s0 + (c + 1) * 128].rearrange("(c p) t -> p c t", p=128), yst[p2][:], r=[ystb[p2]])
            for step in range(NCH + 2):
                if 0 <= step - 1 < NCH:
                    stB(step - 1)
                if 0 <= step - 2 < NCH:
                    stC(step - 2)
                if step < NCH:
                    stA(step)
            P.release(m2)
        P.release(m)

    def ret_pass(l):
        m = P.mark()
        rnw = P.sb([128, 4], F32, 'rnw')
        dmask = P.sb([128, 4, 128], F32, 'dmask')
        gf = P.sb([128, 4, 128], F32, 'gf')
        gb = P.sb([128, 4, 128], F32, 'gb')
        wfb = P.sb([128, 8], F32, 'wfb')
        cdec = P.sb([128, 4], F32, 'cdec')
        o128 = P.sb([128, 128], F32, 'o128')
        kb = Buf()
        load_vec_pc(rnw[:], Wd['ret_norm'][l], 4, kb)
        for nm, dst in (('dmask', dmask), ('gf', gf), ('gb', gb), ('wfb', wfb), ('cdec', cdec)):
            P.dma('sp', dst[:], Cd[nm], w=[kb])
        mset(P, 'dve', o128[:], 1.0 / 128, [kb])
        m2 = P.mark()

        def ring(n, shape, dt, name):
            return [P.sb(shape, dt, name) for _ in range(n)], [Buf() for _ in range(n)]

        def v4(ap):
            return ap.rearrange("p (h x) -> p h x", h=4)

        def bc4(ap4):
            return ap4.unsqueeze(2).to_broadcast([128, 4, 128])
        for (s0, T) in seqs:
            NCH = T // 128
            RF = P.sb([128, NCH, 512], BF16, 'RF')
            RFb = Buf()
            kt, ktb = ring(3, [128, 4, 128], BF16, 'kt')
            vt, vtb = ring(3, [128, 512], BF16, 'vt')
            qt, qtb = ring(2, [128, 4, 128], BF16, 'qt')
            rgt, rgtb = ring(2, [128, 4, 128], F32, 'rgt')
            ktm, ktmb = ring(2, [128, 512], BF16, 'ktm')
            vw, vwb = ring(2, [128, 512], BF16, 'vw')
            r_ = [P.sb([128, 512], F32, 'r') for _ in range(2)]
            rb_ = [Buf(), Buf()]
            tmp, tmpb = ring(2, [128, 512], F32, 'tmp')
            cnt = 0

            def kv_step(c, d_, cnt, btr, bkv):
                j3 = cnt % 3
                j2 = cnt % 2
                P.dma('sp', kt[j3][:], Kd[:, s0 + c * 128:s0 + (c + 1) * 128].rearrange("(h p) t -> p h t", p=128), w=[ktb[j3]])
                P.dma('sp', vt[j3][:], Vd[s0 + c * 128:s0 + (c + 1) * 128, :], w=[vtb[j3]])
                b = btr
                pv = pbankb(b)
                for hh in range(4):
                    trp(P, pv[:, hh * 128:(hh + 1) * 128], kt[j3][:, hh, :], identb[:], r=[ktb[j3], cb_], w=[pb[b]])
                cpy(P, 'act', ktm[j2][:], pv[:, 0:512], r=[pb[b]], w=[ktmb[j2]])
                tt(P, 'pool', v4(vw[j2][:]), v4(vt[j3][:]), bc4(wfb[:, d_ * 4:(d_ + 1) * 4]), ALU.mult, r=[vtb[j3], kb], w=[vwb[j2]])
                b = bkv
                for hh in range(4):
                    mm(P, pbank(b)[:, hh * 128:(hh + 1) * 128], ktm[j2][:, hh * 128:(hh + 1) * 128], vw[j2][:, hh * 128:(hh + 1) * 128],
                       True, True, r=[ktmb[j2], vwb[j2]], w=[pb[b]])
                return b, j3

            def state_update(d_, b):
                tt(P, 'dve', v4(tmp[d_][:]), v4(r_[d_][:]), bc4(cdec[:, 0:4]), ALU.mult, r=[rb_[d_], kb], w=[tmpb[d_]])
                tt(P, 'dve', r_[d_][:], tmp[d_][:], pbank(b), ALU.add, r=[tmpb[d_], pb[b]], w=[rb_[d_]])
            mset(P, 'dve', r_[0][:], 0.0, [rb_[0]])
            cpy(P, 'act', RF[:, 0, :], r_[0][:], r=[rb_[0]], w=[RFb])
            for step in range(NCH):
                if 0 <= step - 1 < NCH - 1:
                    c = step - 1
                    state_update(0, 3 + c % 2)
                    cpy(P, 'act', RF[:, c + 1, :], r_[0][:], r=[rb_[0]], w=[RFb])
                if step < NCH - 1:
                    kv_step(step, 0, cnt, 2, 3 + step % 2)
                    cnt += 1
            Sm, Smb = ring(2, [128, 4, 128], BF16, 'Sm')
            qf, qfb = ring(2, [128, 4, 128], BF16, 'qf')
            qb, qbb = ring(2, [128, 4, 128], BF16, 'qb')
            rbb = P.sb([128, 512], BF16, 'rbb')
            rbbb = Buf()
            kt2, kt2b = ring(3, [128, 4, 128], BF16, 'kt2')
            vt2, vt2b = ring(3, [128, 512], BF16, 'vt2')
            rg4, rg4b = ring(5, [128, 4, 128], F32, 'rg4')
            qt3, qt3b = ring(3, [128, 4, 128], BF16, 'qt3')
            ktm1 = P.sb([128, 512], BF16, 'ktm1')
            ktm1b = Buf()
            vw1 = P.sb([128, 512], BF16, 'vw1')
            vw1b = Buf()
            ysb, ysbb = ring(3, [128, 512], F32, 'ysb')
            ysq, ysqb = ring(2, [128, 512], F32, 'ysq')
            msb, msbb = ring(2, [128, 512], F32, 'msb')
            var, varb = ring(2, [128, 512], F32, 'var')
            m2t = P.sb([128, 512], F32, 'm2t')
            m2b = Buf()
            dd = P.sb([128, 512], F32, 'dd')
            ddb = Buf()
            ost, ostb = ring(2, [128, 4, 128], BF16, 'ost')
            mset(P, 'dve', r_[1][:], 0.0, [rb_[1]])
            BK_TR, BK_KV, BK_ST, BK_Y, BK_M, BK_Q = 0, 1, 3, 4, 5, 6

            def rLoad(kk):
                c = NCH - 1 - kk
                q3, p5 = kk % 3, kk % 5
                tk = slice(s0 + c * 128, s0 + (c + 1) * 128)
                P.dma('sp', qt3[q3][:], Qd[:, tk].rearrange("(h p) t -> p h t", p=128), w=[qt3b[q3]])
                P.dma('sp', rg4[p5][:], RG[:, tk].rearrange("(h p) t -> p h t", p=128), w=[rg4b[p5]])
                P.dma('sp', kt2[q3][:], Kd[:, tk].rearrange("(h p) t -> p h t", p=128), w=[kt2b[q3]])
                P.dma('sp', vt2[q3][:], Vd[tk, :], w=[vt2b[q3]])

            def rA(kk):
                c = NCH - 1 - kk
                p2, q3 = kk % 2, kk % 3
                pv = pbankb(BK_TR)
                for hh in range(4):
                    trp(P, pv[:, hh * 128:(hh + 1) * 128], kt2[q3][:, hh, :], identb[:], r=[kt2b[q3], cb_], w=[pb[BK_TR]])
                cpy(P, 'act', ktm1[:], pv[:, 0:512], r=[pb[BK_TR]], w=[ktm1b])
                tt(P, 'pool', v4(vw1[:]), v4(vt2[q3][:]), bc4(wfb[:, 4:8]), ALU.mult, r=[vt2b[q3], kb], w=[vw1b])
                bkv = BK_KV + p2
                for hh in range(4):
                    hs = slice(hh * 128, (hh + 1) * 128)
                    mm(P, pbank(bkv)[:, hs], ktm1[:, hs], vw1[:, hs], True, True, r=[ktm1b, vw1b], w=[pb[bkv]])
                for hh in range(4):
                    mm(P, pbank(BK_ST)[:, hh * 128:(hh + 1) * 128], kt2[q3][:, hh, :], qt3[q3][:, hh, :], True, True,
                       r=[kt2b[q3], qt3b[q3]], w=[pb[BK_ST]])
                tt(P, 'dve', Sm[p2][:], pbank(BK_ST).rearrange("p (h i) -> p h i", h=4), dmask[:], ALU.mult, r=[pb[BK_ST], kb], w=[Smb[p2]])
                tt(P, 'pool', qf[p2][:], qt3[q3][:], gf[:], ALU.mult, r=[qt3b[q3], kb], w=[qfb[p2]])
                tt(P, 'pool', qb[p2][:], qt3[q3][:], gb[:], ALU.mult, r=[qt3b[q3], kb], w=[qbb[p2]])

            def rB1(kk):
                c = NCH - 1 - kk
                p2, p3 = kk % 2, kk % 3
                cpy(P, 'act', rbb[:], r_[1][:], r=[rb_[1]], w=[rbbb])
                for hh in range(4):
                    o_ = pbank(BK_Y)[:, hh * 128:(hh + 1) * 128]
                    hs = slice(hh * 128, (hh + 1) * 128)
                    mm(P, o_, vt2[kk % 3][:, hs], Sm[p2][:, hh, :], True, False, r=[vt2b[kk % 3], Smb[p2]], w=[pb[BK_Y]])
                    mm(P, o_, RF[:, c, hs], qf[p2][:, hh, :], False, False, r=[RFb, qfb[p2]], w=[pb[BK_Y]])
                    mm(P, o_, rbb[:, hs], qb[p2][:, hh, :], False, True, r=[rbbb, qbb[p2]], w=[pb[BK_Y]])
                if c > 0:
                    state_update(1, BK_KV + p2)
                cpy(P, 'act', ysb[p3][:], pbank(BK_Y), r=[pb[BK_Y]], w=[ysbb[p3]])
                act(P, ysq[p2][:], pbank(BK_Y), AF.Square, r=[pb[BK_Y]], w=[ysqb[p2]])

            def rB2a(kk):
                p2, p3 = kk % 2, kk % 3
                mm(P, pbank(BK_M), o128[:], ysb[p3][:], True, True, r=[kb, ysbb[p3]], w=[pb[BK_M]])
                mm(P, pbank(BK_Q), o128[:], ysq[p2][:], True, True, r=[kb, ysqb[p2]], w=[pb[BK_Q]])
                cpy(P, 'act', msb[p2][:], pbank(BK_M), r=[pb[BK_M]], w=[msbb[p2]])
                act(P, m2t[:], msb[p2][:], AF.Square, r=[msbb[p2]], w=[m2b])

            def rB2b(kk):
                p2 = kk % 2
                stt(P, var[p2][:], m2t[:], -1.0, pbank(BK_Q), ALU.mult, ALU.add, r=[m2b, pb[BK_Q]], w=[varb[p2]])

            def rC1(kk):
                p2, p3 = kk % 2, kk % 3
                act(P, var[p2][:], var[p2][:], AF.Sqrt, r=[varb[p2], cb_], w=[varb[p2]], bias=epst[:, 0:1], scale=1.0)
                tt(P, 'dve', dd[:], ysb[p3][:], msb[p2][:], ALU.subtract, r=[ysbb[p3], msbb[p2]], w=[ddb])

            def rC2(kk):
                c = NCH - 1 - kk
                p2, p3, p4 = kk % 2, kk % 3, kk % 4
                recip(P, var[p2][:], var[p2][:], r=[varb[p2]], w=[varb[p2]])
                tt(P, 'dve', dd[:], dd[:], var[p2][:], ALU.mult, r=[ddb, varb[p2]], w=[ddb])
                tt(P, 'dve', v4(dd[:]), v4(dd[:]), bc4(rnw[:, 0:4]), ALU.mult, r=[ddb, kb], w=[ddb])
                tt(P, 'dve', ost[p2][:], v4(dd[:]), rg4[kk % 5][:], ALU.mult, r=[ddb, rg4b[kk % 5]], w=[ostb[p2]])
                P.dma('sp', Y[1024:1536, s0 + c * 128:s0 + (c + 1) * 128].rearrange("(h p) t -> p h t", p=128), ost[p2][:], r=[ostb[p2]])
            rLoad(0)
            for step in range(NCH + 3):
                if step + 1 < NCH:
                    rLoad(step + 1)
                if 0 <= step - 3 < NCH:
                    rC1(step - 3)
                if 0 <= step - 1 < NCH:
                    rB1(step - 1)
                if step < NCH:
                    rA(step)
                if 0 <= step - 2 < NCH:
                    rB2a(step - 2)
                if 0 <= step - 3 < NCH:
                    rC2(step - 3)
                if 0 <= step - 2 < NCH:
                    rB2b(step - 2)
            P.release(m2)
        P.release(m)

    def s4a_pass(l):
        m = P.mark()
        TT = cfg['TT4']
        wo = P.sb([128, 12, D], BF16, 'wo')
        wob = [Buf() for _ in range(12)]
        for c in range(12):
            P.dma('pool', wo[:, c, :], Wd['w_out'][l, c * 128:(c + 1) * 128, :], w=[wob[c]])
        xt = [P.sb([128, NKC, TT], F32, 'xt') for _ in range(2)]
        xtb = [Buf(), Buf()]
        yt = [P.sb([128, 12, TT], BF16, 'yt') for _ in range(2)]
        ytb = [Buf(), Buf()]
        ntile = NT // TT

        def load(i):
            P.dma('sp', xt[i % 2][:], X[:, i * TT:(i + 1) * TT].rearrange("(c p) t -> p c t", p=128), w=[xtb[i % 2]])
            P.dma('sp', yt[i % 2][:], Y[:, i * TT:(i + 1) * TT].rearrange("(c p) t -> p c t", p=128), w=[ytb[i % 2]])
        load(0)
        for i in range(ntile):
            if i + 1 < ntile:
                load(i + 1)
            x_, xb_, y_, yb_ = xt[i % 2], xtb[i % 2], yt[i % 2], ytb[i % 2]
            for n in range(NKC):
                b = P.bank()
                for c in range(12):
                    mm(P, pbank(b, TT), wo[:, c, n * 128:(n + 1) * 128], y_[:, c, :], c == 0, c == 11, r=[wob[c], yb_], w=[pb[b]])
                tt(P, 'dve', x_[:, n, :], x_[:, n, :], pbank(b, TT), ALU.add, r=[xb_, pb[b]], w=[xb_])
            P.dma('pool', X[:, i * TT:(i + 1) * TT].rearrange("(c p) t -> p c t", p=128), x_[:], r=[xb_])
        P.release(m)

    def s5_pass():
        m = P.mark()
        nw = P.sb([128, NKC], F32, 'nw')
        nwb = Buf()
        load_vec_pc(nw[:], Wd['final_norm'][0], NKC, nwb)
        xt = [P.sb([128, NKC, 512], F32, 'xt') for _ in range(2)]
        xtb = [Buf(), Buf()]
        xo = P.sb([128, NKC, 512], F32, 'xo')
        xob = Buf()
        otm = [P.sb([128, 4, D], F32, 'otm') for _ in range(2)]
        otmb = [Buf(), Buf()]
        sq = [P.sb([128, 512], F32, 'sq') for _ in range(3)]
        sqb = [Buf() for _ in range(3)]
        rs = P.sb([128, 512], F32, 'rs')
        rsb = Buf()
        tiles = []
        for (dst, (s0, T)) in zip((ya, yb), seqs):
            TT = 512 if T % 512 == 0 else 128
            for t0 in range(0, T, TT):
                tiles.append((dst, s0, t0, TT))

        def load(i):
            dst, s0, t0, TT = tiles[i]
            P.dma('sp', xt[i % 2][:, :, 0:TT], X[:, s0 + t0:s0 + t0 + TT].rearrange("(c p) t -> p c t", p=128), w=[xtb[i % 2]])
        load(0)
        for i in range(len(tiles)):
            if i + 1 < len(tiles):
                load(i + 1)
            dst, s0, t0, TT = tiles[i]
            x_, xb_ = xt[i % 2], xtb[i % 2]
            rmsnorm_fm(x_[:, :, 0:TT], xb_, nw, nwb, xo[:, :, 0:TT], xob, TT, sq, sqb, rs, rsb)
            o_, ob_ = otm[i % 2], otmb[i % 2]
            for bl in range(TT // 128):
                for half in range(2):
                    b = P.bank()
                    for q in range(4):
                        kc = half * 4 + q
                        trp(P, pbank(b)[:, q * 128:(q + 1) * 128], xo[:, kc, bl * 128:(bl + 1) * 128], ident[:], r=[xob, cb_], w=[pb[b]])
                    cpy(P, 'act' if half else 'dve', o_[:, bl, half * 512:(half + 1) * 512], pbank(b), r=[pb[b]], w=[ob_])
            P.dma('pool', dst[t0:t0 + TT, :].rearrange("(b p) f -> p b f", p=128), o_[:, 0:TT // 128, :], r=[ob_])
        P.release(m)

    stages = cfg.get('stages', None)

    def on(s):
        return stages is None or s in stages
    if on('s0'):
        s0_pass()
    for l in range(DEPTH):
        if on('s1'):
            ffn_pass(l, 'ffn1')
        if on('s2'):
            s2_pass(l)
        if on('lru'):
            lru_pass(l)
        if on('ssd'):
            ssd_pass(l)
        if on('ret'):
            ret_pass(l)
        if on('s4a'):
            s4a_pass(l)
        if on('s4b'):
            ffn_pass(l, 'ffn2')
    if on('s5'):
        s5_pass()
    P.barrier()
    P.emit()
    return nc, consts, P


CFG_FULL = dict(TA=8192, TB=4096, DEPTH=4, TTF=384, TT2=256, TT4=512, LW=4)
_CACHE = {}


def kernel(**inputs):
    cfg = CFG_FULL
    if 'nc' not in _CACHE:
        _CACHE['nc'] = build(cfg)
    nc, consts, _ = _CACHE['nc']
    xp = np.asarray(inputs['x_prompt'], dtype=np.float32)
    xs = np.asarray(inputs['x_sample'], dtype=np.float32)
    shared = {}
    for k, v in inputs.items():
        if k in ('x_prompt', 'x_sample'):
            continue
        a = np.ascontiguousarray(np.asarray(v, dtype=np.float32))
        if k in ('ssd_dt_bias', 'ssd_a_log'):
            a = a.reshape(a.shape[0], 16)
        if k == 'final_norm':
            a = a.reshape(1, D)
        shared[k] = a
    for k, v in consts.items():
        shared['c_' + k] = v
    in_maps = []
    for i in range(8):
        mp = dict(shared)
        mp['xa'] = np.ascontiguousarray(xp[i])
        mp['xb'] = np.ascontiguousarray(xs[i % 4])
        in_maps.append(mp)
    res = run_bass_kernel_spmd(nc, in_maps, core_ids=list(range(8)))
    yp = np.stack([np.asarray(res.results[i]['ya'], dtype=np.float32) for i in range(8)], 0)
    ysm = np.stack([np.asarray(res.results[i]['yb'], dtype=np.float32) for i in range(4)], 0)
    return (yp, ysm)
```

```python
import contextlib
import numpy as np
import ml_dtypes
import concourse.bass as bass
import concourse.mybir as mybir
from concourse.bass_utils import run_bass_kernel_spmd

F32 = mybir.dt.float32
BF16 = mybir.dt.bfloat16
AF = mybir.ActivationFunctionType
ALU = mybir.AluOpType

D = 1024
DFF = 2816
NKC = 8
NFC = 22
DPROJ = 4368
EPS = 1e-6
COMPUTE = ('pe', 'act', 'dve', 'pool')
ENGS = ('pe', 'act', 'dve', 'pool', 'sp')
SB_BASE = 16640
SB_END = 229376
DMA_RING = {'sp': 16, 'act': 4, 'pool': 16}


class Buf:
    __slots__ = ('w', 'r')

    def __init__(self):
        self.w = {}
        self.r = {}


class Prog:
    def __init__(self, nc):
        self.nc = nc
        self.ops = {e: [] for e in ENGS}
        self.known = {e: {} for e in ENGS}
        self.ndma = {e: 0 for e in DMA_RING}
        self.dma_last = {}
        self.flag = {e: set() for e in COMPUTE}
        self.sb_off = SB_BASE
        self.sb_cnt = 0
        self.sb_max = 0
        self.pb = 0

    def sb(self, shape, dtype, name='t'):
        esz = 4 if dtype == F32 else 2
        per_part = int(np.prod(shape[1:])) * esz
        off = (self.sb_off + 63) // 64 * 64
        self.sb_cnt += 1
        h = self.nc.alloc_sbuf_tensor_at(f"{name}{self.sb_cnt}", list(shape), dtype, offset=off)
        self.sb_off = off + per_part
        self.sb_max = max(self.sb_max, self.sb_off)
        assert self.sb_off <= SB_END, f"SBUF overflow {self.sb_off} at {name}"
        return h

    def mark(self):
        return self.sb_off

    def release(self, m):
        self.barrier()
        self.sb_off = m

    def _need(self, eng, toks):
        out = []
        kn = self.known[eng]
        for src, idx in toks.items():
            if kn.get(src, -1) >= idx:
                continue
            kn[src] = idx
            out.append((src, idx))
            if src in COMPUTE:
                self.flag[src].add(idx)
        return out

    def op(self, eng, fn, r=(), w=()):
        deps = {}
        for b in r:
            for src, idx in b.w.items():
                if deps.get(src, -1) < idx:
                    deps[src] = idx
        for b in w:
            for src, idx in b.w.items():
                if src != eng and deps.get(src, -1) < idx:
                    deps[src] = idx
            for src, idx in b.r.items():
                if src != eng and deps.get(src, -1) < idx:
                    deps[src] = idx
        waits = self._need(eng, deps)
        idx = len(self.ops[eng])
        self.ops[eng].append(('c', fn, waits))
        for b in r:
            b.r[eng] = idx
        for b in w:
            b.w = {eng: idx}
            b.r = {}
        return idx

    def dma(self, q, out, in_, r=(), w=(), **kw):
        deps = {}
        for b in r:
            for src, idx in b.w.items():
                if deps.get(src, -1) < idx:
                    deps[src] = idx
        for b in w:
            for src, idx in list(b.w.items()) + list(b.r.items()):
                if deps.get(src, -1) < idx:
                    deps[src] = idx
        n = self.ndma[q]
        self.ndma[q] = n + 1
        K = DMA_RING[q]
        key = ('dma', q, n % K)
        val = 16 * (n // K + 1)
        if n >= K:
            deps[key] = max(deps.get(key, -1), val - 16)
        waits = self._need(q, deps)
        self.dma_last[key] = val
        self.ops[q].append(('d', (out, in_, kw), waits, key))
        for b in r:
            b.r[key] = val
        for b in w:
            b.w = {key: val}
            b.r = {}

    def barrier(self):
        last = {}
        for e in COMPUTE:
            for i in range(len(self.ops[e]) - 1, -1, -1):
                if self.ops[e][i][0] == 'c':
                    last[e] = i
                    break
        for key, val in self.dma_last.items():
            last[key] = val
        for e in ENGS:
            deps = {s: i for s, i in last.items() if s != e}
            waits = self._need(e, deps)
            if waits:
                self.ops[e].append(('w', None, waits))

    def bank(self):
        b = self.pb
        self.pb = (b + 1) % 8
        return b

    def bank2(self):
        b = (self.pb + 1) // 2 * 2 % 8
        self.pb = (b + 2) % 8
        return b

    def emit(self):
        nc = self.nc
        val = {}
        for e in COMPUTE:
            cnt = 0
            v = {}
            for i, o in enumerate(self.ops[e]):
                if o[0] == 'c' and i in self.flag[e]:
                    cnt += 1
                    v[i] = cnt
            val[e] = v
        sems = {}
        with contextlib.ExitStack() as st:
            for e in COMPUTE:
                sems[e] = st.enter_context(nc.semaphore(f"s_{e}"))
            for q, K in DMA_RING.items():
                for k in range(K):
                    sems[('dma', q, k)] = st.enter_context(nc.semaphore(f"d_{q}{k}"))
            block = st.enter_context(nc.Block())

            def run(e):
                def body(eng):
                    fl = self.flag.get(e, ())
                    for i, o in enumerate(self.ops[e]):
                        for src, idx in o[2]:
                            v = val[src][idx] if src in COMPUTE else idx
                            eng.wait_ge(sems[src], v)
                        if o[0] == 'c':
                            ins = o[1](eng)
                            if i in fl:
                                ins.then_inc(sems[e], 1)
                        elif o[0] == 'd':
                            out, in_, kw = o[1]
                            eng.dma_start(out=out, in_=in_, **kw).then_inc(sems[o[3]], 16)
                return body
            block.tensor(run('pe'))
            block.scalar(run('act'))
            block.vector(run('dve'))
            block.gpsimd(run('pool'))
            block.sync(run('sp'))


def mm(P, out, lhsT, rhs, start, stop, r, w):
    P.op('pe', lambda e: e.matmul(out, lhsT=lhsT, rhs=rhs, start=start, stop=stop), r=r, w=w)


def trp(P, out, in_, ident, r, w):
    P.op('pe', lambda e: e.transpose(out, in_, ident), r=r, w=w)


def act(P, out, in_, func, r, w, bias=None, scale=None, accum=None):
    kw = {}
    if bias is not None:
        kw['bias'] = bias
    if scale is not None:
        kw['scale'] = scale
    if accum is not None:
        kw['accum_out'] = accum
    P.op('act', lambda e: e.activation(out, in_, func, **kw), r=r, w=w)


def tt(P, eng, out, in0, in1, op, r, w):
    P.op(eng, lambda e: e.tensor_tensor(out, in0, in1, op), r=r, w=w)


def ts(P, eng, out, in0, s1, s2, op0, op1, r, w):
    if s2 is None:
        P.op(eng, lambda e: e.tensor_scalar(out, in0, s1, None, op0), r=r, w=w)
    else:
        P.op(eng, lambda e: e.tensor_scalar(out, in0, s1, s2, op0, op1), r=r, w=w)


def stt(P, out, in0, scalar, in1, op0, op1, r, w):
    P.op('dve', lambda e: e.scalar_tensor_tensor(out, in0, scalar, in1, op0, op1), r=r, w=w)


def cpy(P, eng, out, in_, r, w):
    if eng == 'act':
        P.op('act', lambda e: e.copy(out, in_), r=r, w=w)
    else:
        P.op(eng, lambda e: e.tensor_copy(out, in_), r=r, w=w)


def mset(P, eng, ap, v, w):
    P.op(eng, lambda e: e.memset(ap, v), w=w)


def amul(P, out, in_, m_ap, r, w):
    P.op('act', lambda e: e.mul(out, in_, m_ap), r=r, w=w)


def recip(P, out, in_, r, w):
    P.op('dve', lambda e: e.reciprocal(out, in_), r=r, w=w)


def scan(P, out, d0, d1, init, r, w):
    P.op('dve', lambda e: e.tensor_tensor_scan(out, d0, d1, init, ALU.mult, ALU.add), r=r, w=w)


def make_consts():
    c = {}
    c['ident'] = np.eye(128, dtype=np.float32)
    c['identb'] = np.eye(128, dtype=np.float32).astype(ml_dtypes.bfloat16)
    c['ones'] = np.ones((128, 128), np.float32)
    k = np.arange(128)
    c['m_le'] = (k[:, None] <= k[None, :]).astype(np.float32)
    c['m_ge'] = (k[:, None] >= k[None, :]).astype(np.float32)
    c['m_gt'] = (k[:, None] > k[None, :]).astype(np.float32)
    c['m_lt'] = (k[:, None] < k[None, :]).astype(np.float32)
    bm = np.zeros((128, 512), np.float32)
    bm[:64, :256] = 1.0
    bm[64:, 256:] = 1.0
    c['blockmask'] = bm
    d = 128
    inv_freq = (1.0 / (10000.0 ** (np.arange(0, d, 2, dtype=np.float32) / np.float32(d)))).astype(np.float32)
    pos = np.arange(8192, dtype=np.float32)
    ang = (pos[:, None] * inv_freq[None, :]).astype(np.float32)
    cos = np.cos(ang.astype(np.float64)).T
    sin = np.sin(ang.astype(np.float64)).T
    cosf = np.concatenate([cos, cos], 0)
    sins = np.concatenate([-sin, sin], 0)
    sc = 128.0 ** -0.5
    c['cosq'] = cosf.astype(np.float32)
    c['sinq'] = sins.astype(np.float32)
    c['cosk'] = (cosf * sc).astype(np.float32)
    c['sink'] = (sins * sc).astype(np.float32)
    lg = np.log1p(-np.exp2(-5.0 - np.arange(4, dtype=np.float64)))
    pl = np.arange(128, dtype=np.float64)
    dm = np.exp(lg[None, :, None] * np.abs(pl[:, None, None] - pl[None, None, :]))
    c['dmask'] = dm.astype(np.float32)
    gf = np.exp(lg[:, None] * (pl + 1.0)[None, :])
    gb = np.exp(lg[:, None] * (128.0 - pl)[None, :])
    c['gf'] = np.broadcast_to(gf[None], (128, 4, 128)).astype(np.float32).copy()
    c['gb'] = np.broadcast_to(gb[None], (128, 4, 128)).astype(np.float32).copy()
    wf = np.exp(lg[None, :] * (127.0 - pl)[:, None])
    wb = np.exp(lg[None, :] * pl[:, None])
    c['wfb'] = np.concatenate([wf, wb], 1).astype(np.float32)
    c['cdec'] = np.broadcast_to(np.exp(lg * 128.0)[None, :], (128, 4)).astype(np.float32).copy()
    return c


CONST_SHAPES = None


def build(cfg, debug=False):
    TA, TB, DEPTH = cfg['TA'], cfg['TB'], cfg['DEPTH']
    NT = TA + TB
    TMAX = max(TA, TB)
    seqs = [(0, TA), (TA, TB)]
    nc = bass.Bass("TRN2", target_bir_lowering=False)
    P = Prog(nc)

    def din(name, shape, dt=F32):
        return nc.dram_tensor(name, list(shape), dt, kind="ExternalInput").ap()

    def dscr(name, shape, dt=F32):
        return nc.dram_tensor(name, list(shape), dt, kind="ExternalOutput" if debug else "Internal").ap()

    xa = din('xa', [TA, D])
    xb = din('xb', [TB, D])
    Wd = {}
    L = cfg.get('LW', 4)
    for name, shape in [('ffn1_norm', [L, D]), ('ffn1_w_gu', [L, D, 2 * DFF]), ('ffn1_w_down', [L, DFF, D]),
                        ('mix_norm', [L, D]), ('w_in', [L, D, DPROJ]), ('lru_conv_w', [L, 4, 512]),
                        ('lru_conv_b', [L, 512]), ('lru_w_a', [L, 2, 8, 64, 64]), ('lru_b_a', [L, 2, 512]),
                        ('lru_w_i', [L, 2, 8, 64, 64]), ('lru_b_i', [L, 2, 512]), ('lru_lam', [L, 2, 512]),
                        ('ssd_conv_w', [L, 4, 768]), ('ssd_conv_b', [L, 768]), ('ssd_dt_bias', [L, 16]),
                        ('ssd_a_log', [L, 16]), ('ssd_d', [L, 8]), ('ssd_norm', [L, 512]),
                        ('ret_norm', [L, 512]), ('w_out', [L, 1536, D]), ('ffn2_norm', [L, D]),
                        ('ffn2_w_gu', [L, D, 2 * DFF]), ('ffn2_w_down', [L, DFF, D]), ('final_norm', [1, D])]:
        Wd[name] = din(name, shape)
    consts = make_consts()
    Cd = {}
    for name, arr in consts.items():
        Cd[name] = din('c_' + name, arr.shape, BF16 if arr.dtype == ml_dtypes.bfloat16 else F32)
    ya = nc.dram_tensor('ya', [TA, D], F32, kind="ExternalOutput").ap()
    yb = nc.dram_tensor('yb', [TB, D], F32, kind="ExternalOutput").ap()

    X = dscr('X', [D, NT])
    LX = dscr('LX', [512, NT])
    LG = dscr('LG', [512, NT])
    XBC = dscr('XBC', [768, NT])
    RG = dscr('RG', [512, NT])
    Qd = dscr('Q', [512, NT], BF16)
    Kd = dscr('K', [512, NT], BF16)
    SZ = dscr('SZ', [NT, 512])
    Vd = dscr('V', [NT, 512], BF16)
    DT = dscr('DT', [NT, 16])
    Y = dscr('Y', [1536, NT], BF16)
    PF = dscr('PF', [TMAX // 128, 128, 512], BF16)

    ps = nc.alloc_psum_tensor("ps", [128, 4096], F32)
    pb = [Buf() for _ in range(8)]

    def pbank(b, n=512):
        return ps[:, b * 512:b * 512 + n]

    def pbankb(b):
        return ps[:, b * 512:(b + 1) * 512].bitcast(BF16)

    ident = P.sb([128, 128], F32, 'ident')
    identb = P.sb([128, 128], BF16, 'identb')
    ones = P.sb([128, 128], F32, 'ones')
    epst = P.sb([128, 1], F32, 'eps')
    cb_ = Buf()
    P.dma('sp', ident[:], Cd['ident'], w=[cb_])
    P.dma('sp', identb[:], Cd['identb'], w=[cb_])
    P.dma('sp', ones[:], Cd['ones'], w=[cb_])
    mset(P, 'dve', epst[:], EPS, [cb_])
    P.barrier()
    base_mark = P.mark()

    def rmsnorm_fm(xt, xbuf, nw, nwb, out, outb, TT, sq, sqb, rs, rsb, nfeat=D):
        b = P.bank()
        for kc in range(NKC):
            j = kc % len(sq)
            act(P, sq[j][:, :TT], xt[:, kc, :], AF.Square, r=[xbuf], w=[sqb[j]])
            mm(P, pbank(b, TT), ones[:], sq[j][:, :TT], kc == 0, kc == NKC - 1, r=[sqb[j], cb_], w=[pb[b]])
        act(P, rs[:, :TT], pbank(b, TT), AF.Sqrt, r=[pb[b], cb_], w=[rsb], bias=epst[:, 0:1], scale=1.0 / nfeat)
        recip(P, rs[:, :TT], rs[:, :TT], r=[rsb], w=[rsb])
        for kc in range(NKC):
            stt(P, out[:, kc, :], xt[:, kc, :], nw[:, kc:kc + 1], rs[:, :TT], ALU.mult, ALU.mult,
                r=[xbuf, rsb, nwb], w=[outb])

    def load_vec_pc(dst, src_1d, n, buf):
        P.dma('sp', dst, src_1d.rearrange("(c p) -> p c", p=128), w=[buf], allow_slow_non_contiguous=True)

    def s0_pass():
        m = P.mark()
        xin = [P.sb([128, 4, D], F32, 'xin') for _ in range(2)]
        xin_b = [Buf(), Buf()]
        xfm = [P.sb([128, NKC, 512], F32, 'xfm') for _ in range(2)]
        xfm_b = [Buf(), Buf()]
        i = 0
        for (src, (s0, T)) in zip((xa, xb), seqs):
            TT = 512 if T % 512 == 0 else 128
            nb = TT // 128
            for t0 in range(0, T, TT):
                xi, xib, xf, xfb = xin[i % 2], xin_b[i % 2], xfm[i % 2], xfm_b[i % 2]
                P.dma('sp', xi[:, 0:nb, :], src[t0:t0 + TT, :].rearrange("(b p) f -> p b f", p=128), w=[xib])
                for kc in range(NKC):
                    b = P.bank()
                    for bl in range(nb):
                        trp(P, pbank(b)[:, bl * 128:(bl + 1) * 128], xi[:, bl, kc * 128:(kc + 1) * 128], ident[:],
                            r=[xib, cb_], w=[pb[b]])
                    cpy(P, 'act' if kc % 2 else 'dve', xf[:, kc, 0:TT], pbank(b, TT), r=[pb[b]], w=[xfb])
                P.dma('pool', X[:, s0 + t0:s0 + t0 + TT].rearrange("(c p) t -> p c t", p=128), xf[:, :, 0:TT], r=[xfb])
                i += 1
        P.release(m)

    def ffn_pass(l, pre):
        m = P.mark()
        TT = cfg['TTF']
        wgu = P.sb([128, NKC, 2 * DFF], BF16, 'wgu')
        wd = P.sb([128, NFC, D], BF16, 'wd')
        wgub = [Buf() for _ in range(NKC)]
        wdb = [Buf() for _ in range(NFC)]
        nw = P.sb([128, NKC], F32, 'nw')
        nwb = Buf()
        load_vec_pc(nw[:], Wd[pre + '_norm'][l], NKC, nwb)
        for kc in range(NKC):
            P.dma('pool', wgu[:, kc, :], Wd[pre + '_w_gu'][l, kc * 128:(kc + 1) * 128, :], w=[wgub[kc]])
        for fc in range(NFC):
            P.dma('pool', wd[:, fc, :], Wd[pre + '_w_down'][l, fc * 128:(fc + 1) * 128, :], w=[wdb[fc]])
        xt = [P.sb([128, NKC, TT], F32, 'xt') for _ in range(2)]
        xtb = [Buf(), Buf()]
        xn2 = [P.sb([128, NKC, TT], BF16, 'xn') for _ in range(2)]
        xn2b = [Buf(), Buf()]
        h = P.sb([128, NFC, TT], BF16, 'h')
        hb = Buf()
        sq = [P.sb([128, TT], F32, 'sq') for _ in range(3)]
        sqb = [Buf() for _ in range(3)]
        rs = P.sb([128, TT], F32, 'rs')
        rsb = Buf()
        sg = [P.sb([128, TT], F32, 'sg') for _ in range(2)]
        sgb = [Buf(), Buf()]
        ntile = NT // TT

        def load(i):
            P.dma('sp', xt[i % 2][:], X[:, i * TT:(i + 1) * TT].rearrange("(c p) t -> p c t", p=128), w=[xtb[i % 2]])
        load(0)
        if ntile > 1:
            load(1)
        rmsnorm_fm(xt[0], xtb[0], nw, nwb, xn2[0], xn2b[0], TT, sq, sqb, rs, rsb)
        for i in range(ntile):
            x_, xb_ = xt[i % 2], xtb[i % 2]
            xn, xnb = xn2[i % 2], xn2b[i % 2]
            for mc in range(NFC):
                bg = P.bank()
                bu = P.bank()
                for kc in range(NKC):
                    mm(P, pbank(bg, TT), wgu[:, kc, mc * 128:(mc + 1) * 128], xn[:, kc, :], kc == 0, kc == NKC - 1,
                       r=[wgub[kc], xnb], w=[pb[bg]])
                for kc in range(NKC):
                    mm(P, pbank(bu, TT), wgu[:, kc, DFF + mc * 128:DFF + (mc + 1) * 128], xn[:, kc, :], kc == 0,
                       kc == NKC - 1, r=[wgub[kc], xnb], w=[pb[bu]])
                j = mc % 2
                act(P, sg[j][:], pbank(bg, TT), AF.Silu, r=[pb[bg]], w=[sgb[j]])
                tt(P, 'dve', h[:, mc, :], sg[j][:], pbank(bu, TT), ALU.mult, r=[sgb[j], pb[bu]], w=[hb])
            if i + 1 < ntile:
                rmsnorm_fm(xt[(i + 1) % 2], xtb[(i + 1) % 2], nw, nwb, xn2[(i + 1) % 2], xn2b[(i + 1) % 2], TT, sq, sqb, rs, rsb)
            for n in range(NKC):
                b = P.bank()
                for mc in range(NFC):
                    mm(P, pbank(b, TT), wd[:, mc, n * 128:(n + 1) * 128], h[:, mc, :], mc == 0, mc == NFC - 1,
                       r=[wdb[mc], hb], w=[pb[b]])
                stt(P, x_[:, n, :], pbank(b, TT), 0.5, x_[:, n, :], ALU.mult, ALU.add, r=[pb[b], xb_], w=[xb_])
            P.dma('pool', X[:, i * TT:(i + 1) * TT].rearrange("(c p) t -> p c t", p=128), x_[:], r=[xb_])
            if i + 2 < ntile:
                load(i + 2)
        P.release(m)

    def s2_pass(l):
        m = P.mark()
        TT = cfg['TT2']
        nbl = TT // 128
        win = P.sb([128, NKC, DPROJ], BF16, 'win')
        winb = [Buf() for _ in range(NKC)]
        wsw = P.sb([128, NKC, 1024], BF16, 'wsw')
        wswb = Buf()
        nw = P.sb([128, NKC], F32, 'nw')
        nwb = Buf()
        load_vec_pc(nw[:], Wd['mix_norm'][l], NKC, nwb)
        for kc in range(NKC):
            P.dma('pool', win[:, kc, :], Wd['w_in'][l, kc * 128:(kc + 1) * 128, :], w=[winb[kc]])
        for qk in range(2):
            base = 2320 + qk * 512
            for hh in range(4):
                for half in range(2):
                    c0 = base + hh * 128 + (1 - half) * 64
                    d0 = (qk * 4 + hh) * 128 + half * 64
                    P.dma('pool', wsw[:, :, d0:d0 + 64],
                          Wd['w_in'][l, :, c0:c0 + 64].rearrange("(c p) n -> p c n", p=128), w=[wswb])
        dtb = P.sb([128, 16], F32, 'dtb')
        dtbb = Buf()
        P.dma('sp', dtb[:], Wd['ssd_dt_bias'][l:l + 1, :].to_broadcast([128, 16]), w=[dtbb])
        xt = [P.sb([128, NKC, TT], F32, 'xt') for _ in range(2)]
        xtb = [Buf(), Buf()]
        rope = [P.sb([128, 4, TT], F32, 'rope') for _ in range(2)]
        ropeb = [Buf(), Buf()]
        xn2 = [P.sb([128, NKC, TT], BF16, 'xn') for _ in range(2)]
        xn2b = [Buf(), Buf()]
        xn, xnb = xn2[0], xn2b[0]
        sq = [P.sb([128, TT], F32, 'sq') for _ in range(3)]
        sqb = [Buf() for _ in range(3)]
        rs = P.sb([128, TT], F32, 'rs')
        rsb = Buf()

        def stage(shape, dt, name):
            return [P.sb(shape, dt, name) for _ in range(2)], [Buf(), Buf()]
        lxs, lxsb = stage([128, 4, TT], F32, 'lxs')
        lgs, lgsb = stage([128, 4, TT], F32, 'lgs')
        xbs, xbsb = stage([128, 6, TT], F32, 'xbs')
        rgs, rgsb = stage([128, 4, TT], F32, 'rgs')
        qs, qsb = stage([128, 4, TT], BF16, 'qs')
        ks, ksb = stage([128, 4, TT], BF16, 'ks')
        szs, szsb = stage([128, nbl, 512], F32, 'szs')
        vs, vsb = stage([128, nbl, 512], BF16, 'vs')
        dts, dtsb = stage([128, nbl, 16], F32, 'dts')
        t1 = [P.sb([128, TT], F32, 't1') for _ in range(2)]
        t1b = [Buf(), Buf()]
        t2 = [P.sb([128, TT], F32, 't2') for _ in range(2)]
        t2b = [Buf(), Buf()]
        sp1 = P.sb([128, nbl, 16], F32, 'sp1')
        sp2 = P.sb([128, nbl, 16], F32, 'sp2')
        sp3 = P.sb([128, nbl, 16], F32, 'sp3')
        spb = Buf()
        tiles = []
        for (s0, T) in seqs:
            for t0 in range(0, T, TT):
                tiles.append((s0, t0))

        def load(i):
            s0, t0 = tiles[i]
            P.dma('sp', xt[i % 2][:], X[:, s0 + t0:s0 + t0 + TT].rearrange("(c p) t -> p c t", p=128), w=[xtb[i % 2]])
            for k_, nm in enumerate(('cosq', 'sinq', 'cosk', 'sink')):
                P.dma('sp', rope[i % 2][:, k_, :], Cd[nm][:, t0:t0 + TT], w=[ropeb[i % 2]])

        def proj_fm(wt, wbufs, c0):
            b = P.bank()
            for kc in range(NKC):
                mm(P, pbank(b, TT), wt[:, kc, c0:c0 + 128], xn[:, kc, :], kc == 0, kc == NKC - 1,
                   r=[wbufs[kc] if isinstance(wbufs, list) else wbufs, xnb], w=[pb[b]])
            return b
        load(0)
        if len(tiles) > 1:
            load(1)
        rmsnorm_fm(xt[0], xtb[0], nw, nwb, xn2[0], xn2b[0], TT, sq, sqb, rs, rsb)
        for i in range(len(tiles)):
            s0, t0 = tiles[i]
            g0 = s0 + t0
            rp, rpb = rope[i % 2], ropeb[i % 2]
            xn, xnb = xn2[i % 2], xn2b[i % 2]
            j = i % 2
            for c in range(4):
                b = proj_fm(win, winb, c * 128)
                cpy(P, 'act' if c % 2 else 'dve', lxs[j][:, c, :], pbank(b, TT), r=[pb[b]], w=[lxsb[j]])
            P.dma('pool', LX[:, g0:g0 + TT].rearrange("(c p) t -> p c t", p=128), lxs[j][:], r=[lxsb[j]])
            for c in range(4):
                b = proj_fm(win, winb, 512 + c * 128)
                act(P, lgs[j][:, c, :], pbank(b, TT), AF.Gelu_apprx_tanh, r=[pb[b]], w=[lgsb[j]])
            P.dma('pool', LG[:, g0:g0 + TT].rearrange("(c p) t -> p c t", p=128), lgs[j][:], r=[lgsb[j]])
            for c in range(6):
                b = proj_fm(win, winb, 1536 + c * 128)
                cpy(P, 'act' if c % 2 else 'dve', xbs[j][:, c, :], pbank(b, TT), r=[pb[b]], w=[xbsb[j]])
            P.dma('pool', XBC[:, g0:g0 + TT].rearrange("(c p) t -> p c t", p=128), xbs[j][:], r=[xbsb[j]])
            for c in range(4):
                b = proj_fm(win, winb, 3856 + c * 128)
                act(P, rgs[j][:, c, :], pbank(b, TT), AF.Silu, r=[pb[b]], w=[rgsb[j]])
            P.dma('pool', RG[:, g0:g0 + TT].rearrange("(c p) t -> p c t", p=128), rgs[j][:], r=[rgsb[j]])
            for qk, (stg, stgb, dst) in enumerate(((qs, qsb, Qd), (ks, ksb, Kd))):
                for hh in range(4):
                    b1 = proj_fm(win, winb, 2320 + qk * 512 + hh * 128)
                    b2 = proj_fm(wsw, wswb, (qk * 4 + hh) * 128)
                    jj = hh % 2
                    tt(P, 'dve', t1[jj][:], pbank(b1, TT), rp[:, 2 * qk, :], ALU.mult, r=[pb[b1], rpb], w=[t1b[jj]])
                    tt(P, 'dve', t2[jj][:], pbank(b2, TT), rp[:, 2 * qk + 1, :], ALU.mult, r=[pb[b2], rpb], w=[t2b[jj]])
                    tt(P, 'pool', stg[j][:, hh, :], t1[jj][:], t2[jj][:], ALU.add, r=[t1b[jj], t2b[jj]], w=[stgb[j]])
                P.dma('pool', dst[:, g0:g0 + TT].rearrange("(c p) t -> p c t", p=128), stg[j][:], r=[stgb[j]])
            if i + 1 < len(tiles):
                rmsnorm_fm(xt[(i + 1) % 2], xtb[(i + 1) % 2], nw, nwb, xn2[(i + 1) % 2], xn2b[(i + 1) % 2], TT, sq, sqb, rs, rsb)
            if i + 2 < len(tiles):
                load(i + 2)
            bdt = P.bank()
            for bl in range(nbl):
                b = P.bank()
                for kc in range(NKC):
                    mm(P, pbank(b), xn[:, kc, bl * 128:(bl + 1) * 128], win[:, kc, 1024:1536], kc == 0, kc == NKC - 1,
                       r=[winb[kc], xnb], w=[pb[b]])
                act(P, szs[j][:, bl, :], pbank(b), AF.Silu, r=[pb[b]], w=[szsb[j]])
                b = P.bank()
                if b == bdt:
                    b = P.bank()
                for kc in range(NKC):
                    mm(P, pbank(b), xn[:, kc, bl * 128:(bl + 1) * 128], win[:, kc, 3344:3856], kc == 0, kc == NKC - 1,
                       r=[winb[kc], xnb], w=[pb[b]])
                cpy(P, 'dve', vs[j][:, bl, :], pbank(b), r=[pb[b]], w=[vsb[j]])
                for kc in range(NKC):
                    mm(P, pbank(bdt)[:, bl * 16:(bl + 1) * 16], xn[:, kc, bl * 128:(bl + 1) * 128], win[:, kc, 2304:2320],
                       kc == 0, kc == NKC - 1, r=[winb[kc], xnb], w=[pb[bdt]])
            P.dma('pool', SZ[g0:g0 + TT, :].rearrange("(b p) f -> p b f", p=128), szs[j][:], r=[szsb[j]])
            P.dma('pool', Vd[g0:g0 + TT, :].rearrange("(b p) f -> p b f", p=128), vs[j][:], r=[vsb[j]])
            tt(P, 'dve', sp1[:], pbank(bdt)[:, 0:nbl * 16].rearrange("p (b h) -> p b h", h=16),
               dtb[:].unsqueeze(1).to_broadcast([128, nbl, 16]), ALU.add, r=[pb[bdt], dtbb], w=[spb])
            act(P, sp2[:], sp1[:], AF.Abs, r=[spb], w=[spb])
            act(P, sp3[:], sp2[:], AF.Exp, r=[spb], w=[spb], scale=-1.0)
            act(P, sp2[:], sp3[:], AF.Ln, r=[spb], w=[spb], bias=1.0)
            stt(P, dts[j][:], sp1[:], 0.0, sp2[:], ALU.max, ALU.add, r=[spb], w=[dtsb[j]])
            P.dma('pool', DT[g0:g0 + TT, :].rearrange("(b p) h -> p b h", p=128), dts[j][:], r=[dtsb[j]])
        P.release(m)

    def lru_pass(l):
        m = P.mark()
        cw = P.sb([128, 4, 4], F32, 'cw')
        cbv = P.sb([128, 4], F32, 'cbv')
        bab = P.sb([128, 2, 4], F32, 'bab')
        bib = P.sb([128, 2, 4], F32, 'bib')
        lam = P.sb([128, 2, 4], F32, 'lam')
        c1 = P.sb([128, 2, 4], F32, 'c1')
        c2 = P.sb([128, 2, 4], F32, 'c2')
        tl1 = P.sb([128, 2, 4], F32, 'tl1')
        tl2 = P.sb([128, 2, 4], F32, 'tl2')
        kb = Buf()
        for k_ in range(4):
            load_vec_pc(cw[:, :, k_], Wd['lru_conv_w'][l, k_], 4, kb)
        load_vec_pc(cbv[:], Wd['lru_conv_b'][l], 4, kb)
        for nm, dst in (('lru_b_a', bab), ('lru_b_i', bib), ('lru_lam', lam)):
            for d_ in range(2):
                load_vec_pc(dst[:, d_, :], Wd[nm][l, d_], 4, kb)
        act(P, tl1[:], lam[:], AF.Abs, r=[kb], w=[kb])
        act(P, tl2[:], tl1[:], AF.Exp, r=[kb], w=[kb], scale=-1.0)
        act(P, tl1[:], tl2[:], AF.Ln, r=[kb], w=[kb], bias=1.0)
        ts(P, 'dve', tl2[:], lam[:], -1.0, 0.0, ALU.mult, ALU.max, r=[kb], w=[kb])
        tt(P, 'dve', tl1[:], tl1[:], tl2[:], ALU.add, r=[kb], w=[kb])
        ts(P, 'dve', c1[:], tl1[:], -8.0, None, ALU.mult, None, r=[kb], w=[kb])
        ts(P, 'dve', c2[:], tl1[:], -16.0, None, ALU.mult, None, r=[kb], w=[kb])
        WA = P.sb([128, 8, 128], BF16, 'WA')
        WI = P.sb([128, 8, 128], BF16, 'WI')
        wb_ = Buf()
        mset(P, 'pool', WA[:], 0.0, [wb_])
        mset(P, 'pool', WI[:], 0.0, [wb_])
        for nm, dst in (('lru_w_a', WA), ('lru_w_i', WI)):
            for d_ in range(2):
                for r_ in range(2):
                    src = Wd[nm][l, d_].rearrange("(c r) i j -> r i c j", r=2)[r_]
                    P.dma('pool', dst[r_ * 64:(r_ + 1) * 64, d_ * 4:(d_ + 1) * 4, r_ * 64:(r_ + 1) * 64], src, w=[wb_])
        hba = P.sb([128, 2, 4], F32, 'hba')
        hbi = P.sb([128, 2, 4], F32, 'hbi')
        hc1 = P.sb([128, 2, 4], F32, 'hc1')
        ts(P, 'dve', hba[:], bab[:], 0.5, None, ALU.mult, None, r=[kb], w=[kb])
        ts(P, 'dve', hbi[:], bib[:], 0.5, None, ALU.mult, None, r=[kb], w=[kb])
        ts(P, 'dve', hc1[:], c1[:], 0.5, None, ALU.mult, None, r=[kb], w=[kb])
        m2 = P.mark()
        bset = 0
        for (s0, T) in seqs:
            TL = 1024 if T % 1024 == 0 else (512 if T % 512 == 0 else 128)
            ntl = T // TL
            NBT = 2
            nbatch = (ntl + NBT - 1) // NBT
            H = P.sb([128, T], F32, 'H')
            XC = P.sb([128, T], F32, 'XC')
            XCb = P.sb([128, T], BF16, 'XCb')
            Hb, XCbuf, XCbb = Buf(), Buf(), Buf()
            xraw = [P.sb([128, TL + 3], F32, 'xraw') for _ in range(2)]
            xrawb = [Buf(), Buf()]

            def ring(n, shape, dt, name):
                return [P.sb(shape, dt, name) for _ in range(n)], [Buf() for _ in range(n)]
            A_ = [P.sb([128, NBT * TL], F32, 'A') for _ in range(2)]
            S_ = [P.sb([128, NBT * TL], F32, 'S') for _ in range(2)]
            TI = [P.sb([128, NBT * TL], F32, 'TI') for _ in range(2)]
            A_b = [[Buf() for _ in range(NBT)] for _ in range(2)]
            S_b = [[Buf() for _ in range(NBT)] for _ in range(2)]
            TI_b = [[Buf() for _ in range(NBT)] for _ in range(2)]
            tha, thab = ring(2, [128, TL], F32, 'tha')
            hbk, hbkb = ring(2, [128, TL], F32, 'hbk')
            gt, gtb = ring(2, [128, TL], F32, 'gt')
            hs, hsb = ring(2, [128, TL], F32, 'hs')
            yst, ystb = ring(2, [128, TL], BF16, 'yst')
            cnt = 0
            gcnt = 0
            for cc in range(4):
                def gates(d_, t0, st, si, gk):
                    nb = max(1, TL // 512)
                    w_ = min(TL, 512)
                    if TL >= 1024:
                        b_a = P.bank2()
                        b_i = P.bank2()
                    else:
                        b_a = P.bank()
                        b_i = P.bank()
                    for bl in range(nb):
                        mm(P, pbank(b_a + bl, w_), WA[:, d_ * 4 + cc, :], XCb[:, t0 + bl * w_:t0 + (bl + 1) * w_], True, True,
                           r=[wb_, XCbb], w=[pb[b_a + bl]])
                        mm(P, pbank(b_i + bl, w_), WI[:, d_ * 4 + cc, :], XCb[:, t0 + bl * w_:t0 + (bl + 1) * w_], True, True,
                           r=[wb_, XCbb], w=[pb[b_i + bl]])
                    j = gk % 2
                    sl = slice(si * TL, (si + 1) * TL)
                    pa = ps[:, b_a * 512:b_a * 512 + TL]
                    pi = ps[:, b_i * 512:b_i * 512 + TL]
                    pra = [pb[b_a + x] for x in range(nb)]
                    pri = [pb[b_i + x] for x in range(nb)]
                    act(P, tha[j][:], pa, AF.Tanh, r=pra + [kb], w=[thab[j]], bias=hba[:, d_, cc:cc + 1], scale=0.5)
                    act(P, TI[st][:, sl], pi, AF.Tanh, r=pri + [kb], w=[TI_b[st][si]], bias=hbi[:, d_, cc:cc + 1], scale=0.5)
                    act(P, A_[st][:, sl], tha[j][:], AF.Exp, r=[thab[j], kb], w=[A_b[st][si]],
                        scale=hc1[:, d_, cc:cc + 1], bias=hc1[:, d_, cc:cc + 1])
                    act(P, S_[st][:, sl], tha[j][:], AF.Exp, r=[thab[j], kb], w=[S_b[st][si]],
                        scale=c1[:, d_, cc:cc + 1], bias=c1[:, d_, cc:cc + 1])

                def finish_batch(st, n):
                    act(P, S_[st][:, 0:n * TL], S_[st][:, 0:n * TL], AF.Sqrt, r=S_b[st][0:n], w=S_b[st][0:n], scale=-1.0, bias=1.0)

                def make_u(st, si, t0):
                    sl = slice(si * TL, (si + 1) * TL)
                    stt(P, TI[st][:, sl], TI[st][:, sl], 1.0, S_[st][:, sl], ALU.add, ALU.mult, r=[TI_b[st][si], S_b[st][si]], w=[TI_b[st][si]])
                    stt(P, TI[st][:, sl], TI[st][:, sl], 0.5, XC[:, t0:t0 + TL], ALU.mult, ALU.mult, r=[TI_b[st][si], XCbuf], w=[TI_b[st][si]])
                for bi in range(nbatch):
                    tiles = [k for k in range(bi * NBT, min(ntl, (bi + 1) * NBT))]
                    st = bset % 2
                    bset += 1
                    for si, k in enumerate(tiles):
                        t0 = k * TL
                        xr, xrb = xraw[cnt % 2], xrawb[cnt % 2]
                        cnt += 1
                        lo = 2 if k == 0 else 0
                        hi = 1 if k == ntl - 1 else 0
                        if lo:
                            mset(P, 'pool', xr[:, 0:2], 0.0, [xrb])
                        if hi:
                            mset(P, 'pool', xr[:, TL + 2:TL + 3], 0.0, [xrb])
                        P.dma('sp', xr[:, lo:TL + 3 - hi],
                              LX[cc * 128:(cc + 1) * 128, s0 + t0 - 2 + lo:s0 + t0 + TL + 1 - hi], w=[xrb])
                        xc = XC[:, t0:t0 + TL]
                        ts(P, 'dve', xc, xr[:, 0:TL], cw[:, cc, 0:1], cbv[:, cc:cc + 1], ALU.mult, ALU.add, r=[xrb, kb], w=[XCbuf])
                        for tap in range(1, 4):
                            stt(P, xc, xr[:, tap:tap + TL], cw[:, cc, tap:tap + 1], xc, ALU.mult, ALU.add, r=[xrb, kb, XCbuf], w=[XCbuf])
                        cpy(P, 'act', XCb[:, t0:t0 + TL], xc, r=[XCbuf], w=[XCbb])
                        gates(0, t0, st, si, gcnt)
                        gcnt += 1
                    finish_batch(st, len(tiles))
                    for si, k in enumerate(tiles):
                        t0 = k * TL
                        make_u(st, si, t0)
                        init = H[:, t0 - 1:t0] if k > 0 else 0.0
                        sl = slice(si * TL, (si + 1) * TL)
                        scan(P, H[:, t0:t0 + TL], A_[st][:, sl], TI[st][:, sl], init, r=[A_b[st][si], TI_b[st][si], Hb], w=[Hb])
                kk = 0
                for bi in range(nbatch - 1, -1, -1):
                    tiles = [k for k in range(min(ntl, (bi + 1) * NBT) - 1, bi * NBT - 1, -1)]
                    st = bset % 2
                    bset += 1
                    for si, k in enumerate(tiles):
                        gates(1, k * TL, st, si, gcnt)
                        gcnt += 1
                    finish_batch(st, len(tiles))
                    for si, k in enumerate(tiles):
                        t0 = k * TL
                        j2 = kk % 2
                        P.dma('sp', gt[j2][:], LG[cc * 128:(cc + 1) * 128, s0 + t0:s0 + t0 + TL], w=[gtb[j2]])
                        make_u(st, si, t0)
                        sl = slice(si * TL, (si + 1) * TL)
                        init = hbk[1 - j2][:, 0:1] if kk > 0 else 0.0
                        rr = [A_b[st][si], TI_b[st][si]] + ([hbkb[1 - j2]] if kk > 0 else [])
                        scan(P, hbk[j2][:, ::-1], A_[st][:, sl][:, ::-1], TI[st][:, sl][:, ::-1], init, r=rr, w=[hbkb[j2]])
                        tt(P, 'dve', hs[j2][:], hbk[j2][:], H[:, t0:t0 + TL], ALU.add, r=[hbkb[j2], Hb], w=[hsb[j2]])
                        tt(P, 'pool', yst[j2][:], hs[j2][:], gt[j2][:], ALU.mult, r=[hsb[j2], gtb[j2]], w=[ystb[j2]])
                        P.dma('pool', Y[cc * 128:(cc + 1) * 128, s0 + t0:s0 + t0 + TL], yst[j2][:], r=[ystb[j2]])
                        kk += 1
            P.release(m2)
        P.release(m)

    def ssd_pass(l):
        m = P.mark()
        scw = P.sb([128, 6, 4], F32, 'scw')
        scb = P.sb([128, 6], F32, 'scb')
        A16 = P.sb([128, 16], F32, 'A16')
        dsk = P.sb([128, 8], F32, 'dsk')
        snw = P.sb([128, 4], F32, 'snw')
        m_le = P.sb([128, 128], F32, 'm_le')
        m_ge = P.sb([128, 128], F32, 'm_ge')
        m_gt = P.sb([128, 128], F32, 'm_gt')
        m_lt = P.sb([128, 128], F32, 'm_lt')
        bmask = P.sb([128, 512], F32, 'bmask')
        kb = Buf()
        for k_ in range(4):
            load_vec_pc(scw[:, :, k_], Wd['ssd_conv_w'][l, k_], 6, kb)
        load_vec_pc(scb[:], Wd['ssd_conv_b'][l], 6, kb)
        load_vec_pc(snw[:], Wd['ssd_norm'][l], 4, kb)
        P.dma('sp', A16[:], Wd['ssd_a_log'][l:l + 1, :].to_broadcast([128, 16]), w=[kb])
        P.dma('sp', dsk[:], Wd['ssd_d'][l:l + 1, :].to_broadcast([128, 8]), w=[kb])
        for nm, dst in (('m_le', m_le), ('m_ge', m_ge), ('m_gt', m_gt), ('m_lt', m_lt), ('blockmask', bmask)):
            P.dma('sp', dst[:], Cd[nm], w=[kb])
        act(P, A16[:], A16[:], AF.Exp, r=[kb], w=[kb])
        ts(P, 'dve', A16[:], A16[:], -1.0, None, ALU.mult, None, r=[kb], w=[kb])
        m2 = P.mark()
        for (s0, T) in seqs:
            NCH = T // 128
            XS = P.sb([128, NCH, 512], BF16, 'XS')
            BT = P.sb([128, NCH, 128], BF16, 'BT')
            BCf = P.sb([128, 2, T], BF16, 'BCf')
            XSb, BTb, BCb = Buf(), Buf(), Buf()
            DTt = P.sb([128, NCH, 16], F32, 'DTt')
            dtA = P.sb([128, NCH, 16], F32, 'dtA')
            EAC = P.sb([128, NCH, 16], F32, 'EAC')
            DS = P.sb([128, NCH, 16], F32, 'DS')
            CDE = P.sb([128, NCH, 16], F32, 'CDE')
            sb_ = Buf()
            mt = P.mark()
            ACUM = P.sb([128, NCH, 16], F32, 'ACUM')
            TOT = P.sb([128, NCH, 16], F32, 'TOT')
            for c0 in range(0, NCH, 16):
                c1_ = min(NCH, c0 + 16)
                P.dma('sp', DTt[:, c0:c1_, :], DT[s0 + c0 * 128:s0 + c1_ * 128, :].rearrange("(c p) h -> p c h", p=128), w=[sb_])
            tt(P, 'dve', dtA[:], DTt[:], A16[:].unsqueeze(1).to_broadcast([128, NCH, 16]), ALU.mult, r=[sb_, kb], w=[sb_])
            ncol = NCH * 16
            dflat = dtA[:].rearrange("p c h -> p (c h)")
            for c0 in range(0, ncol, 512):
                w_ = min(512, ncol - c0)
                ch0, nch_ = c0 // 16, w_ // 16
                b1, b2, b3 = P.bank(), P.bank(), P.bank()
                mm(P, pbank(b1, w_), m_le[:], dflat[:, c0:c0 + w_], True, True, r=[kb, sb_], w=[pb[b1]])
                mm(P, pbank(b2, w_), m_ge[:], dflat[:, c0:c0 + w_], True, True, r=[kb, sb_], w=[pb[b2]])
                mm(P, pbank(b3, w_), ones[:], dflat[:, c0:c0 + w_], True, True, r=[cb_, sb_], w=[pb[b3]])
                cpy(P, 'dve', ACUM[:, ch0:ch0 + nch_, 0:8], pbank(b1, w_).rearrange("p (c h) -> p c h", h=16)[:, :, 0:8],
                    r=[pb[b1]], w=[sb_])
                cpy(P, 'dve', ACUM[:, ch0:ch0 + nch_, 8:16], pbank(b2, w_).rearrange("p (c h) -> p c h", h=16)[:, :, 8:16],
                    r=[pb[b2]], w=[sb_])
                cpy(P, 'act', TOT[:, ch0:ch0 + nch_, :], pbank(b3, w_).rearrange("p (c h) -> p c h", h=16), r=[pb[b3]], w=[sb_])
            act(P, EAC[:], ACUM[:], AF.Exp, r=[sb_], w=[sb_])
            act(P, CDE[:], TOT[:], AF.Exp, r=[sb_], w=[sb_])
            tt(P, 'dve', DS[:], TOT[:], ACUM[:], ALU.subtract, r=[sb_], w=[sb_])
            act(P, DS[:], DS[:], AF.Exp, r=[sb_], w=[sb_])
            tt(P, 'dve', DS[:], DS[:], DTt[:], ALU.mult, r=[sb_], w=[sb_])
            P.release(mt)
            if cfg.get('ssd_stop', 9) <= 1:
                P.release(m2)
                continue
            m3 = P.mark()
            TL = 512 if T % 512 == 0 else 128
            ntl = T // TL
            xr = [P.sb([128, 6, TL + 3], F32, 'xr') for _ in range(2)]
            xrb = [Buf(), Buf()]
            cv = P.sb([128, 6, TL], F32, 'cv')
            cvb = Buf()
            xsf = P.sb([128, 4, TL], BF16, 'xsf')
            xsfb = Buf()
            for k in range(ntl):
                t0 = k * TL
                x_, xb_ = xr[k % 2], xrb[k % 2]
                lo = 2 if k == 0 else 0
                hi = 1 if k == ntl - 1 else 0
                if lo:
                    mset(P, 'pool', x_[:, :, 0:2], 0.0, [xb_])
                if hi:
                    mset(P, 'pool', x_[:, :, TL + 2:TL + 3], 0.0, [xb_])
                P.dma('sp', x_[:, :, lo:TL + 3 - hi],
                      XBC[:, s0 + t0 - 2 + lo:s0 + t0 + TL + 1 - hi].rearrange("(c p) t -> p c t", p=128), w=[xb_])
                for c in range(6):
                    ts(P, 'dve', cv[:, c, :], x_[:, c, 0:TL], scw[:, c, 0:1], scb[:, c:c + 1], ALU.mult, ALU.add, r=[xb_, kb], w=[cvb])
                    for tap in range(1, 4):
                        stt(P, cv[:, c, :], x_[:, c, tap:tap + TL], scw[:, c, tap:tap + 1], cv[:, c, :], ALU.mult, ALU.add,
                            r=[xb_, kb, cvb], w=[cvb])
                if cfg.get('prep_stop', 9) <= 1:
                    continue
                act(P, xsf[:], cv[:, 0:4, :], AF.Silu, r=[cvb], w=[xsfb])
                act(P, BCf[:, :, t0:t0 + TL], cv[:, 4:6, :], AF.Silu, r=[cvb], w=[BCb])
                if cfg.get('prep_stop', 9) <= 2:
                    continue
                for bl in range(TL // 128):
                    c = t0 // 128 + bl
                    b = P.bank()
                    pv = pbankb(b)
                    for cc in range(4):
                        trp(P, pv[:, cc * 128:(cc + 1) * 128], xsf[:, cc, bl * 128:(bl + 1) * 128], identb[:], r=[xsfb, cb_], w=[pb[b]])
                    if cfg.get('prep_stop', 9) >= 4:
                        trp(P, pv[:, 512:640], BCf[:, 0, t0 + bl * 128:t0 + (bl + 1) * 128], identb[:], r=[BCb, cb_], w=[pb[b]])
                    if cfg.get('prep_stop', 9) <= 4:
                        continue
                    cpv = cfg.get('cpv', 0)
                    if cpv == 0:
                        cpy(P, 'act', XS[:, c, :], pv[:, 0:512], r=[pb[b]], w=[XSb])
                        cpy(P, 'act', BT[:, c, :], pv[:, 512:640], r=[pb[b]], w=[BTb])
                    elif cpv == 1:
                        cpy(P, 'act', XS[:, c, :], pv[:, 0:512], r=[pb[b]], w=[XSb])
                    elif cpv == 2:
                        cpy(P, 'act', BT[:, c, :], pv[:, 512:640], r=[pb[b]], w=[BTb])
                    elif cpv == 3:
                        cpy(P, 'dve', XS[:, c, :].bitcast(F32), pbank(b)[:, 0:256], r=[pb[b]], w=[XSb])
            P.release(m3)
            if cfg.get('ssd_stop', 9) <= 2:
                P.release(m2)
                continue

            def ring(n, shape, dt, name):
                return [P.sb(shape, dt, name) for _ in range(n)], [Buf() for _ in range(n)]
            prev = [P.sb([128, 512], F32, 'prev') for _ in range(2)]
            prevb_ = [Buf(), Buf()]
            tmp, tmpb = ring(2, [128, 512], F32, 'tmp')
            pvb, pvbb = ring(3, [128, 512], BF16, 'pvb')
            xds, xdsb = ring(2, [128, 512], BF16, 'xds')
            pfb = [Buf() for _ in range(NCH)]

            def v8(ap):
                return ap.rearrange("p (h x) -> p h x", h=8)

            def bc8(ap8, n=64):
                return ap8.unsqueeze(2).to_broadcast([128, 8, n])
            mset(P, 'dve', prev[0][:], 0.0, [prevb_[0]])
            mset(P, 'pool', pvb[0][:], 0.0, [pvbb[0]])
            def fwA(c):
                jx = c % 2
                tt(P, 'pool', v8(xds[jx][:]), v8(XS[:, c, :]), bc8(DS[:, c, 0:8]), ALU.mult, r=[XSb, sb_], w=[xdsb[jx]])
                mm(P, pbank(jx), BT[:, c, :], xds[jx][:], True, True, r=[BTb, xdsb[jx]], w=[pb[jx]])

            def fwB(c):
                b = c % 2
                tt(P, 'dve', v8(tmp[0][:]), v8(prev[0][:]), bc8(CDE[:, c, 0:8]), ALU.mult, r=[prevb_[0], sb_], w=[tmpb[0]])
                tt(P, 'dve', prev[0][:], tmp[0][:], pbank(b), ALU.add, r=[tmpb[0], pb[b]], w=[prevb_[0]])
                jn = (c + 1) % 3
                tt(P, 'dve', pvb[jn][:], prev[0][:], bmask[:], ALU.mult, r=[prevb_[0], kb], w=[pvbb[jn]])
                P.dma('sp', PF[c + 1], pvb[jn][:], r=[pvbb[jn]], w=[pfb[c + 1]])
            P.dma('sp', PF[0], pvb[0][:], r=[pvbb[0]], w=[pfb[0]])
            for step in range(NCH):
                if 0 <= step - 1 < NCH - 1:
                    fwB(step - 1)
                if step < NCH - 1:
                    fwA(step)
            if cfg.get('ssd_stop', 9) <= 3:
                P.release(m2)
                continue
            rhs, rhsb = ring(2, [128, 1024], F32, 'rhs')
            E, Eb = ring(2, [128, 1024], F32, 'E')
            CBm, CBmb = ring(2, [128, 256], F32, 'CBm')
            Wt = [[P.sb([128, 1024], BF16, 'Wt') for _ in range(2)] for _ in range(2)]
            Wtb = [[Buf(), Buf()], [Buf(), Buf()]]
            xdt = [[P.sb([128, 512], BF16, 'xdt') for _ in range(2)] for _ in range(2)]
            xdtb = [[Buf(), Buf()], [Buf(), Buf()]]
            xd, xdb = ring(2, [128, 512], BF16, 'xd')
            xdo, xdob = xds, xdsb
            pfl, pflb = ring(2, [128, 512], BF16, 'pfl')
            szt, sztb = ring(3, [128, 512], F32, 'szt')
            y1 = P.sb([128, 512], F32, 'y1')
            y2 = P.sb([128, 512], F32, 'y2')
            y1b, y2b = Buf(), Buf()
            y3, y3b = ring(2, [128, 512], F32, 'y3')
            yn, ynb = ring(2, [128, 512], F32, 'yn')
            ssq, ssqb = ring(2, [128, 2], F32, 'ssq')
            yst, ystb = ring(2, [128, 4, 128], BF16, 'yst')
            pq, pqb = ring(2, [128, 512], BF16, 'pq')
            cml = m_le
            cmg = m_ge
            mset(P, 'dve', prev[1][:], 0.0, [prevb_[1]])
            mset(P, 'pool', pq[0][:], 0.0, [pqb[0]])
            BK_SF, BK_SB, BK_S, BK_YD, BK_OF, BK_OB = 0, 2, 4, 5, 6, 7

            def stA(kk):
                c = NCH - 1 - kk
                p2, p3 = kk % 2, kk % 3
                tok = slice(c * 128, (c + 1) * 128)
                P.dma('sp', pfl[p2][:], PF[c], r=[pfb[c]], w=[pflb[p2]])
                P.dma('sp', szt[p3][:], SZ[s0 + c * 128:s0 + (c + 1) * 128, :], w=[sztb[p3]])
                for d_, (mrhs, mlhs, bk) in enumerate(((m_le, m_gt, BK_SF), (m_ge, m_lt, BK_SB))):
                    tt(P, 'pool', rhs[d_][:].rearrange("p (h i) -> p h i", h=8), mrhs[:].unsqueeze(1).to_broadcast([128, 8, 128]),
                       dtA[:, c, d_ * 8:(d_ + 1) * 8].unsqueeze(2).to_broadcast([128, 8, 128]), ALU.mult, r=[kb, sb_], w=[rhsb[d_]])
                    mm(P, pbank(bk), mlhs[:], rhs[d_][:, 0:512], True, True, r=[kb, rhsb[d_]], w=[pb[bk]])
                    mm(P, pbank(bk + 1), mlhs[:], rhs[d_][:, 512:1024], True, True, r=[kb, rhsb[d_]], w=[pb[bk + 1]])
                    act(P, E[d_][:], ps[:, bk * 512:bk * 512 + 1024], AF.Exp, r=[pb[bk], pb[bk + 1]], w=[Eb[d_]])
                for g in range(2):
                    mm(P, pbank(BK_SF + g, 128), BCf[g * 64:(g + 1) * 64, 0, tok], BCf[g * 64:(g + 1) * 64, 1, tok],
                       True, True, r=[BCb], w=[pb[BK_SF + g]])
                for d_, cmk in enumerate((cml, cmg)):
                    for g in range(2):
                        tt(P, 'dve', CBm[d_][:, g * 128:(g + 1) * 128], pbank(BK_SF + g, 128), cmk[:], ALU.mult,
                           r=[pb[BK_SF + g], kb], w=[CBmb[d_]])
                for d_ in range(2):
                    tt(P, 'dve', Wt[d_][p2][:].rearrange("p (g k i) -> p g k i", g=2, k=4),
                       E[d_][:].rearrange("p (g k i) -> p g k i", g=2, k=4),
                       CBm[d_][:].rearrange("p (g i) -> p g i", g=2).unsqueeze(2).to_broadcast([128, 2, 4, 128]), ALU.mult,
                       r=[Eb[d_], CBmb[d_]], w=[Wtb[d_][p2]])
                    tt(P, 'pool', v8(xdt[d_][p2][:]), v8(XS[:, c, :]), bc8(DTt[:, c, d_ * 8:(d_ + 1) * 8]), ALU.mult,
                       r=[XSb, sb_], w=[xdtb[d_][p2]])
                tt(P, 'pool', v8(xd[p2][:]), v8(XS[:, c, :]), bc8(dsk[:, 0:8]), ALU.mult, r=[XSb, kb], w=[xdb[p2]])
                if c > 0:
                    tt(P, 'pool', v8(xdo[p2][:]), v8(XS[:, c, :]), bc8(DS[:, c, 8:16]), ALU.mult, r=[XSb, sb_], w=[xdob[p2]])

            def stB(kk):
                c = NCH - 1 - kk
                p2 = kk % 2
                tok = slice(c * 128, (c + 1) * 128)
                mm(P, pbank(BK_YD), identb[:], xd[p2][:], True, False, r=[cb_, xdb[p2]], w=[pb[BK_YD]])
                for hh in range(8):
                    for d_ in range(2):
                        mm(P, pbank(BK_YD)[:, hh * 64:(hh + 1) * 64], Wt[d_][p2][:, hh * 128:(hh + 1) * 128],
                           xdt[d_][p2][:, hh * 64:(hh + 1) * 64], False, (hh == 7 and d_ == 1),
                           r=[Wtb[d_][p2], xdtb[d_][p2]], w=[pb[BK_YD]])
                mm(P, pbank(BK_OF), BCf[:, 1, tok], pfl[p2][:], True, True, r=[BCb, pflb[p2]], w=[pb[BK_OF]])
                mm(P, pbank(BK_OB), BCf[:, 1, tok], pq[p2][:], True, True, r=[BCb, pqb[p2]], w=[pb[BK_OB]])
                if c > 0:
                    mm(P, pbank(BK_S), BT[:, c, :], xdo[p2][:], True, True, r=[BTb, xdob[p2]], w=[pb[BK_S]])
                tt(P, 'dve', v8(y1[:]), v8(pbank(BK_OF)), bc8(EAC[:, c, 0:8]), ALU.mult, r=[pb[BK_OF], sb_], w=[y1b])
                tt(P, 'dve', v8(y2[:]), v8(pbank(BK_OB)), bc8(EAC[:, c, 8:16]), ALU.mult, r=[pb[BK_OB], sb_], w=[y2b])
                tt(P, 'dve', y3[p2][:], y1[:], pbank(BK_YD), ALU.add, r=[y1b, pb[BK_YD]], w=[y3b[p2]])
                tt(P, 'dve', y3[p2][:], y3[p2][:], y2[:], ALU.add, r=[y3b[p2], y2b], w=[y3b[p2]])
                if c > 0:
                    tt(P, 'dve', v8(tmp[1][:]), v8(prev[1][:]), bc8(CDE[:, c, 8:16]), ALU.mult, r=[prevb_[1], sb_], w=[tmpb[1]])
                    tt(P, 'dve', prev[1][:], tmp[1][:], pbank(BK_S), ALU.add, r=[tmpb[1], pb[BK_S]], w=[prevb_[1]])
                    tt(P, 'dve', pq[1 - p2][:], prev[1][:], bmask[:], ALU.mult, r=[prevb_[1], kb], w=[pqb[1 - p2]])

            def stC(kk):
                c = NCH - 1 - kk
                p2, p3 = kk % 2, kk % 3
                tt(P, 'dve', y3[p2][:], y3[p2][:], szt[p3][:], ALU.mult, r=[y3b[p2], sztb[p3]], w=[y3b[p2]])
                act(P, yn[p2][:], y3[p2][:], AF.Square, r=[y3b[p2]], w=[ynb[p2], ssqb[p2]], accum=ssq[p2][:, 0:1])
                act(P, ssq[p2][:, 1:2], ssq[p2][:, 0:1], AF.Ln, r=[ssqb[p2], cb_], w=[ssqb[p2]], bias=epst[:, 0:1], scale=1.0 / 512)
                act(P, ssq[p2][:, 1:2], ssq[p2][:, 1:2], AF.Exp, r=[ssqb[p2]], w=[ssqb[p2]], scale=-0.5)
                amul(P, yn[p2][:], y3[p2][:], ssq[p2][:, 1:2], r=[y3b[p2], ssqb[p2], ynb[p2]], w=[ynb[p2]])

            def stC2(kk):
                c = NCH - 1 - kk
                p2 = kk % 2
                for cc in range(4):
                    trp(P, pbank(BK_YD)[:, cc * 128:(cc + 1) * 128], yn[p2][:, cc * 128:(cc + 1) * 128], ident[:],
                        r=[ynb[p2], cb_], w=[pb[BK_YD]])
                for cc in range(4):
                    amul(P, yst[p2][:, cc, :], pbank(BK_YD)[:, cc * 128:(cc + 1) * 128], snw[:, cc:cc + 1],
                         r=[pb[BK_YD], kb], w=[ystb[p2]])
                P.dma('act', Y[512:1024, s0 + c * 128:s0 + (c + 1) * 128].rearrange("(c p) t -> p c t", p=128), yst[p2][:], r=[ystb[p2]])
            for step in range(NCH + 3):
                if 0 <= step - 1 < NCH:
                    stB(step - 1)
                if step < NCH:
                    stA(step)
                if 0 <= step - 2 < NCH:
                    stC(step - 2)
                if 0 <= step - 3 < NCH:
                    stC2(step - 3)
            P.release(m2)
        P.release(m)

    def ret_pass(l):
        m = P.mark()
        rnw = P.sb([128, 4], F32, 'rnw')
        dmask = P.sb([128, 4, 128], F32, 'dmask')
        gf = P.sb([128, 4, 128], F32, 'gf')
        gb = P.sb([128, 4, 128], F32, 'gb')
        wfb = P.sb([128, 8], F32, 'wfb')
        cdec = P.sb([128, 4], F32, 'cdec')
        o128 = P.sb([128, 128], F32, 'o128')
        kb = Buf()
        load_vec_pc(rnw[:], Wd['ret_norm'][l], 4, kb)
        for nm, dst in (('dmask', dmask), ('gf', gf), ('gb', gb), ('wfb', wfb), ('cdec', cdec)):
            P.dma('sp', dst[:], Cd[nm], w=[kb])
        mset(P, 'dve', o128[:], 1.0 / 128, [kb])
        m2 = P.mark()

        def ring(n, shape, dt, name):
            return [P.sb(shape, dt, name) for _ in range(n)], [Buf() for _ in range(n)]

        def v4(ap):
            return ap.rearrange("p (h x) -> p h x", h=4)

        def bc4(ap4):
            return ap4.unsqueeze(2).to_broadcast([128, 4, 128])
        for (s0, T) in seqs:
            NCH = T // 128
            RF = P.sb([128, NCH, 512], BF16, 'RF')
            RFb = Buf()
            kt, ktb = ring(3, [128, 4, 128], BF16, 'kt')
            vt, vtb = ring(3, [128, 512], BF16, 'vt')
            qt, qtb = ring(2, [128, 4, 128], BF16, 'qt')
            rgt, rgtb = ring(2, [128, 4, 128], F32, 'rgt')
            ktm, ktmb = ring(2, [128, 512], BF16, 'ktm')
            vw, vwb = ring(2, [128, 512], BF16, 'vw')
            r_ = [P.sb([128, 512], F32, 'r') for _ in range(2)]
            rb_ = [Buf(), Buf()]
            tmp, tmpb = ring(2, [128, 512], F32, 'tmp')
            cnt = 0

            def kv_step(c, d_, cnt, btr, bkv):
                j3 = cnt % 3
                j2 = cnt % 2
                P.dma('sp', kt[j3][:], Kd[:, s0 + c * 128:s0 + (c + 1) * 128].rearrange("(h p) t -> p h t", p=128), w=[ktb[j3]])
                P.dma('sp', vt[j3][:], Vd[s0 + c * 128:s0 + (c + 1) * 128, :], w=[vtb[j3]])
                b = btr
                pv = pbankb(b)
                for hh in range(4):
                    trp(P, pv[:, hh * 128:(hh + 1) * 128], kt[j3][:, hh, :], identb[:], r=[ktb[j3], cb_], w=[pb[b]])
                cpy(P, 'act', ktm[j2][:], pv[:, 0:512], r=[pb[b]], w=[ktmb[j2]])
                tt(P, 'pool', v4(vw[j2][:]), v4(vt[j3][:]), bc4(wfb[:, d_ * 4:(d_ + 1) * 4]), ALU.mult, r=[vtb[j3], kb], w=[vwb[j2]])
                b = bkv
                for hh in range(4):
                    mm(P, pbank(b)[:, hh * 128:(hh + 1) * 128], ktm[j2][:, hh * 128:(hh + 1) * 128], vw[j2][:, hh * 128:(hh + 1) * 128],
                       True, True, r=[ktmb[j2], vwb[j2]], w=[pb[b]])
                return b, j3

            def state_update(d_, b):
                tt(P, 'dve', v4(tmp[d_][:]), v4(r_[d_][:]), bc4(cdec[:, 0:4]), ALU.mult, r=[rb_[d_], kb], w=[tmpb[d_]])
                tt(P, 'dve', r_[d_][:], tmp[d_][:], pbank(b), ALU.add, r=[tmpb[d_], pb[b]], w=[rb_[d_]])
            mset(P, 'dve', r_[0][:], 0.0, [rb_[0]])
            cpy(P, 'act', RF[:, 0, :], r_[0][:], r=[rb_[0]], w=[RFb])
            for step in range(NCH):
                if 0 <= step - 1 < NCH - 1:
                    c = step - 1
                    state_update(0, 3 + c % 2)
                    cpy(P, 'act', RF[:, c + 1, :], r_[0][:], r=[rb_[0]], w=[RFb])
                if step < NCH - 1:
                    kv_step(step, 0, cnt, 2, 3 + step % 2)
                    cnt += 1
            Sm, Smb = ring(2, [128, 4, 128], BF16, 'Sm')
            qf, qfb = ring(2, [128, 4, 128], BF16, 'qf')
            qb, qbb = ring(2, [128, 4, 128], BF16, 'qb')
            rbb = P.sb([128, 512], BF16, 'rbb')
            rbbb = Buf()
            kt2, kt2b = ring(3, [128, 4, 128], BF16, 'kt2')
            vt2, vt2b = ring(3, [128, 512], BF16, 'vt2')
            rg4, rg4b = ring(5, [128, 4, 128], F32, 'rg4')
            qt3, qt3b = ring(3, [128, 4, 128], BF16, 'qt3')
            ktm1 = P.sb([128, 512], BF16, 'ktm1')
            ktm1b = Buf()
            vw1 = P.sb([128, 512], BF16, 'vw1')
            vw1b = Buf()
            ysb, ysbb = ring(3, [128, 512], F32, 'ysb')
            ysq, ysqb = ring(2, [128, 512], F32, 'ysq')
            msb, msbb = ring(2, [128, 512], F32, 'msb')
            var, varb = ring(2, [128, 512], F32, 'var')
            m2t = P.sb([128, 512], F32, 'm2t')
            m2b = Buf()
            dd = P.sb([128, 512], F32, 'dd')
            ddb = Buf()
            ost, ostb = ring(2, [128, 4, 128], BF16, 'ost')
            mset(P, 'dve', r_[1][:], 0.0, [rb_[1]])
            BK_TR, BK_KV, BK_ST, BK_Y, BK_M, BK_Q = 0, 1, 3, 4, 5, 6

            def rLoad(kk):
                c = NCH - 1 - kk
                q3, p5 = kk % 3, kk % 5
                tk = slice(s0 + c * 128, s0 + (c + 1) * 128)
                P.dma('sp', qt3[q3][:], Qd[:, tk].rearrange("(h p) t -> p h t", p=128), w=[qt3b[q3]])
                P.dma('sp', rg4[p5][:], RG[:, tk].rearrange("(h p) t -> p h t", p=128), w=[rg4b[p5]])
                P.dma('sp', kt2[q3][:], Kd[:, tk].rearrange("(h p) t -> p h t", p=128), w=[kt2b[q3]])
                P.dma('sp', vt2[q3][:], Vd[tk, :], w=[vt2b[q3]])

            def rA(kk):
                c = NCH - 1 - kk
                p2, q3 = kk % 2, kk % 3
                pv = pbankb(BK_TR)
                for hh in range(4):
                    trp(P, pv[:, hh * 128:(hh + 1) * 128], kt2[q3][:, hh, :], identb[:], r=[kt2b[q3], cb_], w=[pb[BK_TR]])
                cpy(P, 'act', ktm1[:], pv[:, 0:512], r=[pb[BK_TR]], w=[ktm1b])
                tt(P, 'pool', v4(vw1[:]), v4(vt2[q3][:]), bc4(wfb[:, 4:8]), ALU.mult, r=[vt2b[q3], kb], w=[vw1b])
                bkv = BK_KV + p2
                for hh in range(4):
                    hs = slice(hh * 128, (hh + 1) * 128)
                    mm(P, pbank(bkv)[:, hs], ktm1[:, hs], vw1[:, hs], True, True, r=[ktm1b, vw1b], w=[pb[bkv]])
                for hh in range(4):
                    mm(P, pbank(BK_ST)[:, hh * 128:(hh + 1) * 128], kt2[q3][:, hh, :], qt3[q3][:, hh, :], True, True,
                       r=[kt2b[q3], qt3b[q3]], w=[pb[BK_ST]])
                tt(P, 'dve', Sm[p2][:], pbank(BK_ST).rearrange("p (h i) -> p h i", h=4), dmask[:], ALU.mult, r=[pb[BK_ST], kb], w=[Smb[p2]])
                tt(P, 'pool', qf[p2][:], qt3[q3][:], gf[:], ALU.mult, r=[qt3b[q3], kb], w=[qfb[p2]])
                tt(P, 'pool', qb[p2][:], qt3[q3][:], gb[:], ALU.mult, r=[qt3b[q3], kb], w=[qbb[p2]])

            def rB1(kk):
                c = NCH - 1 - kk
                p2, p3 = kk % 2, kk % 3
                cpy(P, 'act', rbb[:], r_[1][:], r=[rb_[1]], w=[rbbb])
                for hh in range(4):
                    o_ = pbank(BK_Y)[:, hh * 128:(hh + 1) * 128]
                    hs = slice(hh * 128, (hh + 1) * 128)
                    mm(P, o_, vt2[kk % 3][:, hs], Sm[p2][:, hh, :], True, False, r=[vt2b[kk % 3], Smb[p2]], w=[pb[BK_Y]])
                    mm(P, o_, RF[:, c, hs], qf[p2][:, hh, :], False, False, r=[RFb, qfb[p2]], w=[pb[BK_Y]])
                    mm(P, o_, rbb[:, hs], qb[p2][:, hh, :], False, True, r=[rbbb, qbb[p2]], w=[pb[BK_Y]])
                if c > 0:
                    state_update(1, BK_KV + p2)
                cpy(P, 'act', ysb[p3][:], pbank(BK_Y), r=[pb[BK_Y]], w=[ysbb[p3]])
                act(P, ysq[p2][:], pbank(BK_Y), AF.Square, r=[pb[BK_Y]], w=[ysqb[p2]])

            def rB2a(kk):
                p2, p3 = kk % 2, kk % 3
                mm(P, pbank(BK_M), o128[:], ysb[p3][:], True, True, r=[kb, ysbb[p3]], w=[pb[BK_M]])
                mm(P, pbank(BK_Q), o128[:], ysq[p2][:], True, True, r=[kb, ysqb[p2]], w=[pb[BK_Q]])
                cpy(P, 'act', msb[p2][:], pbank(BK_M), r=[pb[BK_M]], w=[msbb[p2]])
                act(P, m2t[:], msb[p2][:], AF.Square, r=[msbb[p2]], w=[m2b])

            def rB2b(kk):
                p2 = kk % 2
                stt(P, var[p2][:], m2t[:], -1.0, pbank(BK_Q), ALU.mult, ALU.add, r=[m2b, pb[BK_Q]], w=[varb[p2]])

            def rC1(kk):
                p2, p3 = kk % 2, kk % 3
                act(P, var[p2][:], var[p2][:], AF.Ln, r=[varb[p2], cb_], w=[varb[p2]], bias=epst[:, 0:1], scale=1.0)
                act(P, var[p2][:], var[p2][:], AF.Exp, r=[varb[p2]], w=[varb[p2]], scale=-0.5)
                tt(P, 'dve', dd[:], ysb[p3][:], msb[p2][:], ALU.subtract, r=[ysbb[p3], msbb[p2]], w=[ddb])

            def rC2(kk):
                c = NCH - 1 - kk
                p2, p3, p4 = kk % 2, kk % 3, kk % 4
                tt(P, 'dve', dd[:], dd[:], var[p2][:], ALU.mult, r=[ddb, varb[p2]], w=[ddb])
                tt(P, 'dve', v4(dd[:]), v4(dd[:]), bc4(rnw[:, 0:4]), ALU.mult, r=[ddb, kb], w=[ddb])
                tt(P, 'dve', ost[p2][:], v4(dd[:]), rg4[kk % 5][:], ALU.mult, r=[ddb, rg4b[kk % 5]], w=[ostb[p2]])
                P.dma('sp', Y[1024:1536, s0 + c * 128:s0 + (c + 1) * 128].rearrange("(h p) t -> p h t", p=128), ost[p2][:], r=[ostb[p2]])
            rLoad(0)
            for step in range(NCH + 3):
                if step + 1 < NCH:
                    rLoad(step + 1)
                if 0 <= step - 3 < NCH:
                    rC1(step - 3)
                if 0 <= step - 1 < NCH:
                    rB1(step - 1)
                if step < NCH:
                    rA(step)
                if 0 <= step - 2 < NCH:
                    rB2a(step - 2)
                if 0 <= step - 3 < NCH:
                    rC2(step - 3)
                if 0 <= step - 2 < NCH:
                    rB2b(step - 2)
            P.release(m2)
        P.release(m)

    def s4a_pass(l):
        m = P.mark()
        TT = cfg['TT4']
        wo = P.sb([128, 12, D], BF16, 'wo')
        wob = [Buf() for _ in range(12)]
        for c in range(12):
            P.dma('pool', wo[:, c, :], Wd['w_out'][l, c * 128:(c + 1) * 128, :], w=[wob[c]])
        xt = [P.sb([128, NKC, TT], F32, 'xt') for _ in range(2)]
        xtb = [Buf(), Buf()]
        yt = [P.sb([128, 12, TT], BF16, 'yt') for _ in range(2)]
        ytb = [Buf(), Buf()]
        ntile = NT // TT

        def load(i):
            P.dma('sp', xt[i % 2][:], X[:, i * TT:(i + 1) * TT].rearrange("(c p) t -> p c t", p=128), w=[xtb[i % 2]])
            P.dma('sp', yt[i % 2][:], Y[:, i * TT:(i + 1) * TT].rearrange("(c p) t -> p c t", p=128), w=[ytb[i % 2]])
        load(0)
        for i in range(ntile):
            if i + 1 < ntile:
                load(i + 1)
            x_, xb_, y_, yb_ = xt[i % 2], xtb[i % 2], yt[i % 2], ytb[i % 2]
            for n in range(NKC):
                b = P.bank()
                for c in range(12):
                    mm(P, pbank(b, TT), wo[:, c, n * 128:(n + 1) * 128], y_[:, c, :], c == 0, c == 11, r=[wob[c], yb_], w=[pb[b]])
                tt(P, 'dve', x_[:, n, :], x_[:, n, :], pbank(b, TT), ALU.add, r=[xb_, pb[b]], w=[xb_])
            P.dma('pool', X[:, i * TT:(i + 1) * TT].rearrange("(c p) t -> p c t", p=128), x_[:], r=[xb_])
        P.release(m)

    def s5_pass():
        m = P.mark()
        nw = P.sb([128, NKC], F32, 'nw')
        nwb = Buf()
        load_vec_pc(nw[:], Wd['final_norm'][0], NKC, nwb)
        xt = [P.sb([128, NKC, 512], F32, 'xt') for _ in range(2)]
        xtb = [Buf(), Buf()]
        xo = P.sb([128, NKC, 512], F32, 'xo')
        xob = Buf()
        otm = [P.sb([128, 4, D], F32, 'otm') for _ in range(2)]
        otmb = [Buf(), Buf()]
        sq = [P.sb([128, 512], F32, 'sq') for _ in range(3)]
        sqb = [Buf() for _ in range(3)]
        rs = P.sb([128, 512], F32, 'rs')
        rsb = Buf()
        tiles = []
        for (dst, (s0, T)) in zip((ya, yb), seqs):
            TT = 512 if T % 512 == 0 else 128
            for t0 in range(0, T, TT):
                tiles.append((dst, s0, t0, TT))

        def load(i):
            dst, s0, t0, TT = tiles[i]
            P.dma('sp', xt[i % 2][:, :, 0:TT], X[:, s0 + t0:s0 + t0 + TT].rearrange("(c p) t -> p c t", p=128), w=[xtb[i % 2]])
        load(0)
        for i in range(len(tiles)):
            if i + 1 < len(tiles):
                load(i + 1)
            dst, s0, t0, TT = tiles[i]
            x_, xb_ = xt[i % 2], xtb[i % 2]
            rmsnorm_fm(x_[:, :, 0:TT], xb_, nw, nwb, xo[:, :, 0:TT], xob, TT, sq, sqb, rs, rsb)
            o_, ob_ = otm[i % 2], otmb[i % 2]
            for bl in range(TT // 128):
                for half in range(2):
                    b = P.bank()
                    for q in range(4):
                        kc = half * 4 + q
                        trp(P, pbank(b)[:, q * 128:(q + 1) * 128], xo[:, kc, bl * 128:(bl + 1) * 128], ident[:], r=[xob, cb_], w=[pb[b]])
                    cpy(P, 'act' if half else 'dve', o_[:, bl, half * 512:(half + 1) * 512], pbank(b), r=[pb[b]], w=[ob_])
            P.dma('pool', dst[t0:t0 + TT, :].rearrange("(b p) f -> p b f", p=128), o_[:, 0:TT // 128, :], r=[ob_])
        P.release(m)

    stages = cfg.get('stages', None)

    def on(s):
        return stages is None or s in stages
    if on('s0'):
        s0_pass()
    for l in range(DEPTH):
        if on('s1'):
            ffn_pass(l, 'ffn1')
        if on('s2'):
            s2_pass(l)
        if on('lru'):
            lru_pass(l)
        if on('ssd'):
            ssd_pass(l)
        if on('ret'):
            ret_pass(l)
        if on('s4a'):
            s4a_pass(l)
        if on('s4b'):
            ffn_pass(l, 'ffn2')
    if on('s5'):
        s5_pass()
    P.barrier()
    P.emit()
    return nc, consts, P


CFG_FULL = dict(TA=8192, TB=4096, DEPTH=4, TTF=384, TT2=256, TT4=512, LW=4)
_CACHE = {}


def kernel(**inputs):
    cfg = CFG_FULL
    if 'nc' not in _CACHE:
        _CACHE['nc'] = build(cfg)
    nc, consts, _ = _CACHE['nc']
    xp = np.asarray(inputs['x_prompt'], dtype=np.float32)
    xs = np.asarray(inputs['x_sample'], dtype=np.float32)
    shared = {}
    for k, v in inputs.items():
        if k in ('x_prompt', 'x_sample'):
            continue
        a = np.ascontiguousarray(np.asarray(v, dtype=np.float32))
        if k in ('ssd_dt_bias', 'ssd_a_log'):
            a = a.reshape(a.shape[0], 16)
        if k == 'final_norm':
            a = a.reshape(1, D)
        shared[k] = a
    for k, v in consts.items():
        shared['c_' + k] = v
    in_maps = []
    for i in range(8):
        mp = dict(shared)
        mp['xa'] = np.ascontiguousarray(xp[i])
        mp['xb'] = np.ascontiguousarray(xs[i % 4])
        in_maps.append(mp)
    res = run_bass_kernel_spmd(nc, in_maps, core_ids=list(range(8)))
    yp = np.stack([np.asarray(res.results[i]['ya'], dtype=np.float32) for i in range(8)], 0)
    ysm = np.stack([np.asarray(res.results[i]['yb'], dtype=np.float32) for i in range(4)], 0)
    return (yp, ysm)
```

```python
import contextlib
import numpy as np
import ml_dtypes
import concourse.bass as bass
import concourse.mybir as mybir
from concourse.bass_utils import run_bass_kernel_spmd

F32 = mybir.dt.float32
BF16 = mybir.dt.bfloat16
AF = mybir.ActivationFunctionType
ALU = mybir.AluOpType

D = 1024
DFF = 2816
NKC = 8
NFC = 22
DPROJ = 4368
EPS = 1e-6
COMPUTE = ('pe', 'act', 'dve', 'pool')
ENGS = ('pe', 'act', 'dve', 'pool', 'sp')
SB_BASE = 16640
SB_END = 229376
DMA_RING = {'sp': 16, 'act': 4, 'pool': 16}


class Buf:
    __slots__ = ('w', 'r')

    def __init__(self):
        self.w = {}
        self.r = {}


class Prog:
    def __init__(self, nc):
        self.nc = nc
        self.ops = {e: [] for e in ENGS}
        self.known = {e: {} for e in ENGS}
        self.ndma = {e: 0 for e in DMA_RING}
        self.dma_last = {}
        self.flag = {e: set() for e in COMPUTE}
        self.sb_off = SB_BASE
        self.sb_cnt = 0
        self.sb_max = 0
        self.pb = 0

    def sb(self, shape, dtype, name='t'):
        esz = 4 if dtype == F32 else 2
        per_part = int(np.prod(shape[1:])) * esz
        off = (self.sb_off + 63) // 64 * 64
        self.sb_cnt += 1
        h = self.nc.alloc_sbuf_tensor_at(f"{name}{self.sb_cnt}", list(shape), dtype, offset=off)
        self.sb_off = off + per_part
        self.sb_max = max(self.sb_max, self.sb_off)
        assert self.sb_off <= SB_END, f"SBUF overflow {self.sb_off} at {name}"
        return h

    def mark(self):
        return self.sb_off

    def release(self, m):
        self.barrier()
        self.sb_off = m

    def _need(self, eng, toks):
        out = []
        kn = self.known[eng]
        for src, idx in toks.items():
            if kn.get(src, -1) >= idx:
                continue
            kn[src] = idx
            out.append((src, idx))
            if src in COMPUTE:
                self.flag[src].add(idx)
        return out

    def op(self, eng, fn, r=(), w=()):
        deps = {}
        for b in r:
            for src, idx in b.w.items():
                if deps.get(src, -1) < idx:
                    deps[src] = idx
        for b in w:
            for src, idx in b.w.items():
                if src != eng and deps.get(src, -1) < idx:
                    deps[src] = idx
            for src, idx in b.r.items():
                if src != eng and deps.get(src, -1) < idx:
                    deps[src] = idx
        waits = self._need(eng, deps)
        idx = len(self.ops[eng])
        self.ops[eng].append(('c', fn, waits))
        for b in r:
            b.r[eng] = idx
        for b in w:
            b.w = {eng: idx}
            b.r = {}
        return idx

    def dma(self, q, out, in_, r=(), w=(), **kw):
        deps = {}
        for b in r:
            for src, idx in b.w.items():
                if deps.get(src, -1) < idx:
                    deps[src] = idx
        for b in w:
            for src, idx in list(b.w.items()) + list(b.r.items()):
                if deps.get(src, -1) < idx:
                    deps[src] = idx
        n = self.ndma[q]
        self.ndma[q] = n + 1
        K = DMA_RING[q]
        key = ('dma', q, n % K)
        val = 16 * (n // K + 1)
        if n >= K:
            deps[key] = max(deps.get(key, -1), val - 16)
        waits = self._need(q, deps)
        self.dma_last[key] = val
        self.ops[q].append(('d', (out, in_, kw), waits, key))
        for b in r:
            b.r[key] = val
        for b in w:
            b.w = {key: val}
            b.r = {}

    def barrier(self):
        last = {}
        for e in COMPUTE:
            for i in range(len(self.ops[e]) - 1, -1, -1):
                if self.ops[e][i][0] == 'c':
                    last[e] = i
                    break
        for key, val in self.dma_last.items():
            last[key] = val
        for e in ENGS:
            deps = {s: i for s, i in last.items() if s != e}
            waits = self._need(e, deps)
            if waits:
                self.ops[e].append(('w', None, waits))

    def bank(self):
        b = self.pb
        self.pb = (b + 1) % 8
        return b

    def bank2(self):
        b = (self.pb + 1) // 2 * 2 % 8
        self.pb = (b + 2) % 8
        return b

    def emit(self):
        nc = self.nc
        val = {}
        for e in COMPUTE:
            cnt = 0
            v = {}
            for i, o in enumerate(self.ops[e]):
                if o[0] == 'c' and i in self.flag[e]:
                    cnt += 1
                    v[i] = cnt
            val[e] = v
        sems = {}
        with contextlib.ExitStack() as st:
            for e in COMPUTE:
                sems[e] = st.enter_context(nc.semaphore(f"s_{e}"))
            for q, K in DMA_RING.items():
                for k in range(K):
                    sems[('dma', q, k)] = st.enter_context(nc.semaphore(f"d_{q}{k}"))
            block = st.enter_context(nc.Block())

            def run(e):
                def body(eng):
                    fl = self.flag.get(e, ())
                    for i, o in enumerate(self.ops[e]):
                        for src, idx in o[2]:
                            v = val[src][idx] if src in COMPUTE else idx
                            eng.wait_ge(sems[src], v)
                        if o[0] == 'c':
                            ins = o[1](eng)
                            if i in fl:
                                ins.then_inc(sems[e], 1)
                        elif o[0] == 'd':
                            out, in_, kw = o[1]
                            eng.dma_start(out=out, in_=in_, **kw).then_inc(sems[o[3]], 16)
                return body
            block.tensor(run('pe'))
            block.scalar(run('act'))
            block.vector(run('dve'))
            block.gpsimd(run('pool'))
            block.sync(run('sp'))


def mm(P, out, lhsT, rhs, start, stop, r, w):
    P.op('pe', lambda e: e.matmul(out, lhsT=lhsT, rhs=rhs, start=start, stop=stop), r=r, w=w)


def trp(P, out, in_, ident, r, w):
    P.op('pe', lambda e: e.transpose(out, in_, ident), r=r, w=w)


def act(P, out, in_, func, r, w, bias=None, scale=None, accum=None):
    kw = {}
    if bias is not None:
        kw['bias'] = bias
    if scale is not None:
        kw['scale'] = scale
    if accum is not None:
        kw['accum_out'] = accum
    P.op('act', lambda e: e.activation(out, in_, func, **kw), r=r, w=w)


def tt(P, eng, out, in0, in1, op, r, w):
    P.op(eng, lambda e: e.tensor_tensor(out, in0, in1, op), r=r, w=w)


def ts(P, eng, out, in0, s1, s2, op0, op1, r, w):
    if s2 is None:
        P.op(eng, lambda e: e.tensor_scalar(out, in0, s1, None, op0), r=r, w=w)
    else:
        P.op(eng, lambda e: e.tensor_scalar(out, in0, s1, s2, op0, op1), r=r, w=w)


def stt(P, out, in0, scalar, in1, op0, op1, r, w):
    P.op('dve', lambda e: e.scalar_tensor_tensor(out, in0, scalar, in1, op0, op1), r=r, w=w)


def cpy(P, eng, out, in_, r, w):
    if eng == 'act':
        P.op('act', lambda e: e.copy(out, in_), r=r, w=w)
    else:
        P.op(eng, lambda e: e.tensor_copy(out, in_), r=r, w=w)


def mset(P, eng, ap, v, w):
    P.op(eng, lambda e: e.memset(ap, v), w=w)


def amul(P, out, in_, m_ap, r, w):
    P.op('act', lambda e: e.mul(out, in_, m_ap), r=r, w=w)


def recip(P, out, in_, r, w):
    P.op('dve', lambda e: e.reciprocal(out, in_), r=r, w=w)


def scan(P, out, d0, d1, init, r, w):
    P.op('dve', lambda e: e.tensor_tensor_scan(out, d0, d1, init, ALU.mult, ALU.add), r=r, w=w)


def make_consts():
    c = {}
    c['ident'] = np.eye(128, dtype=np.float32)
    c['identb'] = np.eye(128, dtype=np.float32).astype(ml_dtypes.bfloat16)
    c['ones'] = np.ones((128, 128), np.float32)
    k = np.arange(128)
    c['m_le'] = (k[:, None] <= k[None, :]).astype(np.float32)
    c['m_ge'] = (k[:, None] >= k[None, :]).astype(np.float32)
    c['m_gt'] = (k[:, None] > k[None, :]).astype(np.float32)
    c['m_lt'] = (k[:, None] < k[None, :]).astype(np.float32)
    bm = np.zeros((128, 512), np.float32)
    bm[:64, :256] = 1.0
    bm[64:, 256:] = 1.0
    c['blockmask'] = bm
    d = 128
    inv_freq = (1.0 / (10000.0 ** (np.arange(0, d, 2, dtype=np.float32) / np.float32(d)))).astype(np.float32)
    pos = np.arange(8192, dtype=np.float32)
    ang = (pos[:, None] * inv_freq[None, :]).astype(np.float32)
    cos = np.cos(ang.astype(np.float64)).T
    sin = np.sin(ang.astype(np.float64)).T
    cosf = np.concatenate([cos, cos], 0)
    sins = np.concatenate([-sin, sin], 0)
    sc = 128.0 ** -0.5
    c['cosq'] = cosf.astype(np.float32)
    c['sinq'] = sins.astype(np.float32)
    c['cosk'] = (cosf * sc).astype(np.float32)
    c['sink'] = (sins * sc).astype(np.float32)
    lg = np.log1p(-np.exp2(-5.0 - np.arange(4, dtype=np.float64)))
    pl = np.arange(128, dtype=np.float64)
    dm = np.exp(lg[None, :, None] * np.abs(pl[:, None, None] - pl[None, None, :]))
    c['dmask'] = dm.astype(np.float32)
    gf = np.exp(lg[:, None] * (pl + 1.0)[None, :])
    gb = np.exp(lg[:, None] * (128.0 - pl)[None, :])
    c['gf'] = np.broadcast_to(gf[None], (128, 4, 128)).astype(np.float32).copy()
    c['gb'] = np.broadcast_to(gb[None], (128, 4, 128)).astype(np.float32).copy()
    wf = np.exp(lg[None, :] * (127.0 - pl)[:, None])
    wb = np.exp(lg[None, :] * pl[:, None])
    c['wfb'] = np.concatenate([wf, wb], 1).astype(np.float32)
    c['cdec'] = np.broadcast_to(np.exp(lg * 128.0)[None, :], (128, 4)).astype(np.float32).copy()
    return c


CONST_SHAPES = None


def build(cfg, debug=False):
    TA, TB, DEPTH = cfg['TA'], cfg['TB'], cfg['DEPTH']
    NT = TA + TB
    TMAX = max(TA, TB)
    seqs = [(0, TA), (TA, TB)]
    nc = bass.Bass("TRN2", target_bir_lowering=False)
    P = Prog(nc)

    def din(name, shape, dt=F32):
        return nc.dram_tensor(name, list(shape), dt, kind="ExternalInput").ap()

    def dscr(name, shape, dt=F32):
        return nc.dram_tensor(name, list(shape), dt, kind="ExternalOutput" if debug else "Internal").ap()

    xa = din('xa', [TA, D])
    xb = din('xb', [TB, D])
    Wd = {}
    L = cfg.get('LW', 4)
    for name, shape in [('ffn1_norm', [L, D]), ('ffn1_w_gu', [L, D, 2 * DFF]), ('ffn1_w_down', [L, DFF, D]),
                        ('mix_norm', [L, D]), ('w_in', [L, D, DPROJ]), ('lru_conv_w', [L, 4, 512]),
                        ('lru_conv_b', [L, 512]), ('lru_w_a', [L, 2, 8, 64, 64]), ('lru_b_a', [L, 2, 512]),
                        ('lru_w_i', [L, 2, 8, 64, 64]), ('lru_b_i', [L, 2, 512]), ('lru_lam', [L, 2, 512]),
                        ('ssd_conv_w', [L, 4, 768]), ('ssd_conv_b', [L, 768]), ('ssd_dt_bias', [L, 16]),
                        ('ssd_a_log', [L, 16]), ('ssd_d', [L, 8]), ('ssd_norm', [L, 512]),
                        ('ret_norm', [L, 512]), ('w_out', [L, 1536, D]), ('ffn2_norm', [L, D]),
                        ('ffn2_w_gu', [L, D, 2 * DFF]), ('ffn2_w_down', [L, DFF, D]), ('final_norm', [1, D])]:
        Wd[name] = din(name, shape)
    consts = make_consts()
    Cd = {}
    for name, arr in consts.items():
        Cd[name] = din('c_' + name, arr.shape, BF16 if arr.dtype == ml_dtypes.bfloat16 else F32)
    ya = nc.dram_tensor('ya', [TA, D], F32, kind="ExternalOutput").ap()
    yb = nc.dram_tensor('yb', [TB, D], F32, kind="ExternalOutput").ap()

    X = dscr('X', [D, NT])
    LX = dscr('LX', [512, NT])
    LG = dscr('LG', [512, NT])
    XBC = dscr('XBC', [768, NT])
    RG = dscr('RG', [512, NT])
    Qd = dscr('Q', [512, NT], BF16)
    Kd = dscr('K', [512, NT], BF16)
    SZ = dscr('SZ', [NT, 512])
    Vd = dscr('V', [NT, 512], BF16)
    DT = dscr('DT', [NT, 16])
    Y = dscr('Y', [1536, NT], BF16)
    PF = dscr('PF', [TMAX // 128, 128, 512], BF16)

    ps = nc.alloc_psum_tensor("ps", [128, 4096], F32)
    pb = [Buf() for _ in range(8)]

    def pbank(b, n=512):
        return ps[:, b * 512:b * 512 + n]

    def pbankb(b):
        return ps[:, b * 512:(b + 1) * 512].bitcast(BF16)

    ident = P.sb([128, 128], F32, 'ident')
    identb = P.sb([128, 128], BF16, 'identb')
    ones = P.sb([128, 128], F32, 'ones')
    epst = P.sb([128, 1], F32, 'eps')
    cb_ = Buf()
    P.dma('sp', ident[:], Cd['ident'], w=[cb_])
    P.dma('sp', identb[:], Cd['identb'], w=[cb_])
    P.dma('sp', ones[:], Cd['ones'], w=[cb_])
    mset(P, 'dve', epst[:], EPS, [cb_])
    P.barrier()
    base_mark = P.mark()

    def rmsnorm_fm(xt, xbuf, nw, nwb, out, outb, TT, sq, sqb, rs, rsb, nfeat=D):
        b = P.bank()
        for kc in range(NKC):
            j = kc % len(sq)
            act(P, sq[j][:, :TT], xt[:, kc, :], AF.Square, r=[xbuf], w=[sqb[j]])
            mm(P, pbank(b, TT), ones[:], sq[j][:, :TT], kc == 0, kc == NKC - 1, r=[sqb[j], cb_], w=[pb[b]])
        act(P, rs[:, :TT], pbank(b, TT), AF.Sqrt, r=[pb[b], cb_], w=[rsb], bias=epst[:, 0:1], scale=1.0 / nfeat)
        recip(P, rs[:, :TT], rs[:, :TT], r=[rsb], w=[rsb])
        for kc in range(NKC):
            stt(P, out[:, kc, :], xt[:, kc, :], nw[:, kc:kc + 1], rs[:, :TT], ALU.mult, ALU.mult,
                r=[xbuf, rsb, nwb], w=[outb])

    def load_vec_pc(dst, src_1d, n, buf):
        P.dma('sp', dst, src_1d.rearrange("(c p) -> p c", p=128), w=[buf], allow_slow_non_contiguous=True)

    def s0_pass():
        m = P.mark()
        xin = [P.sb([128, 4, D], F32, 'xin') for _ in range(2)]
        xin_b = [Buf(), Buf()]
        xfm = [P.sb([128, NKC, 512], F32, 'xfm') for _ in range(2)]
        xfm_b = [Buf(), Buf()]
        i = 0
        for (src, (s0, T)) in zip((xa, xb), seqs):
            TT = 512 if T % 512 == 0 else 128
            nb = TT // 128
            for t0 in range(0, T, TT):
                xi, xib, xf, xfb = xin[i % 2], xin_b[i % 2], xfm[i % 2], xfm_b[i % 2]
                P.dma('sp', xi[:, 0:nb, :], src[t0:t0 + TT, :].rearrange("(b p) f -> p b f", p=128), w=[xib])
                for kc in range(NKC):
                    b = P.bank()
                    for bl in range(nb):
                        trp(P, pbank(b)[:, bl * 128:(bl + 1) * 128], xi[:, bl, kc * 128:(kc + 1) * 128], ident[:],
                            r=[xib, cb_], w=[pb[b]])
                    cpy(P, 'act' if kc % 2 else 'dve', xf[:, kc, 0:TT], pbank(b, TT), r=[pb[b]], w=[xfb])
                P.dma('pool', X[:, s0 + t0:s0 + t0 + TT].rearrange("(c p) t -> p c t", p=128), xf[:, :, 0:TT], r=[xfb])
                i += 1
        P.release(m)

    def ffn_pass(l, pre):
        m = P.mark()
        TT = cfg['TTF']
        wgu = P.sb([128, NKC, 2 * DFF], BF16, 'wgu')
        wd = P.sb([128, NFC, D], BF16, 'wd')
        wgub = [Buf() for _ in range(NKC)]
        wdb = [Buf() for _ in range(NFC)]
        nw = P.sb([128, NKC], F32, 'nw')
        nwb = Buf()
        load_vec_pc(nw[:], Wd[pre + '_norm'][l], NKC, nwb)
        for kc in range(NKC):
            P.dma('pool', wgu[:, kc, :], Wd[pre + '_w_gu'][l, kc * 128:(kc + 1) * 128, :], w=[wgub[kc]])
        for fc in range(NFC):
            P.dma('pool', wd[:, fc, :], Wd[pre + '_w_down'][l, fc * 128:(fc + 1) * 128, :], w=[wdb[fc]])
        xt = [P.sb([128, NKC, TT], F32, 'xt') for _ in range(2)]
        xtb = [Buf(), Buf()]
        xn2 = [P.sb([128, NKC, TT], BF16, 'xn') for _ in range(2)]
        xn2b = [Buf(), Buf()]
        h = P.sb([128, NFC, TT], BF16, 'h')
        hb = Buf()
        sq = [P.sb([128, TT], F32, 'sq') for _ in range(3)]
        sqb = [Buf() for _ in range(3)]
        rs = P.sb([128, TT], F32, 'rs')
        rsb = Buf()
        sg = [P.sb([128, TT], F32, 'sg') for _ in range(2)]
        sgb = [Buf(), Buf()]
        ntile = NT // TT

        def load(i):
            P.dma('sp', xt[i % 2][:], X[:, i * TT:(i + 1) * TT].rearrange("(c p) t -> p c t", p=128), w=[xtb[i % 2]])
        load(0)
        if ntile > 1:
            load(1)
        rmsnorm_fm(xt[0], xtb[0], nw, nwb, xn2[0], xn2b[0], TT, sq, sqb, rs, rsb)
        for i in range(ntile):
            x_, xb_ = xt[i % 2], xtb[i % 2]
            xn, xnb = xn2[i % 2], xn2b[i % 2]
            for mc in range(NFC):
                bg = P.bank()
                bu = P.bank()
                for kc in range(NKC):
                    mm(P, pbank(bg, TT), wgu[:, kc, mc * 128:(mc + 1) * 128], xn[:, kc, :], kc == 0, kc == NKC - 1,
                       r=[wgub[kc], xnb], w=[pb[bg]])
                for kc in range(NKC):
                    mm(P, pbank(bu, TT), wgu[:, kc, DFF + mc * 128:DFF + (mc + 1) * 128], xn[:, kc, :], kc == 0,
                       kc == NKC - 1, r=[wgub[kc], xnb], w=[pb[bu]])
                j = mc % 2
                act(P, sg[j][:], pbank(bg, TT), AF.Silu, r=[pb[bg]], w=[sgb[j]])
                tt(P, 'dve', h[:, mc, :], sg[j][:], pbank(bu, TT), ALU.mult, r=[sgb[j], pb[bu]], w=[hb])
            if i + 1 < ntile:
                rmsnorm_fm(xt[(i + 1) % 2], xtb[(i + 1) % 2], nw, nwb, xn2[(i + 1) % 2], xn2b[(i + 1) % 2], TT, sq, sqb, rs, rsb)
            for n in range(NKC):
                b = P.bank()
                for mc in range(NFC):
                    mm(P, pbank(b, TT), wd[:, mc, n * 128:(n + 1) * 128], h[:, mc, :], mc == 0, mc == NFC - 1,
                       r=[wdb[mc], hb], w=[pb[b]])
                stt(P, x_[:, n, :], pbank(b, TT), 0.5, x_[:, n, :], ALU.mult, ALU.add, r=[pb[b], xb_], w=[xb_])
            P.dma('pool', X[:, i * TT:(i + 1) * TT].rearrange("(c p) t -> p c t", p=128), x_[:], r=[xb_])
            if i + 2 < ntile:
                load(i + 2)
        P.release(m)

    def s2_pass(l):
        m = P.mark()
        TT = cfg['TT2']
        nbl = TT // 128
        win = P.sb([128, NKC, DPROJ], BF16, 'win')
        winb = [Buf() for _ in range(NKC)]
        wsw = P.sb([128, NKC, 1024], BF16, 'wsw')
        wswb = Buf()
        nw = P.sb([128, NKC], F32, 'nw')
        nwb = Buf()
        load_vec_pc(nw[:], Wd['mix_norm'][l], NKC, nwb)
        for kc in range(NKC):
            P.dma('pool', win[:, kc, :], Wd['w_in'][l, kc * 128:(kc + 1) * 128, :], w=[winb[kc]])
        for qk in range(2):
            base = 2320 + qk * 512
            for hh in range(4):
                for half in range(2):
                    c0 = base + hh * 128 + (1 - half) * 64
                    d0 = (qk * 4 + hh) * 128 + half * 64
                    P.dma('pool', wsw[:, :, d0:d0 + 64],
                          Wd['w_in'][l, :, c0:c0 + 64].rearrange("(c p) n -> p c n", p=128), w=[wswb])
        dtb = P.sb([128, 16], F32, 'dtb')
        dtbb = Buf()
        P.dma('sp', dtb[:], Wd['ssd_dt_bias'][l:l + 1, :].to_broadcast([128, 16]), w=[dtbb])
        xt = [P.sb([128, NKC, TT], F32, 'xt') for _ in range(2)]
        xtb = [Buf(), Buf()]
        rope = [P.sb([128, 4, TT], F32, 'rope') for _ in range(2)]
        ropeb = [Buf(), Buf()]
        xn2 = [P.sb([128, NKC, TT], BF16, 'xn') for _ in range(2)]
        xn2b = [Buf(), Buf()]
        xn, xnb = xn2[0], xn2b[0]
        sq = [P.sb([128, TT], F32, 'sq') for _ in range(3)]
        sqb = [Buf() for _ in range(3)]
        rs = P.sb([128, TT], F32, 'rs')
        rsb = Buf()

        def stage(shape, dt, name):
            return [P.sb(shape, dt, name) for _ in range(2)], [Buf(), Buf()]
        lxs, lxsb = stage([128, 4, TT], F32, 'lxs')
        lgs, lgsb = stage([128, 4, TT], F32, 'lgs')
        xbs, xbsb = stage([128, 6, TT], F32, 'xbs')
        rgs, rgsb = stage([128, 4, TT], F32, 'rgs')
        qs, qsb = stage([128, 4, TT], BF16, 'qs')
        ks, ksb = stage([128, 4, TT], BF16, 'ks')
        szs, szsb = stage([128, nbl, 512], F32, 'szs')
        vs, vsb = stage([128, nbl, 512], BF16, 'vs')
        dts, dtsb = stage([128, nbl, 16], F32, 'dts')
        t1 = [P.sb([128, TT], F32, 't1') for _ in range(2)]
        t1b = [Buf(), Buf()]
        t2 = [P.sb([128, TT], F32, 't2') for _ in range(2)]
        t2b = [Buf(), Buf()]
        sp1 = P.sb([128, nbl, 16], F32, 'sp1')
        sp2 = P.sb([128, nbl, 16], F32, 'sp2')
        sp3 = P.sb([128, nbl, 16], F32, 'sp3')
        spb = Buf()
        tiles = []
        for (s0, T) in seqs:
            for t0 in range(0, T, TT):
                tiles.append((s0, t0))

        def load(i):
            s0, t0 = tiles[i]
            P.dma('sp', xt[i % 2][:], X[:, s0 + t0:s0 + t0 + TT].rearrange("(c p) t -> p c t", p=128), w=[xtb[i % 2]])
            for k_, nm in enumerate(('cosq', 'sinq', 'cosk', 'sink')):
                P.dma('sp', rope[i % 2][:, k_, :], Cd[nm][:, t0:t0 + TT], w=[ropeb[i % 2]])

        def proj_fm(wt, wbufs, c0):
            b = P.bank()
            for kc in range(NKC):
                mm(P, pbank(b, TT), wt[:, kc, c0:c0 + 128], xn[:, kc, :], kc == 0, kc == NKC - 1,
                   r=[wbufs[kc] if isinstance(wbufs, list) else wbufs, xnb], w=[pb[b]])
            return b
        load(0)
        if len(tiles) > 1:
            load(1)
        rmsnorm_fm(xt[0], xtb[0], nw, nwb, xn2[0], xn2b[0], TT, sq, sqb, rs, rsb)
        for i in range(len(tiles)):
            s0, t0 = tiles[i]
            g0 = s0 + t0
            rp, rpb = rope[i % 2], ropeb[i % 2]
            xn, xnb = xn2[i % 2], xn2b[i % 2]
            j = i % 2
            for c in range(4):
                b = proj_fm(win, winb, c * 128)
                cpy(P, 'act' if c % 2 else 'dve', lxs[j][:, c, :], pbank(b, TT), r=[pb[b]], w=[lxsb[j]])
            P.dma('pool', LX[:, g0:g0 + TT].rearrange("(c p) t -> p c t", p=128), lxs[j][:], r=[lxsb[j]])
            for c in range(4):
                b = proj_fm(win, winb, 512 + c * 128)
                act(P, lgs[j][:, c, :], pbank(b, TT), AF.Gelu_apprx_tanh, r=[pb[b]], w=[lgsb[j]])
            P.dma('pool', LG[:, g0:g0 + TT].rearrange("(c p) t -> p c t", p=128), lgs[j][:], r=[lgsb[j]])
            for c in range(6):
                b = proj_fm(win, winb, 1536 + c * 128)
                cpy(P, 'act' if c % 2 else 'dve', xbs[j][:, c, :], pbank(b, TT), r=[pb[b]], w=[xbsb[j]])
            P.dma('pool', XBC[:, g0:g0 + TT].rearrange("(c p) t -> p c t", p=128), xbs[j][:], r=[xbsb[j]])
            for c in range(4):
                b = proj_fm(win, winb, 3856 + c * 128)
                act(P, rgs[j][:, c, :], pbank(b, TT), AF.Silu, r=[pb[b]], w=[rgsb[j]])
            P.dma('pool', RG[:, g0:g0 + TT].rearrange("(c p) t -> p c t", p=128), rgs[j][:], r=[rgsb[j]])
            for qk, (stg, stgb, dst) in enumerate(((qs, qsb, Qd), (ks, ksb, Kd))):
                for hh in range(4):
                    b1 = proj_fm(win, winb, 2320 + qk * 512 + hh * 128)
                    b2 = proj_fm(wsw, wswb, (qk * 4 + hh) * 128)
                    jj = hh % 2
                    tt(P, 'dve', t1[jj][:], pbank(b1, TT), rp[:, 2 * qk, :], ALU.mult, r=[pb[b1], rpb], w=[t1b[jj]])
                    tt(P, 'dve', t2[jj][:], pbank(b2, TT), rp[:, 2 * qk + 1, :], ALU.mult, r=[pb[b2], rpb], w=[t2b[jj]])
                    tt(P, 'pool', stg[j][:, hh, :], t1[jj][:], t2[jj][:], ALU.add, r=[t1b[jj], t2b[jj]], w=[stgb[j]])
                P.dma('pool', dst[:, g0:g0 + TT].rearrange("(c p) t -> p c t", p=128), stg[j][:], r=[stgb[j]])
            if i + 1 < len(tiles):
                rmsnorm_fm(xt[(i + 1) % 2], xtb[(i + 1) % 2], nw, nwb, xn2[(i + 1) % 2], xn2b[(i + 1) % 2], TT, sq, sqb, rs, rsb)
            if i + 2 < len(tiles):
                load(i + 2)
            bdt = P.bank()
            for bl in range(nbl):
                b = P.bank()
                for kc in range(NKC):
                    mm(P, pbank(b), xn[:, kc, bl * 128:(bl + 1) * 128], win[:, kc, 1024:1536], kc == 0, kc == NKC - 1,
                       r=[winb[kc], xnb], w=[pb[b]])
                act(P, szs[j][:, bl, :], pbank(b), AF.Silu, r=[pb[b]], w=[szsb[j]])
                b = P.bank()
                if b == bdt:
                    b = P.bank()
                for kc in range(NKC):
                    mm(P, pbank(b), xn[:, kc, bl * 128:(bl + 1) * 128], win[:, kc, 3344:3856], kc == 0, kc == NKC - 1,
                       r=[winb[kc], xnb], w=[pb[b]])
                cpy(P, 'dve', vs[j][:, bl, :], pbank(b), r=[pb[b]], w=[vsb[j]])
                for kc in range(NKC):
                    mm(P, pbank(bdt)[:, bl * 16:(bl + 1) * 16], xn[:, kc, bl * 128:(bl + 1) * 128], win[:, kc, 2304:2320],
                       kc == 0, kc == NKC - 1, r=[winb[kc], xnb], w=[pb[bdt]])
            P.dma('pool', SZ[g0:g0 + TT, :].rearrange("(b p) f -> p b f", p=128), szs[j][:], r=[szsb[j]])
            P.dma('pool', Vd[g0:g0 + TT, :].rearrange("(b p) f -> p b f", p=128), vs[j][:], r=[vsb[j]])
            tt(P, 'dve', sp1[:], pbank(bdt)[:, 0:nbl * 16].rearrange("p (b h) -> p b h", h=16),
               dtb[:].unsqueeze(1).to_broadcast([128, nbl, 16]), ALU.add, r=[pb[bdt], dtbb], w=[spb])
            act(P, sp2[:], sp1[:], AF.Abs, r=[spb], w=[spb])
            act(P, sp3[:], sp2[:], AF.Exp, r=[spb], w=[spb], scale=-1.0)
            act(P, sp2[:], sp3[:], AF.Ln, r=[spb], w=[spb], bias=1.0)
            stt(P, dts[j][:], sp1[:], 0.0, sp2[:], ALU.max, ALU.add, r=[spb], w=[dtsb[j]])
            P.dma('pool', DT[g0:g0 + TT, :].rearrange("(b p) h -> p b h", p=128), dts[j][:], r=[dtsb[j]])
        P.release(m)

    def lru_pass(l):
        m = P.mark()
        cw = P.sb([128, 4, 4], F32, 'cw')
        cbv = P.sb([128, 4], F32, 'cbv')
        bab = P.sb([128, 2, 4], F32, 'bab')
        bib = P.sb([128, 2, 4], F32, 'bib')
        lam = P.sb([128, 2, 4], F32, 'lam')
        c1 = P.sb([128, 2, 4], F32, 'c1')
        c2 = P.sb([128, 2, 4], F32, 'c2')
        tl1 = P.sb([128, 2, 4], F32, 'tl1')
        tl2 = P.sb([128, 2, 4], F32, 'tl2')
        kb = Buf()
        for k_ in range(4):
            load_vec_pc(cw[:, :, k_], Wd['lru_conv_w'][l, k_], 4, kb)
        load_vec_pc(cbv[:], Wd['lru_conv_b'][l], 4, kb)
        for nm, dst in (('lru_b_a', bab), ('lru_b_i', bib), ('lru_lam', lam)):
            for d_ in range(2):
                load_vec_pc(dst[:, d_, :], Wd[nm][l, d_], 4, kb)
        act(P, tl1[:], lam[:], AF.Abs, r=[kb], w=[kb])
        act(P, tl2[:], tl1[:], AF.Exp, r=[kb], w=[kb], scale=-1.0)
        act(P, tl1[:], tl2[:], AF.Ln, r=[kb], w=[kb], bias=1.0)
        ts(P, 'dve', tl2[:], lam[:], -1.0, 0.0, ALU.mult, ALU.max, r=[kb], w=[kb])
        tt(P, 'dve', tl1[:], tl1[:], tl2[:], ALU.add, r=[kb], w=[kb])
        ts(P, 'dve', c1[:], tl1[:], -8.0, None, ALU.mult, None, r=[kb], w=[kb])
        ts(P, 'dve', c2[:], tl1[:], -16.0, None, ALU.mult, None, r=[kb], w=[kb])
        WA = P.sb([128, 8, 128], BF16, 'WA')
        WI = P.sb([128, 8, 128], BF16, 'WI')
        wb_ = Buf()
        mset(P, 'pool', WA[:], 0.0, [wb_])
        mset(P, 'pool', WI[:], 0.0, [wb_])
        for nm, dst in (('lru_w_a', WA), ('lru_w_i', WI)):
            for d_ in range(2):
                for r_ in range(2):
                    src = Wd[nm][l, d_].rearrange("(c r) i j -> r i c j", r=2)[r_]
                    P.dma('pool', dst[r_ * 64:(r_ + 1) * 64, d_ * 4:(d_ + 1) * 4, r_ * 64:(r_ + 1) * 64], src, w=[wb_])
        hba = P.sb([128, 2, 4], F32, 'hba')
        hbi = P.sb([128, 2, 4], F32, 'hbi')
        hc1 = P.sb([128, 2, 4], F32, 'hc1')
        ts(P, 'dve', hba[:], bab[:], 0.5, None, ALU.mult, None, r=[kb], w=[kb])
        ts(P, 'dve', hbi[:], bib[:], 0.5, None, ALU.mult, None, r=[kb], w=[kb])
        ts(P, 'dve', hc1[:], c1[:], 0.5, None, ALU.mult, None, r=[kb], w=[kb])
        m2 = P.mark()
        bset = 0
        for (s0, T) in seqs:
            TL = 1024 if T % 1024 == 0 else (512 if T % 512 == 0 else 128)
            ntl = T // TL
            NBT = 2
            nbatch = (ntl + NBT - 1) // NBT
            H = P.sb([128, T], F32, 'H')
            XC = P.sb([128, T], F32, 'XC')
            XCb = P.sb([128, T], BF16, 'XCb')
            Hb, XCbuf, XCbb = Buf(), Buf(), Buf()
            xraw = [P.sb([128, TL + 3], F32, 'xraw') for _ in range(2)]
            xrawb = [Buf(), Buf()]

            def ring(n, shape, dt, name):
                return [P.sb(shape, dt, name) for _ in range(n)], [Buf() for _ in range(n)]
            A_ = [P.sb([128, NBT * TL], F32, 'A') for _ in range(2)]
            S_ = [P.sb([128, NBT * TL], F32, 'S') for _ in range(2)]
            TI = [P.sb([128, NBT * TL], F32, 'TI') for _ in range(2)]
            A_b = [[Buf() for _ in range(NBT)] for _ in range(2)]
            S_b = [[Buf() for _ in range(NBT)] for _ in range(2)]
            TI_b = [[Buf() for _ in range(NBT)] for _ in range(2)]
            tha, thab = ring(2, [128, TL], F32, 'tha')
            hbk, hbkb = ring(2, [128, TL], F32, 'hbk')
            gt, gtb = ring(2, [128, TL], F32, 'gt')
            hs, hsb = ring(2, [128, TL], F32, 'hs')
            yst, ystb = ring(2, [128, TL], BF16, 'yst')
            cnt = 0
            gcnt = 0
            for cc in range(4):
                def gates(d_, t0, st, si, gk):
                    nb = max(1, TL // 512)
                    w_ = min(TL, 512)
                    if TL >= 1024:
                        b_a = P.bank2()
                        b_i = P.bank2()
                    else:
                        b_a = P.bank()
                        b_i = P.bank()
                    for bl in range(nb):
                        mm(P, pbank(b_a + bl, w_), WA[:, d_ * 4 + cc, :], XCb[:, t0 + bl * w_:t0 + (bl + 1) * w_], True, True,
                           r=[wb_, XCbb], w=[pb[b_a + bl]])
                        mm(P, pbank(b_i + bl, w_), WI[:, d_ * 4 + cc, :], XCb[:, t0 + bl * w_:t0 + (bl + 1) * w_], True, True,
                           r=[wb_, XCbb], w=[pb[b_i + bl]])
                    j = gk % 2
                    sl = slice(si * TL, (si + 1) * TL)
                    pa = ps[:, b_a * 512:b_a * 512 + TL]
                    pi = ps[:, b_i * 512:b_i * 512 + TL]
                    pra = [pb[b_a + x] for x in range(nb)]
                    pri = [pb[b_i + x] for x in range(nb)]
                    act(P, tha[j][:], pa, AF.Tanh, r=pra + [kb], w=[thab[j]], bias=hba[:, d_, cc:cc + 1], scale=0.5)
                    act(P, TI[st][:, sl], pi, AF.Tanh, r=pri + [kb], w=[TI_b[st][si]], bias=hbi[:, d_, cc:cc + 1], scale=0.5)
                    act(P, A_[st][:, sl], tha[j][:], AF.Exp, r=[thab[j], kb], w=[A_b[st][si]],
                        scale=hc1[:, d_, cc:cc + 1], bias=hc1[:, d_, cc:cc + 1])
                    act(P, S_[st][:, sl], tha[j][:], AF.Exp, r=[thab[j], kb], w=[S_b[st][si]],
                        scale=c1[:, d_, cc:cc + 1], bias=c1[:, d_, cc:cc + 1])

                def finish_batch(st, n):
                    act(P, S_[st][:, 0:n * TL], S_[st][:, 0:n * TL], AF.Sqrt, r=S_b[st][0:n], w=S_b[st][0:n], scale=-1.0, bias=1.0)

                def make_u(st, si, t0):
                    sl = slice(si * TL, (si + 1) * TL)
                    stt(P, TI[st][:, sl], TI[st][:, sl], 1.0, S_[st][:, sl], ALU.add, ALU.mult, r=[TI_b[st][si], S_b[st][si]], w=[TI_b[st][si]])
                    stt(P, TI[st][:, sl], TI[st][:, sl], 0.5, XC[:, t0:t0 + TL], ALU.mult, ALU.mult, r=[TI_b[st][si], XCbuf], w=[TI_b[st][si]])
                for bi in range(nbatch):
                    tiles = [k for k in range(bi * NBT, min(ntl, (bi + 1) * NBT))]
                    st = bset % 2
                    bset += 1
                    for si, k in enumerate(tiles):
                        t0 = k * TL
                        xr, xrb = xraw[cnt % 2], xrawb[cnt % 2]
                        cnt += 1
                        lo = 2 if k == 0 else 0
                        hi = 1 if k == ntl - 1 else 0
                        if lo:
                            mset(P, 'pool', xr[:, 0:2], 0.0, [xrb])
                        if hi:
                            mset(P, 'pool', xr[:, TL + 2:TL + 3], 0.0, [xrb])
                        P.dma('sp', xr[:, lo:TL + 3 - hi],
                              LX[cc * 128:(cc + 1) * 128, s0 + t0 - 2 + lo:s0 + t0 + TL + 1 - hi], w=[xrb])
                        xc = XC[:, t0:t0 + TL]
                        ts(P, 'dve', xc, xr[:, 0:TL], cw[:, cc, 0:1], cbv[:, cc:cc + 1], ALU.mult, ALU.add, r=[xrb, kb], w=[XCbuf])
                        for tap in range(1, 4):
                            stt(P, xc, xr[:, tap:tap + TL], cw[:, cc, tap:tap + 1], xc, ALU.mult, ALU.add, r=[xrb, kb, XCbuf], w=[XCbuf])
                        cpy(P, 'act', XCb[:, t0:t0 + TL], xc, r=[XCbuf], w=[XCbb])
                        gates(0, t0, st, si, gcnt)
                        gcnt += 1
                    finish_batch(st, len(tiles))
                    for si, k in enumerate(tiles):
                        t0 = k * TL
                        make_u(st, si, t0)
                        init = H[:, t0 - 1:t0] if k > 0 else 0.0
                        sl = slice(si * TL, (si + 1) * TL)
                        scan(P, H[:, t0:t0 + TL], A_[st][:, sl], TI[st][:, sl], init, r=[A_b[st][si], TI_b[st][si], Hb], w=[Hb])
                kk = 0
                for bi in range(nbatch - 1, -1, -1):
                    tiles = [k for k in range(min(ntl, (bi + 1) * NBT) - 1, bi * NBT - 1, -1)]
                    st = bset % 2
                    bset += 1
                    for si, k in enumerate(tiles):
                        gates(1, k * TL, st, si, gcnt)
                        gcnt += 1
                    finish_batch(st, len(tiles))
                    for si, k in enumerate(tiles):
                        t0 = k * TL
                        j2 = kk % 2
                        P.dma('sp', gt[j2][:], LG[cc * 128:(cc + 1) * 128, s0 + t0:s0 + t0 + TL], w=[gtb[j2]])
                        make_u(st, si, t0)
                        sl = slice(si * TL, (si + 1) * TL)
                        init = hbk[1 - j2][:, 0:1] if kk > 0 else 0.0
                        rr = [A_b[st][si], TI_b[st][si]] + ([hbkb[1 - j2]] if kk > 0 else [])
                        scan(P, hbk[j2][:, ::-1], A_[st][:, sl][:, ::-1], TI[st][:, sl][:, ::-1], init, r=rr, w=[hbkb[j2]])
                        tt(P, 'dve', hs[j2][:], hbk[j2][:], H[:, t0:t0 + TL], ALU.add, r=[hbkb[j2], Hb], w=[hsb[j2]])
                        tt(P, 'pool', yst[j2][:], hs[j2][:], gt[j2][:], ALU.mult, r=[hsb[j2], gtb[j2]], w=[ystb[j2]])
                        P.dma('pool', Y[cc * 128:(cc + 1) * 128, s0 + t0:s0 + t0 + TL], yst[j2][:], r=[ystb[j2]])
                        kk += 1
            P.release(m2)
        P.release(m)

    def ssd_pass(l):
        m = P.mark()
        scw = P.sb([128, 6, 4], F32, 'scw')
        scb = P.sb([128, 6], F32, 'scb')
        A16 = P.sb([128, 16], F32, 'A16')
        dsk = P.sb([128, 8], F32, 'dsk')
        snw = P.sb([128, 4], F32, 'snw')
        m_le = P.sb([128, 128], F32, 'm_le')
        m_ge = P.sb([128, 128], F32, 'm_ge')
        m_gt = P.sb([128, 128], F32, 'm_gt')
        m_lt = P.sb([128, 128], F32, 'm_lt')
        bmask = P.sb([128, 512], F32, 'bmask')
        kb = Buf()
        for k_ in range(4):
            load_vec_pc(scw[:, :, k_], Wd['ssd_conv_w'][l, k_], 6, kb)
        load_vec_pc(scb[:], Wd['ssd_conv_b'][l], 6, kb)
        load_vec_pc(snw[:], Wd['ssd_norm'][l], 4, kb)
        P.dma('sp', A16[:], Wd['ssd_a_log'][l:l + 1, :].to_broadcast([128, 16]), w=[kb])
        P.dma('sp', dsk[:], Wd['ssd_d'][l:l + 1, :].to_broadcast([128, 8]), w=[kb])
        for nm, dst in (('m_le', m_le), ('m_ge', m_ge), ('m_gt', m_gt), ('m_lt', m_lt), ('blockmask', bmask)):
            P.dma('sp', dst[:], Cd[nm], w=[kb])
        act(P, A16[:], A16[:], AF.Exp, r=[kb], w=[kb])
        ts(P, 'dve', A16[:], A16[:], -1.0, None, ALU.mult, None, r=[kb], w=[kb])
        m2 = P.mark()
        for (s0, T) in seqs:
            NCH = T // 128
            XS = P.sb([128, NCH, 512], BF16, 'XS')
            BT = P.sb([128, NCH, 128], BF16, 'BT')
            BCf = P.sb([128, 2, T], BF16, 'BCf')
            XSb, BTb, BCb = Buf(), Buf(), Buf()
            DTt = P.sb([128, NCH, 16], F32, 'DTt')
            dtA = P.sb([128, NCH, 16], F32, 'dtA')
            EAC = P.sb([128, NCH, 16], F32, 'EAC')
            DS = P.sb([128, NCH, 16], F32, 'DS')
            CDE = P.sb([128, NCH, 16], F32, 'CDE')
            sb_ = Buf()
            mt = P.mark()
            ACUM = P.sb([128, NCH, 16], F32, 'ACUM')
            TOT = P.sb([128, NCH, 16], F32, 'TOT')
            for c0 in range(0, NCH, 16):
                c1_ = min(NCH, c0 + 16)
                P.dma('sp', DTt[:, c0:c1_, :], DT[s0 + c0 * 128:s0 + c1_ * 128, :].rearrange("(c p) h -> p c h", p=128), w=[sb_])
            tt(P, 'dve', dtA[:], DTt[:], A16[:].unsqueeze(1).to_broadcast([128, NCH, 16]), ALU.mult, r=[sb_, kb], w=[sb_])
            ncol = NCH * 16
            dflat = dtA[:].rearrange("p c h -> p (c h)")
            for c0 in range(0, ncol, 512):
                w_ = min(512, ncol - c0)
                ch0, nch_ = c0 // 16, w_ // 16
                b1, b2, b3 = P.bank(), P.bank(), P.bank()
                mm(P, pbank(b1, w_), m_le[:], dflat[:, c0:c0 + w_], True, True, r=[kb, sb_], w=[pb[b1]])
                mm(P, pbank(b2, w_), m_ge[:], dflat[:, c0:c0 + w_], True, True, r=[kb, sb_], w=[pb[b2]])
                mm(P, pbank(b3, w_), ones[:], dflat[:, c0:c0 + w_], True, True, r=[cb_, sb_], w=[pb[b3]])
                cpy(P, 'dve', ACUM[:, ch0:ch0 + nch_, 0:8], pbank(b1, w_).rearrange("p (c h) -> p c h", h=16)[:, :, 0:8],
                    r=[pb[b1]], w=[sb_])
                cpy(P, 'dve', ACUM[:, ch0:ch0 + nch_, 8:16], pbank(b2, w_).rearrange("p (c h) -> p c h", h=16)[:, :, 8:16],
                    r=[pb[b2]], w=[sb_])
                cpy(P, 'act', TOT[:, ch0:ch0 + nch_, :], pbank(b3, w_).rearrange("p (c h) -> p c h", h=16), r=[pb[b3]], w=[sb_])
            act(P, EAC[:], ACUM[:], AF.Exp, r=[sb_], w=[sb_])
            act(P, CDE[:], TOT[:], AF.Exp, r=[sb_], w=[sb_])
            tt(P, 'dve', DS[:], TOT[:], ACUM[:], ALU.subtract, r=[sb_], w=[sb_])
            act(P, DS[:], DS[:], AF.Exp, r=[sb_], w=[sb_])
            tt(P, 'dve', DS[:], DS[:], DTt[:], ALU.mult, r=[sb_], w=[sb_])
            P.release(mt)
            if cfg.get('ssd_stop', 9) <= 1:
                P.release(m2)
                continue
            m3 = P.mark()
            TL = 512 if T % 512 == 0 else 128
            ntl = T // TL
            xr = [P.sb([128, 6, TL + 3], F32, 'xr') for _ in range(2)]
            xrb = [Buf(), Buf()]
            cv = P.sb([128, 6, TL], F32, 'cv')
            cvb = Buf()
            xsf = P.sb([128, 4, TL], BF16, 'xsf')
            xsfb = Buf()
            for k in range(ntl):
                t0 = k * TL
                x_, xb_ = xr[k % 2], xrb[k % 2]
                lo = 2 if k == 0 else 0
                hi = 1 if k == ntl - 1 else 0
                if lo:
                    mset(P, 'pool', x_[:, :, 0:2], 0.0, [xb_])
                if hi:
                    mset(P, 'pool', x_[:, :, TL + 2:TL + 3], 0.0, [xb_])
                P.dma('sp', x_[:, :, lo:TL + 3 - hi],
                      XBC[:, s0 + t0 - 2 + lo:s0 + t0 + TL + 1 - hi].rearrange("(c p) t -> p c t", p=128), w=[xb_])
                for c in range(6):
                    ts(P, 'dve', cv[:, c, :], x_[:, c, 0:TL], scw[:, c, 0:1], scb[:, c:c + 1], ALU.mult, ALU.add, r=[xb_, kb], w=[cvb])
                    for tap in range(1, 4):
                        stt(P, cv[:, c, :], x_[:, c, tap:tap + TL], scw[:, c, tap:tap + 1], cv[:, c, :], ALU.mult, ALU.add,
                            r=[xb_, kb, cvb], w=[cvb])
                if cfg.get('prep_stop', 9) <= 1:
                    continue
                act(P, xsf[:], cv[:, 0:4, :], AF.Silu, r=[cvb], w=[xsfb])
                act(P, BCf[:, :, t0:t0 + TL], cv[:, 4:6, :], AF.Silu, r=[cvb], w=[BCb])
                if cfg.get('prep_stop', 9) <= 2:
                    continue
                for bl in range(TL // 128):
                    c = t0 // 128 + bl
                    b = P.bank()
                    pv = pbankb(b)
                    for cc in range(4):
                        trp(P, pv[:, cc * 128:(cc + 1) * 128], xsf[:, cc, bl * 128:(bl + 1) * 128], identb[:], r=[xsfb, cb_], w=[pb[b]])
                    if cfg.get('prep_stop', 9) >= 4:
                        trp(P, pv[:, 512:640], BCf[:, 0, t0 + bl * 128:t0 + (bl + 1) * 128], identb[:], r=[BCb, cb_], w=[pb[b]])
                    if cfg.get('prep_stop', 9) <= 4:
                        continue
                    cpv = cfg.get('cpv', 0)
                    if cpv == 0:
                        cpy(P, 'act', XS[:, c, :], pv[:, 0:512], r=[pb[b]], w=[XSb])
                        cpy(P, 'act', BT[:, c, :], pv[:, 512:640], r=[pb[b]], w=[BTb])
                    elif cpv == 1:
                        cpy(P, 'act', XS[:, c, :], pv[:, 0:512], r=[pb[b]], w=[XSb])
                    elif cpv == 2:
                        cpy(P, 'act', BT[:, c, :], pv[:, 512:640], r=[pb[b]], w=[BTb])
                    elif cpv == 3:
                        cpy(P, 'dve', XS[:, c, :].bitcast(F32), pbank(b)[:, 0:256], r=[pb[b]], w=[XSb])
            P.release(m3)
            if cfg.get('ssd_stop', 9) <= 2:
                P.release(m2)
                continue

            def ring(n, shape, dt, name):
                return [P.sb(shape, dt, name) for _ in range(n)], [Buf() for _ in range(n)]
            prev = [P.sb([128, 512], F32, 'prev') for _ in range(2)]
            prevb_ = [Buf(), Buf()]
            tmp, tmpb = ring(2, [128, 512], F32, 'tmp')
            pvb, pvbb = ring(3, [128, 512], BF16, 'pvb')
            xds, xdsb = ring(2, [128, 512], BF16, 'xds')
            pfb = [Buf() for _ in range(NCH)]

            def v8(ap):
                return ap.rearrange("p (h x) -> p h x", h=8)

            def bc8(ap8, n=64):
                return ap8.unsqueeze(2).to_broadcast([128, 8, n])
            mset(P, 'dve', prev[0][:], 0.0, [prevb_[0]])
            mset(P, 'pool', pvb[0][:], 0.0, [pvbb[0]])
            def fwA(c):
                jx = c % 2
                tt(P, 'pool', v8(xds[jx][:]), v8(XS[:, c, :]), bc8(DS[:, c, 0:8]), ALU.mult, r=[XSb, sb_], w=[xdsb[jx]])
                mm(P, pbank(jx), BT[:, c, :], xds[jx][:], True, True, r=[BTb, xdsb[jx]], w=[pb[jx]])

            def fwB(c):
                b = c % 2
                tt(P, 'dve', v8(tmp[0][:]), v8(prev[0][:]), bc8(CDE[:, c, 0:8]), ALU.mult, r=[prevb_[0], sb_], w=[tmpb[0]])
                tt(P, 'dve', prev[0][:], tmp[0][:], pbank(b), ALU.add, r=[tmpb[0], pb[b]], w=[prevb_[0]])
                jn = (c + 1) % 3
                tt(P, 'dve', pvb[jn][:], prev[0][:], bmask[:], ALU.mult, r=[prevb_[0], kb], w=[pvbb[jn]])
                P.dma('sp', PF[c + 1], pvb[jn][:], r=[pvbb[jn]], w=[pfb[c + 1]])
            P.dma('sp', PF[0], pvb[0][:], r=[pvbb[0]], w=[pfb[0]])
            for step in range(NCH):
                if 0 <= step - 1 < NCH - 1:
                    fwB(step - 1)
                if step < NCH - 1:
                    fwA(step)
            if cfg.get('ssd_stop', 9) <= 3:
                P.release(m2)
                continue
            rhs, rhsb = ring(2, [128, 1024], F32, 'rhs')
            E, Eb = ring(2, [128, 1024], F32, 'E')
            CBm, CBmb = ring(2, [128, 256], F32, 'CBm')
            Wt = [[P.sb([128, 1024], BF16, 'Wt') for _ in range(2)] for _ in range(2)]
            Wtb = [[Buf(), Buf()], [Buf(), Buf()]]
            xdt = [[P.sb([128, 512], BF16, 'xdt') for _ in range(2)] for _ in range(2)]
            xdtb = [[Buf(), Buf()], [Buf(), Buf()]]
            xd, xdb = ring(2, [128, 512], BF16, 'xd')
            xdo, xdob = xds, xdsb
            pfl, pflb = ring(2, [128, 512], BF16, 'pfl')
            szt, sztb = ring(3, [128, 512], F32, 'szt')
            y1 = P.sb([128, 512], F32, 'y1')
            y2 = P.sb([128, 512], F32, 'y2')
            y1b, y2b = Buf(), Buf()
            y3, y3b = ring(2, [128, 512], F32, 'y3')
            yn, ynb = ring(2, [128, 512], F32, 'yn')
            ssq, ssqb = ring(2, [128, 2], F32, 'ssq')
            yst, ystb = ring(2, [128, 4, 128], BF16, 'yst')
            pq, pqb = ring(2, [128, 512], BF16, 'pq')
            cml = m_le
            cmg = m_ge
            mset(P, 'dve', prev[1][:], 0.0, [prevb_[1]])
            mset(P, 'pool', pq[0][:], 0.0, [pqb[0]])
            BK_SF, BK_SB, BK_S, BK_YD, BK_OF, BK_OB = 0, 2, 4, 5, 6, 7

            def stA(kk):
                c = NCH - 1 - kk
                p2, p3 = kk % 2, kk % 3
                tok = slice(c * 128, (c + 1) * 128)
                P.dma('sp', pfl[p2][:], PF[c], r=[pfb[c]], w=[pflb[p2]])
                P.dma('sp', szt[p3][:], SZ[s0 + c * 128:s0 + (c + 1) * 128, :], w=[sztb[p3]])
                for d_, (mrhs, mlhs, bk) in enumerate(((m_le, m_gt, BK_SF), (m_ge, m_lt, BK_SB))):
                    tt(P, 'pool', rhs[d_][:].rearrange("p (h i) -> p h i", h=8), mrhs[:].unsqueeze(1).to_broadcast([128, 8, 128]),
                       dtA[:, c, d_ * 8:(d_ + 1) * 8].unsqueeze(2).to_broadcast([128, 8, 128]), ALU.mult, r=[kb, sb_], w=[rhsb[d_]])
                    mm(P, pbank(bk), mlhs[:], rhs[d_][:, 0:512], True, True, r=[kb, rhsb[d_]], w=[pb[bk]])
                    mm(P, pbank(bk + 1), mlhs[:], rhs[d_][:, 512:1024], True, True, r=[kb, rhsb[d_]], w=[pb[bk + 1]])
                    act(P, E[d_][:], ps[:, bk * 512:bk * 512 + 1024], AF.Exp, r=[pb[bk], pb[bk + 1]], w=[Eb[d_]])
                for g in range(2):
                    mm(P, pbank(BK_SF + g, 128), BCf[g * 64:(g + 1) * 64, 0, tok], BCf[g * 64:(g + 1) * 64, 1, tok],
                       True, True, r=[BCb], w=[pb[BK_SF + g]])
                for d_, cmk in enumerate((cml, cmg)):
                    for g in range(2):
                        tt(P, 'dve', CBm[d_][:, g * 128:(g + 1) * 128], pbank(BK_SF + g, 128), cmk[:], ALU.mult,
                           r=[pb[BK_SF + g], kb], w=[CBmb[d_]])
                for d_ in range(2):
                    tt(P, 'dve', Wt[d_][p2][:].rearrange("p (g k i) -> p g k i", g=2, k=4),
                       E[d_][:].rearrange("p (g k i) -> p g k i", g=2, k=4),
                       CBm[d_][:].rearrange("p (g i) -> p g i", g=2).unsqueeze(2).to_broadcast([128, 2, 4, 128]), ALU.mult,
                       r=[Eb[d_], CBmb[d_]], w=[Wtb[d_][p2]])
                    tt(P, 'pool', v8(xdt[d_][p2][:]), v8(XS[:, c, :]), bc8(DTt[:, c, d_ * 8:(d_ + 1) * 8]), ALU.mult,
                       r=[XSb, sb_], w=[xdtb[d_][p2]])
                tt(P, 'pool', v8(xd[p2][:]), v8(XS[:, c, :]), bc8(dsk[:, 0:8]), ALU.mult, r=[XSb, kb], w=[xdb[p2]])
                if c > 0:
                    tt(P, 'pool', v8(xdo[p2][:]), v8(XS[:, c, :]), bc8(DS[:, c, 8:16]), ALU.mult, r=[XSb, sb_], w=[xdob[p2]])

            def stB(kk):
                c = NCH - 1 - kk
                p2 = kk % 2
                tok = slice(c * 128, (c + 1) * 128)
                mm(P, pbank(BK_YD), identb[:], xd[p2][:], True, False, r=[cb_, xdb[p2]], w=[pb[BK_YD]])
                for hh in range(8):
                    for d_ in range(2):
                        mm(P, pbank(BK_YD)[:, hh * 64:(hh + 1) * 64], Wt[d_][p2][:, hh * 128:(hh + 1) * 128],
                           xdt[d_][p2][:, hh * 64:(hh + 1) * 64], False, (hh == 7 and d_ == 1),
                           r=[Wtb[d_][p2], xdtb[d_][p2]], w=[pb[BK_YD]])
                mm(P, pbank(BK_OF), BCf[:, 1, tok], pfl[p2][:], True, True, r=[BCb, pflb[p2]], w=[pb[BK_OF]])
                mm(P, pbank(BK_OB), BCf[:, 1, tok], pq[p2][:], True, True, r=[BCb, pqb[p2]], w=[pb[BK_OB]])
                if c > 0:
                    mm(P, pbank(BK_S), BT[:, c, :], xdo[p2][:], True, True, r=[BTb, xdob[p2]], w=[pb[BK_S]])
                tt(P, 'dve', v8(y1[:]), v8(pbank(BK_OF)), bc8(EAC[:, c, 0:8]), ALU.mult, r=[pb[BK_OF], sb_], w=[y1b])
                tt(P, 'dve', v8(y2[:]), v8(pbank(BK_OB)), bc8(EAC[:, c, 8:16]), ALU.mult, r=[pb[BK_OB], sb_], w=[y2b])
                tt(P, 'dve', y3[p2][:], y1[:], pbank(BK_YD), ALU.add, r=[y1b, pb[BK_YD]], w=[y3b[p2]])
                tt(P, 'dve', y3[p2][:], y3[p2][:], y2[:], ALU.add, r=[y3b[p2], y2b], w=[y3b[p2]])
                if c > 0:
                    tt(P, 'dve', v8(tmp[1][:]), v8(prev[1][:]), bc8(CDE[:, c, 8:16]), ALU.mult, r=[prevb_[1], sb_], w=[tmpb[1]])
                    tt(P, 'dve', prev[1][:], tmp[1][:], pbank(BK_S), ALU.add, r=[tmpb[1], pb[BK_S]], w=[prevb_[1]])
                    tt(P, 'dve', pq[1 - p2][:], prev[1][:], bmask[:], ALU.mult, r=[prevb_[1], kb], w=[pqb[1 - p2]])

            def stC(kk):
                c = NCH - 1 - kk
                p2, p3 = kk % 2, kk % 3
                tt(P, 'dve', y3[p2][:], y3[p2][:], szt[p3][:], ALU.mult, r=[y3b[p2], sztb[p3]], w=[y3b[p2]])
                act(P, yn[p2][:], y3[p2][:], AF.Square, r=[y3b[p2]], w=[ynb[p2], ssqb[p2]], accum=ssq[p2][:, 0:1])
                act(P, ssq[p2][:, 1:2], ssq[p2][:, 0:1], AF.Ln, r=[ssqb[p2], cb_], w=[ssqb[p2]], bias=epst[:, 0:1], scale=1.0 / 512)
                act(P, ssq[p2][:, 1:2], ssq[p2][:, 1:2], AF.Exp, r=[ssqb[p2]], w=[ssqb[p2]], scale=-0.5)
                amul(P, yn[p2][:], y3[p2][:], ssq[p2][:, 1:2], r=[y3b[p2], ssqb[p2], ynb[p2]], w=[ynb[p2]])

            def stC2(kk):
                c = NCH - 1 - kk
                p2 = kk % 2
                for cc in range(4):
                    trp(P, pbank(BK_YD)[:, cc * 128:(cc + 1) * 128], yn[p2][:, cc * 128:(cc + 1) * 128], ident[:],
                        r=[ynb[p2], cb_], w=[pb[BK_YD]])
                for cc in range(4):
                    amul(P, yst[p2][:, cc, :], pbank(BK_YD)[:, cc * 128:(cc + 1) * 128], snw[:, cc:cc + 1],
                         r=[pb[BK_YD], kb], w=[ystb[p2]])
                P.dma('act', Y[512:1024, s0 + c * 128:s0 + (c + 1) * 128].rearrange("(c p) t -> p c t", p=128), yst[p2][:], r=[ystb[p2]])
            for step in range(NCH + 3):
                if 0 <= step - 1 < NCH:
                    stB(step - 1)
                if step < NCH:
                    stA(step)
                if 0 <= step - 3 < NCH:
                    stC2(step - 3)
                if 0 <= step - 2 < NCH:
                    stC(step - 2)
            P.release(m2)
        P.release(m)

    def ret_pass(l):
        m = P.mark()
        rnw = P.sb([128, 4], F32, 'rnw')
        dmask = P.sb([128, 4, 128], F32, 'dmask')
        gf = P.sb([128, 4, 128], F32, 'gf')
        gb = P.sb([128, 4, 128], F32, 'gb')
        wfb = P.sb([128, 8], F32, 'wfb')
        cdec = P.sb([128, 4], F32, 'cdec')
        o128 = P.sb([128, 128], F32, 'o128')
        kb = Buf()
        load_vec_pc(rnw[:], Wd['ret_norm'][l], 4, kb)
        for nm, dst in (('dmask', dmask), ('gf', gf), ('gb', gb), ('wfb', wfb), ('cdec', cdec)):
            P.dma('sp', dst[:], Cd[nm], w=[kb])
        mset(P, 'dve', o128[:], 1.0 / 128, [kb])
        m2 = P.mark()

        def ring(n, shape, dt, name):
            return [P.sb(shape, dt, name) for _ in range(n)], [Buf() for _ in range(n)]

        def v4(ap):
            return ap.rearrange("p (h x) -> p h x", h=4)

        def bc4(ap4):
            return ap4.unsqueeze(2).to_broadcast([128, 4, 128])
        for (s0, T) in seqs:
            NCH = T // 128
            RF = P.sb([128, NCH, 512], BF16, 'RF')
            RFb = Buf()
            kt, ktb = ring(3, [128, 4, 128], BF16, 'kt')
            vt, vtb = ring(3, [128, 512], BF16, 'vt')
            qt, qtb = ring(2, [128, 4, 128], BF16, 'qt')
            rgt, rgtb = ring(2, [128, 4, 128], F32, 'rgt')
            ktm, ktmb = ring(2, [128, 512], BF16, 'ktm')
            vw, vwb = ring(2, [128, 512], BF16, 'vw')
            r_ = [P.sb([128, 512], F32, 'r') for _ in range(2)]
            rb_ = [Buf(), Buf()]
            tmp, tmpb = ring(2, [128, 512], F32, 'tmp')
            cnt = 0

            def kv_step(c, d_, cnt, btr, bkv):
                j3 = cnt % 3
                j2 = cnt % 2
                P.dma('sp', kt[j3][:], Kd[:, s0 + c * 128:s0 + (c + 1) * 128].rearrange("(h p) t -> p h t", p=128), w=[ktb[j3]])
                P.dma('sp', vt[j3][:], Vd[s0 + c * 128:s0 + (c + 1) * 128, :], w=[vtb[j3]])
                b = btr
                pv = pbankb(b)
                for hh in range(4):
                    trp(P, pv[:, hh * 128:(hh + 1) * 128], kt[j3][:, hh, :], identb[:], r=[ktb[j3], cb_], w=[pb[b]])
                cpy(P, 'act', ktm[j2][:], pv[:, 0:512], r=[pb[b]], w=[ktmb[j2]])
                tt(P, 'pool', v4(vw[j2][:]), v4(vt[j3][:]), bc4(wfb[:, d_ * 4:(d_ + 1) * 4]), ALU.mult, r=[vtb[j3], kb], w=[vwb[j2]])
                b = bkv
                for hh in range(4):
                    mm(P, pbank(b)[:, hh * 128:(hh + 1) * 128], ktm[j2][:, hh * 128:(hh + 1) * 128], vw[j2][:, hh * 128:(hh + 1) * 128],
                       True, True, r=[ktmb[j2], vwb[j2]], w=[pb[b]])
                return b, j3

            def state_update(d_, b):
                tt(P, 'dve', v4(tmp[d_][:]), v4(r_[d_][:]), bc4(cdec[:, 0:4]), ALU.mult, r=[rb_[d_], kb], w=[tmpb[d_]])
                tt(P, 'dve', r_[d_][:], tmp[d_][:], pbank(b), ALU.add, r=[tmpb[d_], pb[b]], w=[rb_[d_]])
            mset(P, 'dve', r_[0][:], 0.0, [rb_[0]])
            cpy(P, 'act', RF[:, 0, :], r_[0][:], r=[rb_[0]], w=[RFb])
            for step in range(NCH):
                if 0 <= step - 1 < NCH - 1:
                    c = step - 1
                    state_update(0, 3 + c % 2)
                    cpy(P, 'act', RF[:, c + 1, :], r_[0][:], r=[rb_[0]], w=[RFb])
                if step < NCH - 1:
                    kv_step(step, 0, cnt, 2, 3 + step % 2)
                    cnt += 1
            Sm, Smb = ring(2, [128, 4, 128], BF16, 'Sm')
            qf, qfb = ring(2, [128, 4, 128], BF16, 'qf')
            qb, qbb = ring(2, [128, 4, 128], BF16, 'qb')
            rbb = P.sb([128, 512], BF16, 'rbb')
            rbbb = Buf()
            kt2, kt2b = ring(3, [128, 4, 128], BF16, 'kt2')
            vt2, vt2b = ring(3, [128, 512], BF16, 'vt2')
            rg4, rg4b = ring(5, [128, 4, 128], F32, 'rg4')
            qt3, qt3b = ring(3, [128, 4, 128], BF16, 'qt3')
            ktm1 = P.sb([128, 512], BF16, 'ktm1')
            ktm1b = Buf()
            vw1 = P.sb([128, 512], BF16, 'vw1')
            vw1b = Buf()
            ysb, ysbb = ring(3, [128, 512], F32, 'ysb')
            ysq, ysqb = ring(2, [128, 512], F32, 'ysq')
            msb, msbb = ring(2, [128, 512], F32, 'msb')
            var, varb = ring(2, [128, 512], F32, 'var')
            m2t = P.sb([128, 512], F32, 'm2t')
            m2b = Buf()
            dd = P.sb([128, 512], F32, 'dd')
            ddb = Buf()
            ost, ostb = ring(2, [128, 4, 128], BF16, 'ost')
            mset(P, 'dve', r_[1][:], 0.0, [rb_[1]])
            BK_TR, BK_KV, BK_ST, BK_Y, BK_M, BK_Q = 0, 1, 3, 4, 5, 6

            def rLoad(kk):
                c = NCH - 1 - kk
                q3, p5 = kk % 3, kk % 5
                tk = slice(s0 + c * 128, s0 + (c + 1) * 128)
                P.dma('sp', qt3[q3][:], Qd[:, tk].rearrange("(h p) t -> p h t", p=128), w=[qt3b[q3]])
                P.dma('sp', rg4[p5][:], RG[:, tk].rearrange("(h p) t -> p h t", p=128), w=[rg4b[p5]])
                P.dma('sp', kt2[q3][:], Kd[:, tk].rearrange("(h p) t -> p h t", p=128), w=[kt2b[q3]])
                P.dma('sp', vt2[q3][:], Vd[tk, :], w=[vt2b[q3]])

            def rA(kk):
                c = NCH - 1 - kk
                p2, q3 = kk % 2, kk % 3
                pv = pbankb(BK_TR)
                for hh in range(4):
                    trp(P, pv[:, hh * 128:(hh + 1) * 128], kt2[q3][:, hh, :], identb[:], r=[kt2b[q3], cb_], w=[pb[BK_TR]])
                cpy(P, 'act', ktm1[:], pv[:, 0:512], r=[pb[BK_TR]], w=[ktm1b])
                tt(P, 'pool', v4(vw1[:]), v4(vt2[q3][:]), bc4(wfb[:, 4:8]), ALU.mult, r=[vt2b[q3], kb], w=[vw1b])
                bkv = BK_KV + p2
                for hh in range(4):
                    hs = slice(hh * 128, (hh + 1) * 128)
                    mm(P, pbank(bkv)[:, hs], ktm1[:, hs], vw1[:, hs], True, True, r=[ktm1b, vw1b], w=[pb[bkv]])
                for hh in range(4):
                    mm(P, pbank(BK_ST)[:, hh * 128:(hh + 1) * 128], kt2[q3][:, hh, :], qt3[q3][:, hh, :], True, True,
                       r=[kt2b[q3], qt3b[q3]], w=[pb[BK_ST]])
                tt(P, 'dve', Sm[p2][:], pbank(BK_ST).rearrange("p (h i) -> p h i", h=4), dmask[:], ALU.mult, r=[pb[BK_ST], kb], w=[Smb[p2]])
                tt(P, 'pool', qf[p2][:], qt3[q3][:], gf[:], ALU.mult, r=[qt3b[q3], kb], w=[qfb[p2]])
                tt(P, 'pool', qb[p2][:], qt3[q3][:], gb[:], ALU.mult, r=[qt3b[q3], kb], w=[qbb[p2]])

            def rB1(kk):
                c = NCH - 1 - kk
                p2, p3 = kk % 2, kk % 3
                cpy(P, 'act', rbb[:], r_[1][:], r=[rb_[1]], w=[rbbb])
                for hh in range(4):
                    o_ = pbank(BK_Y)[:, hh * 128:(hh + 1) * 128]
                    hs = slice(hh * 128, (hh + 1) * 128)
                    mm(P, o_, vt2[kk % 3][:, hs], Sm[p2][:, hh, :], True, False, r=[vt2b[kk % 3], Smb[p2]], w=[pb[BK_Y]])
                    mm(P, o_, RF[:, c, hs], qf[p2][:, hh, :], False, False, r=[RFb, qfb[p2]], w=[pb[BK_Y]])
                    mm(P, o_, rbb[:, hs], qb[p2][:, hh, :], False, True, r=[rbbb, qbb[p2]], w=[pb[BK_Y]])
                if c > 0:
                    state_update(1, BK_KV + p2)
                cpy(P, 'act', ysb[p3][:], pbank(BK_Y), r=[pb[BK_Y]], w=[ysbb[p3]])
                act(P, ysq[p2][:], pbank(BK_Y), AF.Square, r=[pb[BK_Y]], w=[ysqb[p2]])

            def rB2a(kk):
                p2, p3 = kk % 2, kk % 3
                mm(P, pbank(BK_M), o128[:], ysb[p3][:], True, True, r=[kb, ysbb[p3]], w=[pb[BK_M]])
                mm(P, pbank(BK_Q), o128[:], ysq[p2][:], True, True, r=[kb, ysqb[p2]], w=[pb[BK_Q]])
                cpy(P, 'act', msb[p2][:], pbank(BK_M), r=[pb[BK_M]], w=[msbb[p2]])
                act(P, m2t[:], msb[p2][:], AF.Square, r=[msbb[p2]], w=[m2b])

            def rB2b(kk):
                p2 = kk % 2
                stt(P, var[p2][:], m2t[:], -1.0, pbank(BK_Q), ALU.mult, ALU.add, r=[m2b, pb[BK_Q]], w=[varb[p2]])

            def rC1(kk):
                p2, p3 = kk % 2, kk % 3
                act(P, var[p2][:], var[p2][:], AF.Ln, r=[varb[p2], cb_], w=[varb[p2]], bias=epst[:, 0:1], scale=1.0)
                act(P, var[p2][:], var[p2][:], AF.Exp, r=[varb[p2]], w=[varb[p2]], scale=-0.5)
                tt(P, 'dve', dd[:], ysb[p3][:], msb[p2][:], ALU.subtract, r=[ysbb[p3], msbb[p2]], w=[ddb])

            def rC2(kk):
                c = NCH - 1 - kk
                p2, p3, p4 = kk % 2, kk % 3, kk % 4
                tt(P, 'dve', dd[:], dd[:], var[p2][:], ALU.mult, r=[ddb, varb[p2]], w=[ddb])
                tt(P, 'dve', v4(dd[:]), v4(dd[:]), bc4(rnw[:, 0:4]), ALU.mult, r=[ddb, kb], w=[ddb])
                tt(P, 'dve', ost[p2][:], v4(dd[:]), rg4[kk % 5][:], ALU.mult, r=[ddb, rg4b[kk % 5]], w=[ostb[p2]])
                P.dma('sp', Y[1024:1536, s0 + c * 128:s0 + (c + 1) * 128].rearrange("(h p) t -> p h t", p=128), ost[p2][:], r=[ostb[p2]])
            rLoad(0)
            for step in range(NCH + 3):
                if step + 1 < NCH:
                    rLoad(step + 1)
                if 0 <= step - 1 < NCH:
                    rB1(step - 1)
                if 0 <= step - 3 < NCH:
                    rC1(step - 3)
                if step < NCH:
                    rA(step)
                if 0 <= step - 2 < NCH:
                    rB2a(step - 2)
                if 0 <= step - 3 < NCH:
                    rC2(step - 3)
                if 0 <= step - 2 < NCH:
                    rB2b(step - 2)
            P.release(m2)
        P.release(m)

    def s4a_pass(l):
        m = P.mark()
        TT = cfg['TT4']
        wo = P.sb([128, 12, D], BF16, 'wo')
        wob = [Buf() for _ in range(12)]
        for c in range(12):
            P.dma('pool', wo[:, c, :], Wd['w_out'][l, c * 128:(c + 1) * 128, :], w=[wob[c]])
        xt = [P.sb([128, NKC, TT], F32, 'xt') for _ in range(2)]
        xtb = [Buf(), Buf()]
        yt = [P.sb([128, 12, TT], BF16, 'yt') for _ in range(2)]
        ytb = [Buf(), Buf()]
        ntile = NT // TT

        def load(i):
            P.dma('sp', xt[i % 2][:], X[:, i * TT:(i + 1) * TT].rearrange("(c p) t -> p c t", p=128), w=[xtb[i % 2]])
            P.dma('sp', yt[i % 2][:], Y[:, i * TT:(i + 1) * TT].rearrange("(c p) t -> p c t", p=128), w=[ytb[i % 2]])
        load(0)
        for i in range(ntile):
            if i + 1 < ntile:
                load(i + 1)
            x_, xb_, y_, yb_ = xt[i % 2], xtb[i % 2], yt[i % 2], ytb[i % 2]
            for n in range(NKC):
                b = P.bank()
                for c in range(12):
                    mm(P, pbank(b, TT), wo[:, c, n * 128:(n + 1) * 128], y_[:, c, :], c == 0, c == 11, r=[wob[c], yb_], w=[pb[b]])
                tt(P, 'dve', x_[:, n, :], x_[:, n, :], pbank(b, TT), ALU.add, r=[xb_, pb[b]], w=[xb_])
            P.dma('pool', X[:, i * TT:(i + 1) * TT].rearrange("(c p) t -> p c t", p=128), x_[:], r=[xb_])
        P.release(m)

    def s5_pass():
        m = P.mark()
        nw = P.sb([128, NKC], F32, 'nw')
        nwb = Buf()
        load_vec_pc(nw[:], Wd['final_norm'][0], NKC, nwb)
        xt = [P.sb([128, NKC, 512], F32, 'xt') for _ in range(2)]
        xtb = [Buf(), Buf()]
        xo = P.sb([128, NKC, 512], F32, 'xo')
        xob = Buf()
        otm = [P.sb([128, 4, D], F32, 'otm') for _ in range(2)]
        otmb = [Buf(), Buf()]
        sq = [P.sb([128, 512], F32, 'sq') for _ in range(3)]
        sqb = [Buf() for _ in range(3)]
        rs = P.sb([128, 512], F32, 'rs')
        rsb = Buf()
        tiles = []
        for (dst, (s0, T)) in zip((ya, yb), seqs):
            TT = 512 if T % 512 == 0 else 128
            for t0 in range(0, T, TT):
                tiles.append((dst, s0, t0, TT))

        def load(i):
            dst, s0, t0, TT = tiles[i]
            P.dma('sp', xt[i % 2][:, :, 0:TT], X[:, s0 + t0:s0 + t0 + TT].rearrange("(c p) t -> p c t", p=128), w=[xtb[i % 2]])
        load(0)
        for i in range(len(tiles)):
            if i + 1 < len(tiles):
                load(i + 1)
            dst, s0, t0, TT = tiles[i]
            x_, xb_ = xt[i % 2], xtb[i % 2]
            rmsnorm_fm(x_[:, :, 0:TT], xb_, nw, nwb, xo[:, :, 0:TT], xob, TT, sq, sqb, rs, rsb)
            o_, ob_ = otm[i % 2], otmb[i % 2]
            for bl in range(TT // 128):
                for half in range(2):
                    b = P.bank()
                    for q in range(4):
                        kc = half * 4 + q
                        trp(P, pbank(b)[:, q * 128:(q + 1) * 128], xo[:, kc, bl * 128:(bl + 1) * 128], ident[:], r=[xob, cb_], w=[pb[b]])
                    cpy(P, 'act' if half else 'dve', o_[:, bl, half * 512:(half + 1) * 512], pbank(b), r=[pb[b]], w=[ob_])
            P.dma('pool', dst[t0:t0 + TT, :].rearrange("(b p) f -> p b f", p=128), o_[:, 0:TT // 128, :], r=[ob_])
        P.release(m)

    stages = cfg.get('stages', None)

    def on(s):
        return stages is None or s in stages
    if on('s0'):
        s0_pass()
    for l in range(DEPTH):
        if on('s1'):
            ffn_pass(l, 'ffn1')
        if on('s2'):
            s2_pass(l)
        if on('lru'):
            lru_pass(l)
        if on('ssd'):
            ssd_pass(l)
        if on('ret'):
            ret_pass(l)
        if on('s4a'):
            s4a_pass(l)
        if on('s4b'):
            ffn_pass(l, 'ffn2')
    if on('s5'):
        s5_pass()
    P.barrier()
    P.emit()
    return nc, consts, P


CFG_FULL = dict(TA=8192, TB=4096, DEPTH=4, TTF=384, TT2=256, TT4=512, LW=4)
_CACHE = {}


def kernel(**inputs):
    cfg = CFG_FULL
    if 'nc' not in _CACHE:
        _CACHE['nc'] = build(cfg)
    nc, consts, _ = _CACHE['nc']
    xp = np.asarray(inputs['x_prompt'], dtype=np.float32)
    xs = np.asarray(inputs['x_sample'], dtype=np.float32)
    shared = {}
    for k, v in inputs.items():
        if k in ('x_prompt', 'x_sample'):
            continue
        a = np.ascontiguousarray(np.asarray(v, dtype=np.float32))
        if k in ('ssd_dt_bias', 'ssd_a_log'):
            a = a.reshape(a.shape[0], 16)
        if k == 'final_norm':
            a = a.reshape(1, D)
        shared[k] = a
    for k, v in consts.items():
        shared['c_' + k] = v
    in_maps = []
    for i in range(8):
        mp = dict(shared)
        mp['xa'] = np.ascontiguousarray(xp[i])
        mp['xb'] = np.ascontiguousarray(xs[i % 4])
        in_maps.append(mp)
    res = run_bass_kernel_spmd(nc, in_maps, core_ids=list(range(8)))
    yp = np.stack([np.asarray(res.results[i]['ya'], dtype=np.float32) for i in range(8)], 0)
    ysm = np.stack([np.asarray(res.results[i]['yb'], dtype=np.float32) for i in range(4)], 0)
    return (yp, ysm)
```
